# Optimizing a Trainium2 kernel written in Bass

```python
import math
import jax, jax.numpy as jnp
from jax import lax
import numpy as np

D_MODEL = 1024
BATCH = 4
SEQ = 4096
DEPTH = 4

SB_HEADS = 4
SB_DIM = 64
DIFF_HEADS = 4
DIFF_QK_DIM = 64
DIFF_V_DIM = 2 * DIFF_QK_DIM
MLA_HEADS = 4
MLA_NOPE = 64
MLA_ROPE = 32
MLA_V = 64
MLA_Q_RANK = 256
MLA_KV_RANK = 128
MIX_WIDTH = SB_HEADS * SB_DIM + DIFF_HEADS * DIFF_V_DIM + MLA_HEADS * MLA_V
W_IN_SPLITS = (
    SB_HEADS * SB_DIM,
    SB_HEADS * SB_DIM,
    SB_HEADS * SB_DIM,
    DIFF_HEADS * 2 * DIFF_QK_DIM,
    DIFF_HEADS * 2 * DIFF_QK_DIM,
    DIFF_HEADS * DIFF_V_DIM,
    MLA_Q_RANK,
    MLA_KV_RANK,
    MLA_ROPE,
)
W_IN_COLS = sum(W_IN_SPLITS)
D_FF = 2816
PLE_DIM = 256
ROPE_THETA = 500000.0
PARTIAL_ROT = DIFF_QK_DIM // 4
Q_BLOCK = 128
EPS = 1e-6

kernel_name = 'hybrid_sb_diff_mla_macaron_trunk'


def rms_norm(x, gain):
    xf = x.astype(jnp.float32)
    y = xf * lax.rsqrt(jnp.mean(xf * xf, axis=-1, keepdims=True) + EPS)
    return (y * gain.astype(jnp.float32)).astype(x.dtype)


def rope_tables(positions, rot_dim):
    inv_freq = 1.0 / (ROPE_THETA ** (jnp.arange(0, rot_dim, 2, dtype=jnp.float32) / rot_dim))
    ang = positions.astype(jnp.float32)[..., None] * inv_freq
    return jnp.cos(ang), jnp.sin(ang)


def apply_rope(x, cos, sin):
    half = cos.shape[-1]
    x1 = x[..., :half].astype(jnp.float32)
    x2 = x[..., half:2 * half].astype(jnp.float32)
    rot = jnp.concatenate([x1 * cos - x2 * sin, x2 * cos + x1 * sin], axis=-1).astype(x.dtype)
    return jnp.concatenate([rot, x[..., 2 * half:]], axis=-1)


def swiglu(u, w_gu, w_down):
    g, v = jnp.split(u @ w_gu, 2, axis=-1)
    return (jax.nn.silu(g) * v) @ w_down


def to_blocks(t):
    b, s = t.shape[:2]
    return jnp.swapaxes(t.reshape((b, s // Q_BLOCK, Q_BLOCK) + t.shape[2:]), 0, 1)


def from_blocks(t):
    nb, b, qb = t.shape[:3]
    return jnp.swapaxes(t, 0, 1).reshape((b, nb * qb) + t.shape[3:])


def block_sweep(fn, q):
    nb = q.shape[1] // Q_BLOCK
    starts = jnp.arange(nb, dtype=jnp.int32) * Q_BLOCK
    return from_blocks(lax.map(fn, (to_blocks(q), starts)))


def causal_mask(start, seq, strict):
    qpos = start + jnp.arange(Q_BLOCK, dtype=jnp.int32)
    kpos = jnp.arange(seq, dtype=jnp.int32)
    if strict:
        return kpos[None, :] < qpos[:, None]
    return kpos[None, :] <= qpos[:, None]


def stick_breaking_attention(q, k, v):
    scale = q.shape[-1] ** -0.5

    def blk(args):
        qb, start = args
        z = jnp.einsum('bqhd,bkhd->bhqk', qb, k).astype(jnp.float32) * scale
        mask = causal_mask(start, k.shape[1], True)
        log_beta = jax.nn.log_sigmoid(z)
        log_keep = jnp.where(mask, jax.nn.log_sigmoid(-z), 0.0)
        log_between = lax.cumsum(log_keep, axis=3, reverse=True) - log_keep
        a = jnp.where(mask, jnp.exp(log_beta + log_between), 0.0)
        return jnp.einsum('bhqk,bkhd->bqhd', a.astype(v.dtype), v)

    return block_sweep(blk, q)


def differential_attention(q, k, v, lam):
    scale = q.shape[-1] ** -0.5

    def blk(args):
        qb, start = args
        s = jnp.einsum('bqhcd,bkhcd->bchqk', qb, k).astype(jnp.float32) * scale
        mask = causal_mask(start, k.shape[1], False)
        pr = jax.nn.softmax(jnp.where(mask, s, -jnp.inf), axis=-1)
        attn = pr[:, 0] - lam * pr[:, 1]
        return jnp.einsum('bhqk,bkhe->bqhe', attn.astype(v.dtype), v)

    return block_sweep(blk, q)


def causal_softmax_attention(q, k, v):
    scale = q.shape[-1] ** -0.5

    def blk(args):
        qb, start = args
        s = jnp.einsum('bqhd,bkhd->bhqk', qb, k).astype(jnp.float32) * scale
        mask = causal_mask(start, k.shape[1], False)
        pr = jax.nn.softmax(jnp.where(mask, s, -jnp.inf), axis=-1)
        return jnp.einsum('bhqk,bkhd->bqhd', pr.astype(v.dtype), v)

    return block_sweep(blk, q)


def token_mixing(u, rope_diff, rope_mla, layer, w_in, mla_q_norm, mla_w_uq, mla_kv_norm,
                 mla_w_ukv, lq1, lk1, lq2, lk2, diff_subln, w_out):
    b, s, _ = u.shape
    offsets = list(np.cumsum(W_IN_SPLITS)[:-1])
    (sb_q, sb_k, sb_v, df_q, df_k, df_v, c_q, c_kv, k_pe) = jnp.split(u @ w_in, offsets, axis=-1)

    shp = (b, s, SB_HEADS, SB_DIM)
    out_a = stick_breaking_attention(sb_q.reshape(shp), sb_k.reshape(shp), sb_v.reshape(shp))

    cos_d, sin_d = rope_diff
    qk_shp = (b, s, DIFF_HEADS, 2, DIFF_QK_DIM)
    dq = apply_rope(df_q.reshape(qk_shp), cos_d, sin_d)
    dk = apply_rope(df_k.reshape(qk_shp), cos_d, sin_d)
    dv = df_v.reshape(b, s, DIFF_HEADS, DIFF_V_DIM)
    lambda_init = 0.8 - 0.6 * math.exp(-0.3 * layer)
    f32 = jnp.float32
    lam = (jnp.exp(jnp.sum(lq1.astype(f32) * lk1.astype(f32)))
           - jnp.exp(jnp.sum(lq2.astype(f32) * lk2.astype(f32))) + lambda_init)
    out_b = differential_attention(dq, dk, dv, lam)
    out_b = rms_norm(out_b, diff_subln) * (1.0 - lambda_init)

    cos_m, sin_m = rope_mla
    q_c = (rms_norm(c_q, mla_q_norm) @ mla_w_uq).reshape(b, s, MLA_HEADS, MLA_NOPE + MLA_ROPE)
    q_c = jnp.concatenate([q_c[..., :MLA_NOPE], apply_rope(q_c[..., MLA_NOPE:], cos_m, sin_m)], axis=-1)
    kv = (rms_norm(c_kv, mla_kv_norm) @ mla_w_ukv).reshape(b, s, MLA_HEADS, MLA_NOPE + MLA_V)
    k_rot = apply_rope(k_pe[:, :, None, :], cos_m, sin_m)
    k_c = jnp.concatenate([kv[..., :MLA_NOPE],
                           jnp.broadcast_to(k_rot, (b, s, MLA_HEADS, MLA_ROPE))], axis=-1)
    out_c = causal_softmax_attention(q_c, k_c, kv[..., MLA_NOPE:])

    mixed = jnp.concatenate([out_a.reshape(b, s, -1), out_b.reshape(b, s, -1),
                             out_c.reshape(b, s, -1)], axis=-1)
    return mixed @ w_out


def setup_inputs(seed: int = 0) -> dict:
    key = jax.random.key(seed)
    ks = jax.random.split(key, 32)
    f32 = jnp.float32

    def w(k, shape, fan_in):
        return jax.random.normal(k, shape, f32) * (fan_in ** -0.5)

    def gain(k, shape):
        return 1.0 + 0.02 * jax.random.normal(k, shape, f32)

    L, D = DEPTH, D_MODEL
    return {
        'x': jax.random.normal(ks[0], (BATCH, SEQ, D), f32),
        'p': jax.random.normal(ks[1], (DEPTH, BATCH, SEQ, PLE_DIM), f32),
        'positions': jnp.broadcast_to(jnp.arange(SEQ, dtype=jnp.int32), (BATCH, SEQ)),
        'norm_ffn1': gain(ks[2], (L, D)),
        'w_ffn1_gu': w(ks[3], (L, D, 2 * D_FF), D),
        'w_ffn1_down': w(ks[4], (L, D_FF, D), D_FF),
        'norm_mix': gain(ks[5], (L, D)),
        'w_in': w(ks[6], (L, D, W_IN_COLS), D),
        'mla_q_norm': gain(ks[7], (L, MLA_Q_RANK)),
        'mla_w_uq': w(ks[8], (L, MLA_Q_RANK, MLA_HEADS * (MLA_NOPE + MLA_ROPE)), MLA_Q_RANK),
        'mla_kv_norm': gain(ks[9], (L, MLA_KV_RANK)),
        'mla_w_ukv': w(ks[10], (L, MLA_KV_RANK, MLA_HEADS * (MLA_NOPE + MLA_V)), MLA_KV_RANK),
        'diff_lambda_q1': 0.1 * jax.random.normal(ks[11], (L, DIFF_QK_DIM), f32),
        'diff_lambda_k1': 0.1 * jax.random.normal(ks[12], (L, DIFF_QK_DIM), f32),
        'diff_lambda_q2': 0.1 * jax.random.normal(ks[13], (L, DIFF_QK_DIM), f32),
        'diff_lambda_k2': 0.1 * jax.random.normal(ks[14], (L, DIFF_QK_DIM), f32),
        'diff_subln': gain(ks[15], (L, DIFF_V_DIM)),
        'w_out': w(ks[16], (L, MIX_WIDTH, D), MIX_WIDTH),
        'norm_ffn2': gain(ks[17], (L, D)),
        'w_ffn2_gu': w(ks[18], (L, D, 2 * D_FF), D),
        'w_ffn2_down': w(ks[19], (L, D_FF, D), D_FF),
        'norm_ple': gain(ks[20], (L, D)),
        'w_ple_gate': w(ks[21], (L, D, D), D),
        'w_ple_proj': w(ks[22], (L, PLE_DIM, D), PLE_DIM),
        'norm_final': gain(ks[23], (D,)),
    }


def reference(x, p, positions, norm_ffn1, w_ffn1_gu, w_ffn1_down, norm_mix, w_in, mla_q_norm,
              mla_w_uq, mla_kv_norm, mla_w_ukv, diff_lambda_q1, diff_lambda_k1, diff_lambda_q2,
              diff_lambda_k2, diff_subln, w_out, norm_ffn2, w_ffn2_gu, w_ffn2_down, norm_ple,
              w_ple_gate, w_ple_proj, norm_final):
    cos_d, sin_d = rope_tables(positions, PARTIAL_ROT)
    rope_diff = (cos_d[:, :, None, None, :], sin_d[:, :, None, None, :])
    cos_m, sin_m = rope_tables(positions, MLA_ROPE)
    rope_mla = (cos_m[:, :, None, :], sin_m[:, :, None, :])

    h = x
    for i in range(DEPTH):
        h = h + 0.5 * swiglu(rms_norm(h, norm_ffn1[i]), w_ffn1_gu[i], w_ffn1_down[i])
        h = h + token_mixing(rms_norm(h, norm_mix[i]), rope_diff, rope_mla, i, w_in[i],
                             mla_q_norm[i], mla_w_uq[i], mla_kv_norm[i], mla_w_ukv[i],
                             diff_lambda_q1[i], diff_lambda_k1[i], diff_lambda_q2[i],
                             diff_lambda_k2[i], diff_subln[i], w_out[i])
        h = h + 0.5 * swiglu(rms_norm(h, norm_ffn2[i]), w_ffn2_gu[i], w_ffn2_down[i])
        gate = jax.nn.sigmoid(rms_norm(h, norm_ple[i]) @ w_ple_gate[i])
        h = h + (p[i] @ w_ple_proj[i]) * gate
    return rms_norm(h, norm_final)
```

```python
import math
from contextlib import ExitStack
import numpy as np
import concourse.bass as bass
import concourse.mybir as mybir
from concourse.bass_utils import run_bass_kernel_spmd

F32 = mybir.dt.float32
BF16 = mybir.dt.bfloat16
I32 = mybir.dt.int32
ALU = mybir.AluOpType
AF = mybir.ActivationFunctionType
AX = mybir.AxisListType

D = 1024
KC = 8
S = 4096
TL = 2048
NT = 512
DFF = 2816
FC = 22
DEPTH = 4
EPS = 1e-6
THETA = 500000.0
NWIN = 2112
GROUPS = [[0, 1], [2, 3], [4, 5], [6, 7]]

V_GF1, V_GMIX, V_GF2, V_GPLE = 0, 32, 64, 96
V_GFIN = 128
V_QN = 136
V_KVN = 144
V_SUBLN = 148
V_SEL = 152
V_INVF = 154
V_SGN = 157
V_NEGPI = 160
V_LAM = 164
NV = V_LAM + 4 * 4 * 64


class T:
    __slots__ = ("ap", "w", "r", "name")

    def __init__(self, ap, name=""):
        self.ap = ap
        self.w = {}
        self.r = {}
        self.name = name


class Prog:
    ISSUERS = ("pe", "act", "dve", "pool", "sync")
    KSLOT = 8
    KQ = {"sync": 8, "pool": 4}

    def __init__(self, nc, es):
        self.nc = nc
        self.es = es
        self.ops = {e: [] for e in self.ISSUERS}
        self.seen = {e: {} for e in self.ISSUERS}
        self.cnt = {}
        self.sem = {}
        self.dma_i = {"sync": 0, "pool": 0}
        self.ncc = 0
        for e in ("pe", "act", "dve", "pool"):
            self.sem[e] = es.enter_context(nc.semaphore("s_" + e))
            self.cnt[e] = 0
        for q in ("sync", "pool"):
            for k in range(self.KSLOT):
                p = (q, k)
                self.sem[p] = es.enter_context(nc.semaphore("d_%s%d" % (q, k)))
                self.cnt[p] = 0

    def _waits(self, issuer, deps, skip_self_pe=True):
        out = []
        seen = self.seen[issuer]
        for p, c in deps.items():
            if c <= 0:
                continue
            if p == "pe" and issuer == "pe":
                continue
            if seen.get(p, 0) >= c:
                continue
            seen[p] = c
            mult = 1 if isinstance(p, str) else (16 if p[0] != "cc" else 1)
            out.append((self.sem[p], c * mult))
        return out

    def emit(self, issuer, fn, reads=(), writes=(), kind="c", inc=True):
        deps = {}

        def merge(d):
            for p, c in d.items():
                if deps.get(p, 0) < c:
                    deps[p] = c
        for t in reads:
            merge(t.w)
        for t in writes:
            merge(t.w)
            merge(t.r)
        if kind == "c":
            prod = issuer
            inc_default = 1
        elif kind == "dma":
            i = self.dma_i[issuer]
            self.dma_i[issuer] = i + 1
            prod = (issuer, i % self.KQ[issuer])
            if self.cnt[prod] > 0:
                merge({prod: self.cnt[prod]})
            inc_default = 16
        else:
            prod = ("cc", self.ncc)
            self.ncc += 1
            self.sem[prod] = self.es.enter_context(self.nc.semaphore("cc%d" % prod[1]))
            self.cnt[prod] = 0
            inc_default = 1
        waits = self._waits(issuer, deps)
        if kind == "c" and not inc:
            my = self.cnt[prod] + 1
            inc_amt = 0
        else:
            self.cnt[prod] += 1
            my = self.cnt[prod]
            inc_amt = inc_default
        for t in reads:
            if t.r.get(prod, 0) < my:
                t.r[prod] = my
        for t in writes:
            t.w = {prod: my}
            t.r = {}
        self.ops[issuer].append((waits, fn, self.sem[prod], inc_amt))

    def barrier(self):
        allp = {p: c for p, c in self.cnt.items() if c > 0}
        for issuer in self.ISSUERS:
            waits = self._waits(issuer, dict(allp))
            if issuer == "pe" and self.cnt["pe"] > 0:
                pass
            if waits:
                self.ops[issuer].append((waits, None, None, 0))

    def replay(self, issuer, eng):
        for waits, fn, sem, inc in self.ops[issuer]:
            for s, v in waits:
                eng.wait_ge(s, v)
            if fn is not None:
                if inc:
                    fn(eng).then_inc(sem, inc)
                else:
                    fn(eng)

    def final_wait(self, issuer, eng):
        for p, c in self.cnt.items():
            if c > 0:
                mult = 1 if isinstance(p, str) else (16 if p[0] != "cc" else 1)
                eng.wait_ge(self.sem[p], c * mult)


class Arena:
    def __init__(self, ap, n):
        self.ap = ap
        self.n = n
        self.off = 0

    def alloc(self, ncols):
        assert self.off + ncols <= self.n, ("arena overflow", self.off, ncols, self.n)
        a = self.ap[:, self.off:self.off + ncols]
        self.off += ncols
        return a

    def mark(self):
        return self.off

    def reset(self, m):
        self.off = m


def build_program(depth=DEPTH, dbg=None):
    nc = bass.Bass("TRN2", target_bir_lowering=False)
    es = ExitStack()

    def din(name, shape, dt=F32):
        return nc.dram_tensor(name, list(shape), dt, kind="ExternalInput").ap()

    xT = din("xT", [D, TL])
    pT = din("pT", [DEPTH, 256, TL])
    posr = din("posr", [128, S], I32)
    vecs_d = din("vecs", [128, NV])
    cmask_d = din("cmask", [128, 8 * 512 + 128], BF16)
    w1gu = din("w1gu", [DEPTH, D, 2 * DFF])
    w1d = din("w1d", [DEPTH, DFF, D])
    w2gu = din("w2gu", [DEPTH, D, 2 * DFF])
    w2d = din("w2d", [DEPTH, DFF, D])
    win = din("win", [DEPTH, D, NWIN])
    wuq = din("wuq", [DEPTH, 256, 384])
    wukv = din("wukv", [DEPTH, 128, 256])
    wout = din("wout", [DEPTH, D, D])
    wgate = din("wgate", [DEPTH, D, D])
    wproj = din("wproj", [DEPTH, 256, D])
    outT = nc.dram_tensor("outT", [D, TL], F32, kind="ExternalOutput").ap()

    tabs = [nc.dram_tensor("tab%d" % i, [128, S], F32) for i in range(6)]
    u_loc = [[nc.dram_tensor("uloc%d_%d" % (L, c), [256, TL], BF16) for c in range(4)] for L in range(depth)]
    u_g = [[nc.dram_tensor("ug%d_%d" % (L, c), [512, TL], BF16) for c in range(4)] for L in range(depth)]
    mx_loc = [[nc.dram_tensor("mxl%d_%d" % (L, c), [128, S], BF16) for c in range(4)] for L in range(depth)]
    mx_g = [[nc.dram_tensor("mxg%d_%d" % (L, c), [256, S], BF16) for c in range(4)] for L in range(depth)]

    NBF = 50176
    NF = 8448
    hT_t = es.enter_context(nc.sbuf_tensor("hT", [128, KC * TL], F32))
    abf_t = es.enter_context(nc.sbuf_tensor("abf", [128, NBF], BF16))
    af_t = es.enter_context(nc.sbuf_tensor("af32", [128, NF], F32))
    P = Prog(nc, es)
    abf = Arena(abf_t[:, :], NBF)
    af = Arena(af_t[:, :], NF)
    psum = [T(es.enter_context(nc.psum_tensor("ps%d" % i, [128, 512], F32))[:, :], "ps%d" % i) for i in range(8)]

    hT = hT_t[:, :].rearrange("p (k t) -> p k t", k=KC)
    h = [[T(hT[:, kc, t * NT:(t + 1) * NT], "h%d_%d" % (kc, t)) for t in range(4)] for kc in range(KC)]

    vecs = T(af.alloc(NV), "vecs")
    lamv = T(af.alloc(8), "lamv")
    cm = T(abf.alloc(8 * 512 + 128), "cmask")
    ones = T(abf.alloc(128), "ones")
    P.emit("sync", lambda e: e.dma_start(out=vecs.ap, in_=vecs_d[:, :]), writes=[vecs], kind="dma")
    P.emit("pool", lambda e: e.dma_start(out=cm.ap, in_=cmask_d[:, :]), writes=[cm], kind="dma")
    P.emit("dve", lambda e: e.memset(ones.ap, 1.0), writes=[ones])
    maskI = [cm.ap[:, j * 512:(j + 1) * 512] for j in range(4)]
    maskS = [cm.ap[:, (4 + j) * 512:(5 + j) * 512] for j in range(4)]
    trim = cm.ap[:, 8 * 512:8 * 512 + 128]
    for kc in range(KC):
        P.emit("sync", (lambda kc: lambda e: e.dma_start(out=hT[:, kc, :], in_=xT[kc * 128:(kc + 1) * 128, :]))(kc),
               writes=h[kc], kind="dma")
    pers_bf = abf.mark()
    pers_f = af.mark()

    def vcol(c, n=1):
        return vecs.ap[:, c:c + n]

    def setup():
        HS = 1024
        ki_t = es.enter_context(nc.sbuf_tensor("ki", [128, HS], I32))
        ki = T(ki_t[:, :])
        posf = T(af.alloc(HS))
        ang = T(af.alloc(HS))
        tq = T(af.alloc(HS))
        yy = T(af.alloc(HS))
        sv = T(af.alloc(HS))
        TWO_PI = 2 * math.pi
        for part in range(S // HS):
            c0 = part * HS
            P.emit("pool", (lambda c0: lambda e: e.dma_start(out=posf.ap, in_=posr[:, c0:c0 + HS]))(c0), writes=[posf], kind="dma")
            for s in range(3):
                P.emit("dve", (lambda s: lambda e: e.tensor_scalar(out=ang.ap, in0=posf.ap, scalar1=vcol(V_INVF + s), scalar2=None,
                                                                    op0=ALU.mult))(s), reads=[posf, vecs], writes=[ang])
                for which, phase in ((1, 0.0), (0, 0.5 * math.pi)):
                    P.emit("dve", (lambda phase: lambda e: e.tensor_scalar(out=tq.ap, in0=ang.ap, scalar1=phase, scalar2=1.0 / TWO_PI,
                                                                            op0=ALU.add, op1=ALU.mult))(phase), reads=[ang], writes=[tq])
                    P.emit("dve", lambda e: e.tensor_copy(out=ki.ap, in_=tq.ap), reads=[tq], writes=[ki])
                    P.emit("dve", lambda e: e.tensor_copy(out=tq.ap, in_=ki.ap), reads=[ki], writes=[tq])
                    P.emit("dve", (lambda phase: lambda e: e.tensor_scalar(out=yy.ap, in0=ang.ap, scalar1=phase, scalar2=None, op0=ALU.add))(phase),
                           reads=[ang], writes=[yy])
                    P.emit("dve", lambda e: e.scalar_tensor_tensor(out=yy.ap, in0=tq.ap, scalar=-TWO_PI, in1=yy.ap, op0=ALU.mult, op1=ALU.add),
                           reads=[tq, yy], writes=[yy])
                    P.emit("dve", lambda e: e.tensor_scalar(out=yy.ap, in0=yy.ap, scalar1=-3.141592, scalar2=3.141592, op0=ALU.max, op1=ALU.min),
                           reads=[yy], writes=[yy])
                    P.emit("act", lambda e: e.activation(out=sv.ap, in_=yy.ap, func=AF.Sin), reads=[yy], writes=[sv])
                    if which == 1:
                        P.emit("dve", (lambda s: lambda e: e.tensor_scalar(out=sv.ap, in0=sv.ap, scalar1=vcol(V_SGN + s), scalar2=None,
                                                                            op0=ALU.mult))(s), reads=[sv, vecs], writes=[sv])
                    tt = T(None)
                    P.emit("sync", (lambda s, which, c0: lambda e: e.dma_start(out=tabs[2 * s + which][:, c0:c0 + HS], in_=sv.ap))(s, which, c0),
                           reads=[sv], writes=[tt], kind="dma")
        pr = T(af.alloc(64))
        d12 = T(af.alloc(8))
        for L in range(depth):
            for j in range(2):
                a = V_LAM + (2 * j) * 256 + L * 64
                b = V_LAM + (2 * j + 1) * 256 + L * 64
                P.emit("dve", (lambda a, b: lambda e: e.tensor_tensor(out=pr.ap, in0=vcol(a, 64), in1=vcol(b, 64), op=ALU.mult))(a, b),
                       reads=[vecs], writes=[pr])
                P.emit("dve", (lambda j: lambda e: e.reduce_sum(out=d12.ap[:, j:j + 1], in_=pr.ap, axis=AX.X))(j), reads=[pr], writes=[d12])
            P.emit("act", lambda e: e.activation(out=d12.ap[:, 2:4], in_=d12.ap[:, 0:2], func=AF.Exp), reads=[d12], writes=[d12])
            lam_init = 0.8 - 0.6 * math.exp(-0.3 * L)
            P.emit("dve", (lambda L, li: lambda e: e.scalar_tensor_tensor(out=lamv.ap[:, L:L + 1], in0=d12.ap[:, 3:4], scalar=-li,
                                                                            in1=d12.ap[:, 2:3], op0=ALU.add, op1=ALU.subtract))(L, lam_init),
                   reads=[d12], writes=[lamv])
        P.barrier()
        af.reset(pers_f)

    def rmsnorm(src_chunks, nch, dim, gcol, out_tiles, sq, ps_ss, rstd, src_aps=None):
        for c in range(nch):
            P.emit("act", (lambda c: lambda e: e.activation(out=sq[c].ap, in_=src_chunks[c].ap, func=AF.Square))(c),
                   reads=[src_chunks[c]], writes=[sq[c]])
        for c in range(nch):
            P.emit("pe", (lambda c: lambda e: e.matmul(ps_ss.ap, lhsT=ones.ap, rhs=sq[c].ap, start=(c == 0), stop=(c == nch - 1)))(c),
                   reads=[ones, sq[c]], writes=[ps_ss], inc=(c == nch - 1))
        P.emit("act", lambda e: e.activation(out=rstd.ap, in_=ps_ss.ap, func=AF.Sqrt, bias=EPS, scale=1.0 / dim),
               reads=[ps_ss], writes=[rstd])
        P.emit("dve", lambda e: e.reciprocal(out=rstd.ap, in_=rstd.ap), reads=[rstd], writes=[rstd])
        for c in range(nch):
            P.emit("dve", (lambda c: lambda e: e.scalar_tensor_tensor(out=out_tiles[c].ap, in0=src_chunks[c].ap, scalar=vcol(gcol + c),
                                                                       in1=rstd.ap, op0=ALU.mult, op1=ALU.mult))(c),
                   reads=[src_chunks[c], rstd, vecs], writes=[out_tiles[c]])

    def wview(w, L, p=128):
        return w[L].rearrange("(k p) c -> p k c", p=p)

    def ffn(L, wgu, wd, gcol):
        mb, mf = abf.mark(), af.mark()
        u = [T(abf.alloc(NT)) for _ in range(KC)]
        sq = [T(abf.alloc(NT)) for _ in range(KC)]
        act = [T(abf.alloc(NT)) for _ in range(FC)]
        NG = 3
        gbuf = [T(abf.alloc(KC * 256)) for _ in range(NG)]
        vbuf = [T(abf.alloc(KC * 256)) for _ in range(NG)]
        dbuf = [T(abf.alloc(FC * 256)) for _ in range(2)]
        rstd = T(af.alloc(NT))
        sg = [T(af.alloc(NT)) for _ in range(2)]
        wg_v = wview(wgu, L)
        wd_v = wview(wd, L)
        gseq = [(t, j) for t in range(4) for j in range(11)]
        dseq = [(t, m) for t in range(4) for m in range(4)]
        gl = [0]
        dl = [0]

        def load_g(upto):
            while gl[0] <= upto and gl[0] < len(gseq):
                k = gl[0]
                _, j = gseq[k]
                b = k % NG
                gv = gbuf[b].ap.rearrange("p (k c) -> p k c", k=KC)
                vv = vbuf[b].ap.rearrange("p (k c) -> p k c", k=KC)
                P.emit("pool", (lambda gv, j: lambda e: e.dma_start(out=gv, in_=wg_v[:, :, j * 256:(j + 1) * 256]))(gv, j),
                       writes=[gbuf[b]], kind="dma")
                P.emit("pool", (lambda vv, j: lambda e: e.dma_start(out=vv, in_=wg_v[:, :, DFF + j * 256:DFF + (j + 1) * 256]))(vv, j),
                       writes=[vbuf[b]], kind="dma")
                gl[0] += 1

        def load_d(upto):
            while dl[0] <= upto and dl[0] < len(dseq):
                k = dl[0]
                _, m = dseq[k]
                b = k % 2
                dv = dbuf[b].ap.rearrange("p (f c) -> p f c", f=FC)
                P.emit("pool", (lambda dv, m: lambda e: e.dma_start(out=dv, in_=wd_v[:, :, m * 256:(m + 1) * 256]))(dv, m),
                       writes=[dbuf[b]], kind="dma")
                dl[0] += 1

        load_g(1)
        load_d(0)
        gk = 0
        dk = 0
        for t in range(4):
            ps_ss = psum[0]
            rmsnorm([h[kc][t] for kc in range(KC)], KC, D, gcol, u, sq, ps_ss, rstd)
            for j in range(11):
                load_g(gk + 2)
                b = gk % NG
                gv = gbuf[b].ap.rearrange("p (k c) -> p k c", k=KC)
                vv = vbuf[b].ap.rearrange("p (k c) -> p k c", k=KC)
                for jj in range(2):
                    fc = 2 * j + jj
                    pg = psum[1 + (fc % 2) * 2]
                    pv = psum[2 + (fc % 2) * 2]
                    for kc in range(KC):
                        P.emit("pe", (lambda kc, jj, gv, pg: lambda e: e.matmul(pg.ap, lhsT=gv[:, kc, jj * 128:(jj + 1) * 128], rhs=u[kc].ap,
                                                                                start=(kc == 0), stop=(kc == KC - 1)))(kc, jj, gv, pg),
                               reads=[gbuf[b], u[kc]], writes=[pg], inc=(kc == KC - 1))
                    for kc in range(KC):
                        P.emit("pe", (lambda kc, jj, vv, pv: lambda e: e.matmul(pv.ap, lhsT=vv[:, kc, jj * 128:(jj + 1) * 128], rhs=u[kc].ap,
                                                                                start=(kc == 0), stop=(kc == KC - 1)))(kc, jj, vv, pv),
                               reads=[vbuf[b], u[kc]], writes=[pv], inc=(kc == KC - 1))
                    sgt = sg[fc % 2]
                    P.emit("act", (lambda pg, sgt: lambda e: e.activation(out=sgt.ap, in_=pg.ap, func=AF.Silu))(pg, sgt),
                           reads=[pg], writes=[sgt])
                    P.emit("dve", (lambda pv, sgt, fc: lambda e: e.tensor_tensor(out=act[fc].ap, in0=sgt.ap, in1=pv.ap, op=ALU.mult))(pv, sgt, fc),
                           reads=[pv, sgt], writes=[act[fc]])
                gk += 1
            for m in range(4):
                load_d(dk + 1)
                b = dk % 2
                dv = dbuf[b].ap.rearrange("p (f c) -> p f c", f=FC)
                for jj in range(2):
                    dc = 2 * m + jj
                    po = psum[5 + (dc % 2)]
                    for fc in range(FC):
                        P.emit("pe", (lambda fc, jj, dv, po: lambda e: e.matmul(po.ap, lhsT=dv[:, fc, jj * 128:(jj + 1) * 128], rhs=act[fc].ap,
                                                                                start=(fc == 0), stop=(fc == FC - 1)))(fc, jj, dv, po),
                               reads=[dbuf[b], act[fc]], writes=[po], inc=(fc == FC - 1))
                    ht = h[dc][t]
                    P.emit("dve", (lambda po, ht: lambda e: e.scalar_tensor_tensor(out=ht.ap, in0=po.ap, scalar=0.5, in1=ht.ap,
                                                                                    op0=ALU.mult, op1=ALU.add))(po, ht),
                           reads=[po, ht], writes=[ht])
                dk += 1
        P.barrier()
        abf.reset(mb)
        af.reset(mf)

    def ple(L):
        mb, mf = abf.mark(), af.mark()
        u = [T(abf.alloc(NT)) for _ in range(KC)]
        sq = [T(abf.alloc(NT)) for _ in range(KC)]
        wg = T(abf.alloc(KC * D))
        wp = T(abf.alloc(2 * D))
        pt = [T(abf.alloc(2 * NT)) for _ in range(2)]
        rstd = T(af.alloc(NT))
        sgm = [T(af.alloc(NT)) for _ in range(2)]
        wgv = wg.ap.rearrange("p (k c) -> p k c", k=KC)
        wpv = wp.ap.rearrange("p (k c) -> p k c", k=2)
        for half in range(2):
            P.emit("pool", (lambda half: lambda e: e.dma_start(out=wgv[:, half * 4:(half + 1) * 4, :],
                                                              in_=wview(wgate, L)[:, half * 4:(half + 1) * 4, :]))(half),
                   writes=[wg], kind="dma")
        P.emit("pool", lambda e: e.dma_start(out=wpv, in_=wview(wproj, L)), writes=[wp], kind="dma")
        for t in range(4):
            ptv = pt[t % 2].ap.rearrange("p (k c) -> p k c", k=2)
            P.emit("pool", (lambda ptv, t: lambda e: e.dma_start(out=ptv, in_=pT[L].rearrange("(k p) t -> p k t", p=128)[:, :, t * NT:(t + 1) * NT]))(ptv, t),
                   writes=[pt[t % 2]], kind="dma")
            rmsnorm([h[kc][t] for kc in range(KC)], KC, D, V_GPLE + L * 8, u, sq, psum[0], rstd)
            for dc in range(KC):
                pg = psum[1 + (dc % 2) * 2]
                pp = psum[2 + (dc % 2) * 2]
                for kc in range(KC):
                    P.emit("pe", (lambda kc, dc, pg: lambda e: e.matmul(pg.ap, lhsT=wgv[:, kc, dc * 128:(dc + 1) * 128], rhs=u[kc].ap,
                                                                        start=(kc == 0), stop=(kc == KC - 1)))(kc, dc, pg),
                           reads=[wg, u[kc]], writes=[pg], inc=(kc == KC - 1))
                for k2 in range(2):
                    P.emit("pe", (lambda k2, dc, pp, ptv: lambda e: e.matmul(pp.ap, lhsT=wpv[:, k2, dc * 128:(dc + 1) * 128], rhs=ptv[:, k2, :],
                                                                             start=(k2 == 0), stop=(k2 == 1)))(k2, dc, pp, ptv),
                           reads=[wp, pt[t % 2]], writes=[pp], inc=(k2 == 1))
                s_ = sgm[dc % 2]
                P.emit("act", (lambda pg, s_: lambda e: e.activation(out=s_.ap, in_=pg.ap, func=AF.Sigmoid))(pg, s_), reads=[pg], writes=[s_])
                P.emit("dve", (lambda pp, s_: lambda e: e.tensor_tensor(out=s_.ap, in0=s_.ap, in1=pp.ap, op=ALU.mult))(pp, s_),
                       reads=[pp, s_], writes=[s_])
                ht = h[dc][t]
                P.emit("dve", (lambda s_, ht: lambda e: e.tensor_tensor(out=ht.ap, in0=ht.ap, in1=s_.ap, op=ALU.add))(s_, ht),
                       reads=[s_, ht], writes=[ht])
        P.barrier()
        abf.reset(mb)
        af.reset(mf)

    def load_u(L, T8, ut, ug_t):
        rank, lt = T8 // 4, T8 % 4
        for c in range(4):
            for i in range(2):
                kc = 2 * c + i
                P.emit("sync", (lambda kc, c, i: lambda e: e.dma_start(
                    out=ut[kc].ap, in_=u_g[L][c][rank * 256 + i * 128: rank * 256 + (i + 1) * 128, lt * NT:(lt + 1) * NT]))(kc, c, i),
                    reads=[ug_t[c]], writes=[ut[kc]], kind="dma")

    def load_tab(idx, T8, dst):
        P.emit("sync", lambda e: e.dma_start(out=dst.ap, in_=tabs[idx][:, T8 * NT:(T8 + 1) * NT]), writes=[dst], kind="dma")

    def proj_fm(ps, wv, c0, ncol, ut, wt):
        for kc in range(KC):
            P.emit("pe", (lambda kc: lambda e: e.matmul(ps.ap[0:ncol, :], lhsT=wv[:, kc, c0:c0 + ncol], rhs=ut[kc].ap,
                                                        start=(kc == 0), stop=(kc == KC - 1)))(kc),
                   reads=[wt, ut[kc]], writes=[ps], inc=(kc == KC - 1))

    def proj_tm(ps, wv, c0, ncol, ut, wt, tb):
        for kc in range(KC):
            P.emit("pe", (lambda kc: lambda e: e.matmul(ps.ap[:, 0:ncol], lhsT=ut[kc].ap[:, tb * 128:(tb + 1) * 128], rhs=wv[:, kc, c0:c0 + ncol],
                                                        start=(kc == 0), stop=(kc == KC - 1)))(kc),
                   reads=[wt, ut[kc]], writes=[ps], inc=(kc == KC - 1))

    def rope_evac(px, pxp, cosT, sinT, nrow, out_ap, out_t, t1, t2, scale):
        P.emit("dve", lambda e: e.scalar_tensor_tensor(out=t1.ap[0:nrow, :], in0=px.ap[0:nrow, :], scalar=scale, in1=cosT.ap[0:nrow, :],
                                                       op0=ALU.mult, op1=ALU.mult),
               reads=[px, cosT], writes=[t1])
        P.emit("dve", lambda e: e.scalar_tensor_tensor(out=t2.ap[0:nrow, :], in0=pxp.ap[0:nrow, :], scalar=scale, in1=sinT.ap[0:nrow, :],
                                                       op0=ALU.mult, op1=ALU.mult),
               reads=[pxp, sinT], writes=[t2])
        P.emit("dve", lambda e: e.tensor_tensor(out=out_ap, in0=t1.ap[0:nrow, :], in1=t2.ap[0:nrow, :], op=ALU.add),
               reads=[t1, t2], writes=[out_t])

    def softmax_attn(qT_ap, kT_ap, qk_t, v_t, vaug_fn, Q8, pO, pD, sbank, pts, den_ones):
        nkb = 4 * Q8 + 4
        q_ap = qT_ap[:, Q8 * NT:(Q8 + 1) * NT]

        NB = len(sbank)
        LA = NB - 1

        def s_mm(i):
            ps = sbank[i % NB]
            P.emit("pe", lambda e: e.matmul(ps.ap, lhsT=kT_ap[:, i * 128:(i + 1) * 128], rhs=q_ap, start=True, stop=True),
                   reads=[qk_t], writes=[ps])
        for i0 in range(min(LA, nkb)):
            s_mm(i0)
        for i in range(nkb):
            if i + LA < nkb:
                s_mm(i + LA)
            ps = sbank[i % NB]
            pt_ = pts[i % len(pts)]
            P.emit("act", (lambda ps, pt_: lambda e: e.activation(out=pt_.ap, in_=ps.ap, func=AF.Exp))(ps, pt_), reads=[ps], writes=[pt_])
            jd = i - 4 * Q8
            if jd >= 0:
                P.emit("dve", (lambda pt_, jd: lambda e: e.tensor_tensor(out=pt_.ap, in0=pt_.ap, in1=maskI[jd], op=ALU.mult))(pt_, jd),
                       reads=[pt_, cm], writes=[pt_])
            va = vaug_fn(i)
            P.emit("pe", (lambda va, pt_, i: lambda e: e.matmul(pO.ap, lhsT=va, rhs=pt_.ap, start=(i == 0), stop=(i == nkb - 1)))(va, pt_, i),
                   reads=[v_t, pt_], writes=[pO])
            if den_ones:
                P.emit("pe", (lambda pt_, i: lambda e: e.matmul(pD.ap, lhsT=ones.ap, rhs=pt_.ap, start=(i == 0), stop=(i == nkb - 1)))(pt_, i),
                       reads=[ones, pt_], writes=[pD])

    def mix_gather(L, ug_t):
        mb0, mf0 = abf.mark(), af.mark()
        u = [T(abf.alloc(NT)) for _ in range(KC)]
        sq = [T(abf.alloc(NT)) for _ in range(KC)]
        rstd = T(af.alloc(NT))
        uloc_t = [T(None) for c in range(4)]
        for t in range(4):
            rmsnorm([h[kc][t] for kc in range(KC)], KC, D, V_GMIX + L * 8, u, sq, psum[0], rstd)
            for kc in range(KC):
                c, i = kc // 2, kc % 2
                P.emit("sync", (lambda kc, c, i, t: lambda e: e.dma_start(out=u_loc[L][c][i * 128:(i + 1) * 128, t * NT:(t + 1) * NT], in_=u[kc].ap))(kc, c, i, t),
                       reads=[u[kc]], writes=[uloc_t[c]], kind="dma")
        for c in range(4):
            P.emit("pool", (lambda c: lambda e: e.collective_compute("AllGather", ALU.bypass, replica_groups=GROUPS,
                                                                    ins=[u_loc[L][c].ap().opt()], outs=[u_g[L][c].ap().opt()]))(c),
                   reads=[uloc_t[c]], writes=[ug_t[c]], kind="cc")
        P.barrier()
        abf.reset(mb0)
        af.reset(mf0)

    def mix_sb(L, ug_t, mxl_t):
        mb, mf = abf.mark(), af.mark()
        win_v = wview(win, L)
        wt = T(abf.alloc(KC * 384))
        wv = wt.ap.rearrange("p (k c) -> p k c", k=KC)
        P.emit("pool", lambda e: e.dma_start(out=wv, in_=win_v[:, :, 0:384]), writes=[wt], kind="dma")
        qk_t = T(None, "qk")
        v_t = T(None, "v")
        qT = abf.alloc(S)
        kT = abf.alloc(S)
        vv_ = abf.alloc(32 * 128).rearrange("p (b c) -> p b c", b=32)
        uts = [[T(abf.alloc(NT)) for _ in range(KC)] for _ in range(2)]
        for T8 in range(8):
            ut = uts[T8 % 2]
            load_u(L, T8, ut, ug_t)
            pq, pk = psum[(T8 % 2) * 2], psum[(T8 % 2) * 2 + 1]
            proj_fm(pq, wv, 0, 128, ut, wt)
            proj_fm(pk, wv, 128, 128, ut, wt)
            P.emit("act", (lambda pq, T8: lambda e: e.activation(out=qT[:, T8 * NT:(T8 + 1) * NT], in_=pq.ap, func=AF.Copy, scale=0.125))(pq, T8),
                   reads=[pq], writes=[qk_t])
            P.emit("dve", (lambda pk, T8: lambda e: e.tensor_copy(out=kT[:, T8 * NT:(T8 + 1) * NT], in_=pk.ap))(pk, T8), reads=[pk], writes=[qk_t])
            for tb in range(4):
                pvv = psum[4 + tb % 2]
                proj_tm(pvv, wv, 256, 128, ut, wt, tb)
                P.emit("act", (lambda pvv, T8, tb: lambda e: e.activation(out=vv_[:, T8 * 4 + tb, :], in_=pvv.ap[:, 0:128], func=AF.Copy))(pvv, T8, tb),
                       reads=[pvv], writes=[v_t])
        ebuf = [T(af.alloc(NT)) for _ in range(3)]
        t1b = [T(af.alloc(NT)) for _ in range(2)]
        Rt = T(af.alloc(NT))
        spb = [T(abf.alloc(NT)) for _ in range(3)]
        Ab = [T(abf.alloc(NT)) for _ in range(2)]
        ob = [T(abf.alloc(NT)) for _ in range(2)]
        zb = [psum[0], psum[1], psum[7]]
        cb = [psum[2], psum[3]]
        csb = [psum[4], psum[5]]
        pO = psum[6]

        def sb_tile(hh, Q8):
            r0 = hh * 64
            nkb = 4 * Q8 + 4
            order = list(range(nkb - 1, -1, -1))
            q_ap = qT[r0:r0 + 64, Q8 * NT:(Q8 + 1) * NT]

            def st1(n):
                i = order[n]
                pz = zb[n % 3]
                P.emit("pe", lambda e: e.matmul(pz.ap, lhsT=kT[r0:r0 + 64, i * 128:(i + 1) * 128], rhs=q_ap, start=True, stop=True),
                       reads=[qk_t], writes=[pz])
                eb, sp = ebuf[n % 3], spb[n % 3]
                P.emit("act", lambda e: e.activation(out=eb.ap, in_=pz.ap, func=AF.Exp), reads=[pz], writes=[eb])
                P.emit("act", lambda e: e.activation(out=sp.ap, in_=eb.ap, func=AF.Ln, bias=1.0, scale=1.0), reads=[eb], writes=[sp])
                jd = i - 4 * Q8
                if jd >= 0:
                    P.emit("dve", lambda e: e.tensor_tensor(out=sp.ap, in0=sp.ap, in1=maskS[jd], op=ALU.mult), reads=[sp, cm], writes=[sp])

            def st2(n):
                i = order[n]
                pz, sp = zb[n % 3], spb[n % 3]
                pc, pcs = cb[n % 2], csb[n % 2]
                P.emit("pe", lambda e: e.matmul(pc.ap, lhsT=trim, rhs=sp.ap, start=True, stop=True), reads=[cm, sp], writes=[pc])
                if n < nkb - 1:
                    P.emit("pe", lambda e: e.matmul(pcs.ap, lhsT=ones.ap, rhs=sp.ap, start=True, stop=True), reads=[ones, sp], writes=[pcs])
                t1 = t1b[n % 2]
                if n == 0:
                    P.emit("dve", lambda e: e.tensor_copy(out=t1.ap, in_=pz.ap), reads=[pz], writes=[t1])
                else:
                    P.emit("dve", lambda e: e.tensor_tensor(out=t1.ap, in0=pz.ap, in1=Rt.ap, op=ALU.subtract), reads=[pz, Rt], writes=[t1])
                P.emit("dve", lambda e: e.tensor_tensor(out=t1.ap, in0=t1.ap, in1=pc.ap, op=ALU.subtract), reads=[t1, pc], writes=[t1])
                A = Ab[n % 2]
                P.emit("act", lambda e: e.activation(out=A.ap, in_=t1.ap, func=AF.Exp), reads=[t1], writes=[A])
                jd = i - 4 * Q8
                if jd >= 0:
                    P.emit("dve", lambda e: e.tensor_tensor(out=A.ap, in0=A.ap, in1=maskS[jd], op=ALU.mult), reads=[A, cm], writes=[A])
                if n < nkb - 1:
                    if n == 0:
                        P.emit("dve", lambda e: e.tensor_copy(out=Rt.ap, in_=pcs.ap), reads=[pcs], writes=[Rt])
                    else:
                        P.emit("dve", lambda e: e.tensor_tensor(out=Rt.ap, in0=Rt.ap, in1=pcs.ap, op=ALU.add), reads=[pcs, Rt], writes=[Rt])
                P.emit("pe", lambda e: e.matmul(pO.ap[0:64, :], lhsT=vv_[:, i, r0:r0 + 64], rhs=A.ap, start=(n == 0), stop=(n == nkb - 1)),
                       reads=[v_t, A], writes=[pO])
            st1(0)
            st1(1)
            for n in range(nkb):
                if n + 2 < nkb:
                    st1(n + 2)
                st2(n)
            o_ = ob[Q8 % 2]
            P.emit("act", lambda e: e.activation(out=o_.ap[0:64, :], in_=pO.ap[0:64, :], func=AF.Copy), reads=[pO], writes=[o_])
            P.emit("sync", lambda e: e.dma_start(out=mx_loc[L][0][r0:r0 + 64, Q8 * NT:(Q8 + 1) * NT], in_=o_.ap[0:64, :]),
                   reads=[o_], writes=[mxl_t[0]], kind="dma")
        for hh in range(2):
            for Q8 in range(8):
                sb_tile(hh, Q8)
        P.barrier()
        abf.reset(mb)
        af.reset(mf)

    def mix_diff(L, ug_t, mxl_t):
        mb, mf = abf.mark(), af.mark()
        win_v = wview(win, L)
        wt = T(abf.alloc(KC * 1280))
        wv = wt.ap.rearrange("p (k c) -> p k c", k=KC)
        for part in range(5):
            P.emit("pool", (lambda part: lambda e: e.dma_start(out=wv[:, :, part * 256:(part + 1) * 256],
                                                              in_=win_v[:, :, 384 + part * 256:384 + (part + 1) * 256]))(part),
                   writes=[wt], kind="dma")
        qk_t = T(None, "qk")
        v_t = T(None, "v")
        qT2 = abf.alloc(2 * S).rearrange("p (h t) -> p h t", h=2)
        kT2 = abf.alloc(2 * S).rearrange("p (h t) -> p h t", h=2)
        vd = abf.alloc(32 * 256).rearrange("p (b c) -> p b c", b=32)
        uts = [[T(abf.alloc(NT)) for _ in range(KC)] for _ in range(2)]
        cosT = [T(af.alloc(NT)) for _ in range(2)]
        sinT = [T(af.alloc(NT)) for _ in range(2)]
        t1 = T(af.alloc(NT))
        t2 = T(af.alloc(NT))
        for T8 in range(8):
            ut = uts[T8 % 2]
            load_u(L, T8, ut, ug_t)
            load_tab(0, T8, cosT[T8 % 2])
            load_tab(1, T8, sinT[T8 % 2])
            n = 0
            for which, dstT, cbase in ((0, qT2, 0), (1, kT2, 512)):
                for hh in range(2):
                    px, pxp = psum[(n % 2) * 2], psum[(n % 2) * 2 + 1]
                    n += 1
                    proj_fm(px, wv, cbase + hh * 128, 128, ut, wt)
                    proj_fm(pxp, wv, cbase + 256 + hh * 128, 128, ut, wt)
                    rope_evac(px, pxp, cosT[T8 % 2], sinT[T8 % 2], 128, dstT[:, hh, T8 * NT:(T8 + 1) * NT], qk_t, t1, t2,
                              0.125 if which == 0 else 1.0)
            for tb in range(4):
                pvv = psum[4 + tb % 2]
                proj_tm(pvv, wv, 1024, 256, ut, wt, tb)
                P.emit("act", (lambda pvv, T8, tb: lambda e: e.activation(out=vd[:, T8 * 4 + tb, :], in_=pvv.ap[:, 0:256], func=AF.Copy))(pvv, T8, tb),
                       reads=[pvv], writes=[v_t])
        P.barrier()
        af.reset(mf)
        pts = [T(abf.alloc(NT)) for _ in range(3)]
        ob = [T(abf.alloc(NT)) for _ in range(1)]
        sqd = T(abf.alloc(NT))
        rec = T(af.alloc(NT))
        o1 = T(af.alloc(NT))
        o2 = T(af.alloc(NT))
        rstd = T(af.alloc(NT))
        lam_init = 0.8 - 0.6 * math.exp(-0.3 * L)

        def diff_tile(hh, Q8):
            for comp in range(2):
                r0 = comp * 64
                pO, pD = psum[2 + comp * 2], psum[3 + comp * 2]
                softmax_attn(qT2[r0:r0 + 64, hh, :], kT2[r0:r0 + 64, hh, :], qk_t, v_t,
                             (lambda i: vd[:, i, hh * 128:(hh + 1) * 128]), Q8, pO, pD, [psum[0], psum[1], psum[7]], pts, True)
                oc = o1 if comp == 0 else o2
                P.emit("dve", (lambda pD: lambda e: e.reciprocal(out=rec.ap, in_=pD.ap))(pD), reads=[pD], writes=[rec])
                P.emit("dve", (lambda pO, oc: lambda e: e.tensor_tensor(out=oc.ap, in0=pO.ap, in1=rec.ap, op=ALU.mult))(pO, oc),
                       reads=[pO, rec], writes=[oc])
            P.emit("dve", lambda e: e.scalar_tensor_tensor(out=o1.ap, in0=o2.ap, scalar=lamv.ap[:, L:L + 1], in1=o1.ap, op0=ALU.mult, op1=ALU.add),
                   reads=[o1, o2, lamv], writes=[o1])
            P.emit("act", lambda e: e.activation(out=sqd.ap, in_=o1.ap, func=AF.Square), reads=[o1], writes=[sqd])
            pss = psum[6]
            P.emit("pe", lambda e: e.matmul(pss.ap, lhsT=ones.ap, rhs=sqd.ap, start=True, stop=True), reads=[ones, sqd], writes=[pss])
            P.emit("act", lambda e: e.activation(out=rstd.ap, in_=pss.ap, func=AF.Sqrt, bias=EPS, scale=1.0 / 128),
                   reads=[pss], writes=[rstd])
            P.emit("dve", lambda e: e.reciprocal(out=rstd.ap, in_=rstd.ap), reads=[rstd], writes=[rstd])
            P.emit("dve", lambda e: e.tensor_scalar(out=rstd.ap, in0=rstd.ap, scalar1=(1.0 - lam_init), scalar2=None, op0=ALU.mult),
                   reads=[rstd], writes=[rstd])
            o_ = ob[0]
            P.emit("dve", lambda e: e.scalar_tensor_tensor(out=o_.ap, in0=o1.ap, scalar=vcol(V_SUBLN + L), in1=rstd.ap, op0=ALU.mult, op1=ALU.mult),
                   reads=[o1, rstd, vecs], writes=[o_])
            P.emit("sync", lambda e: e.dma_start(out=mx_loc[L][1 + hh][:, Q8 * NT:(Q8 + 1) * NT], in_=o_.ap),
                   reads=[o_], writes=[mxl_t[1 + hh]], kind="dma")
        for hh in range(2):
            for Q8 in range(8):
                diff_tile(hh, Q8)
        P.barrier()
        abf.reset(mb)
        af.reset(mf)

    def mix_mla(L, ug_t, mxl_t):
        mb, mf = abf.mark(), af.mark()
        win_v = wview(win, L)
        wt = T(abf.alloc(KC * 448))
        wv = wt.ap.rearrange("p (k c) -> p k c", k=KC)
        P.emit("pool", lambda e: e.dma_start(out=wv, in_=win_v[:, :, 1664:2112]), writes=[wt], kind="dma")
        wq_t = T(abf.alloc(2 * 384))
        wqv = wq_t.ap.rearrange("p (k c) -> p k c", k=2)
        P.emit("pool", lambda e: e.dma_start(out=wqv, in_=wview(wuq, L)), writes=[wq_t], kind="dma")
        wkv_t = T(abf.alloc(256))
        P.emit("pool", lambda e: e.dma_start(out=wkv_t.ap, in_=wukv[L]), writes=[wkv_t], kind="dma")
        qk_t = T(None, "qk")
        v_t = T(None, "v")
        qT2 = abf.alloc(2 * S).rearrange("p (h t) -> p h t", h=2)
        kT2 = abf.alloc(2 * S).rearrange("p (h t) -> p h t", h=2)
        vm = abf.alloc(32 * 256).rearrange("p (b c) -> p b c", b=32)
        P.emit("dve", lambda e: e.memset(vm, 1.0), writes=[v_t])
        uts = [[T(abf.alloc(NT)) for _ in range(KC)] for _ in range(2)]
        cqn = [T(abf.alloc(NT)) for _ in range(2)]
        ckvn = [T(abf.alloc(NT))]
        sqm = [T(abf.alloc(NT)) for _ in range(2)]
        cosM = [T(af.alloc(NT)) for _ in range(2)]
        sinM = [T(af.alloc(NT)) for _ in range(2)]
        cosK = [T(af.alloc(NT)) for _ in range(2)]
        sinK = [T(af.alloc(NT)) for _ in range(2)]
        cq = [T(af.alloc(NT)) for _ in range(2)]
        ckv = [T(af.alloc(NT))]
        t1 = T(af.alloc(NT))
        t2 = T(af.alloc(NT))
        rstd = T(af.alloc(NT))
        sc_m = 96.0 ** -0.5

        def mla_proj(T8):
            ut = uts[T8 % 2]
            load_u(L, T8, ut, ug_t)
            b2 = T8 % 2
            load_tab(2, T8, cosM[b2])
            load_tab(3, T8, sinM[b2])
            load_tab(4, T8, cosK[b2])
            load_tab(5, T8, sinK[b2])
            for c in range(2):
                proj_fm(psum[c], wv, c * 128, 128, ut, wt)
                P.emit("act", (lambda c: lambda e: e.activation(out=cq[c].ap, in_=psum[c].ap, func=AF.Copy))(c), reads=[psum[c]], writes=[cq[c]])
            proj_fm(psum[2], wv, 256, 128, ut, wt)
            P.emit("act", lambda e: e.activation(out=ckv[0].ap, in_=psum[2].ap, func=AF.Copy), reads=[psum[2]], writes=[ckv[0]])
            rmsnorm(cq, 2, 256, V_QN + L * 2, cqn, sqm, psum[3], rstd)
            rmsnorm(ckv, 1, 128, V_KVN + L, ckvn, sqm, psum[3], rstd)
            for hh in range(2):
                px, pxp = psum[4], psum[5]
                for c in range(2):
                    P.emit("pe", (lambda c, hh: lambda e: e.matmul(px.ap[0:96, :], lhsT=wqv[:, c, hh * 96:(hh + 1) * 96], rhs=cqn[c].ap,
                                                                   start=(c == 0), stop=(c == 1)))(c, hh), reads=[wq_t, cqn[c]], writes=[px])
                for c in range(2):
                    P.emit("pe", (lambda c, hh: lambda e: e.matmul(pxp.ap[0:96, :], lhsT=wqv[:, c, 192 + hh * 96:192 + (hh + 1) * 96], rhs=cqn[c].ap,
                                                                   start=(c == 0), stop=(c == 1)))(c, hh), reads=[wq_t, cqn[c]], writes=[pxp])
                rope_evac(px, pxp, cosM[b2], sinM[b2], 96, qT2[0:96, hh, T8 * NT:(T8 + 1) * NT], qk_t, t1, t2, sc_m)
            for hh in range(2):
                pkn = psum[6]
                P.emit("pe", (lambda hh: lambda e: e.matmul(pkn.ap[0:64, :], lhsT=wkv_t.ap[:, hh * 64:(hh + 1) * 64], rhs=ckvn[0].ap, start=True, stop=True))(hh),
                       reads=[wkv_t, ckvn[0]], writes=[pkn])
                P.emit("act", (lambda hh: lambda e: e.activation(out=kT2[0:64, hh, T8 * NT:(T8 + 1) * NT], in_=pkn.ap[0:64, :], func=AF.Copy))(hh),
                       reads=[pkn], writes=[qk_t])
            px, pxp = psum[4], psum[5]
            proj_fm(px, wv, 384, 32, ut, wt)
            proj_fm(pxp, wv, 416, 32, ut, wt)
            rope_evac(px, pxp, cosK[b2], sinK[b2], 32, t1.ap[0:32, :], t1, t1, t2, 1.0)
            for hh in range(2):
                P.emit("act", (lambda hh: lambda e: e.activation(out=kT2[64:96, hh, T8 * NT:(T8 + 1) * NT], in_=t1.ap[0:32, :], func=AF.Copy))(hh),
                       reads=[t1], writes=[qk_t])
            for tb in range(4):
                pvv = psum[7]
                P.emit("pe", (lambda tb: lambda e: e.matmul(pvv.ap[:, 0:128], lhsT=ckvn[0].ap[:, tb * 128:(tb + 1) * 128], rhs=wkv_t.ap[:, 128:256],
                                                            start=True, stop=True))(tb), reads=[wkv_t, ckvn[0]], writes=[pvv])
                for hh in range(2):
                    P.emit("act", (lambda tb, hh: lambda e: e.activation(out=vm[:, T8 * 4 + tb, hh * 128:hh * 128 + 64],
                                                                         in_=pvv.ap[:, hh * 64:(hh + 1) * 64], func=AF.Copy))(tb, hh),
                           reads=[pvv], writes=[v_t])
        for T8 in range(8):
            mla_proj(T8)
        P.barrier()
        af.reset(mf)
        pts = [T(abf.alloc(NT)) for _ in range(4)]
        ob = [T(abf.alloc(NT)) for _ in range(2)]
        rec = T(af.alloc(NT))

        def mla_tile(hh, Q8):
            pO = psum[2 + (Q8 % 2)]
            softmax_attn(qT2[0:96, hh, :], kT2[0:96, hh, :], qk_t, v_t,
                         (lambda i: vm[:, i, hh * 128:(hh + 1) * 128]), Q8, pO, None, [psum[0], psum[1], psum[4], psum[5]], pts, False)
            P.emit("act", lambda e: e.activation(out=rec.ap[0:64, :], in_=pO.ap[64:128, :], func=AF.Copy), reads=[pO], writes=[rec])
            P.emit("dve", lambda e: e.reciprocal(out=rec.ap[0:64, :], in_=rec.ap[0:64, :]), reads=[rec], writes=[rec])
            o_ = ob[Q8 % 2]
            P.emit("dve", lambda e: e.tensor_tensor(out=o_.ap[0:64, :], in0=pO.ap[0:64, :], in1=rec.ap[0:64, :], op=ALU.mult),
                   reads=[pO, rec], writes=[o_])
            P.emit("sync", lambda e: e.dma_start(out=mx_loc[L][3][hh * 64:(hh + 1) * 64, Q8 * NT:(Q8 + 1) * NT], in_=o_.ap[0:64, :]),
                   reads=[o_], writes=[mxl_t[3]], kind="dma")
        for hh in range(2):
            for Q8 in range(8):
                mla_tile(hh, Q8)
        P.barrier()
        abf.reset(mb)
        af.reset(mf)

    def mix_out(L, mxl_t):
        mb, mf = abf.mark(), af.mark()
        mxg_t = [T(None) for c in range(4)]
        for c in range(4):
            P.emit("pool", (lambda c: lambda e: e.collective_compute("AllGather", ALU.bypass, replica_groups=GROUPS,
                                                                    ins=[mx_loc[L][c].ap().opt()], outs=[mx_g[L][c].ap().opt()]))(c),
                   reads=[mxl_t[c]], writes=[mxg_t[c]], kind="cc")
        wo_t = T(abf.alloc(KC * D))
        wov = wo_t.ap.rearrange("p (k c) -> p k c", k=KC)
        for half in range(2):
            P.emit("pool", (lambda half: lambda e: e.dma_start(out=wov[:, half * 4:(half + 1) * 4, :],
                                                              in_=wview(wout, L)[:, half * 4:(half + 1) * 4, :]))(half),
                   writes=[wo_t], kind="dma")
        ca = [[T(abf.alloc(NT)) for _ in range(KC)] for _ in range(2)]
        cb_ = [[T(abf.alloc(NT)) for _ in range(KC)] for _ in range(2)]
        ms = [[T(abf.alloc(NT)) for _ in range(KC)] for _ in range(2)]

        def wo_tile(t):
            for K in range(KC):
                c, rp = K // 2, K % 2
                A, B, M = ca[t % 2][K], cb_[t % 2][K], ms[t % 2][K]
                P.emit("sync", (lambda A, c, rp: lambda e: e.dma_start(out=A.ap, in_=mx_g[L][c][rp * 128:(rp + 1) * 128, t * NT:(t + 1) * NT]))(A, c, rp),
                       reads=[mxg_t[c]], writes=[A], kind="dma")
                P.emit("sync", (lambda B, c, rp: lambda e: e.dma_start(out=B.ap, in_=mx_g[L][c][rp * 128:(rp + 1) * 128, TL + t * NT:TL + (t + 1) * NT]))(B, c, rp),
                       reads=[mxg_t[c]], writes=[B], kind="dma")
                P.emit("dve", (lambda A, M: lambda e: e.tensor_scalar(out=M.ap, in0=A.ap, scalar1=vcol(V_SEL), scalar2=None, op0=ALU.mult))(A, M),
                       reads=[A, vecs], writes=[M])
                P.emit("dve", (lambda B, M: lambda e: e.scalar_tensor_tensor(out=M.ap, in0=B.ap, scalar=vcol(V_SEL + 1), in1=M.ap,
                                                                             op0=ALU.mult, op1=ALU.add))(B, M),
                       reads=[B, M, vecs], writes=[M])
            if dbg is not None and dbg[0] == "mixraw":
                for K in range(KC):
                    M = ms[t % 2][K]
                    P.emit("pool", (lambda K, M: lambda e: e.dma_start(out=outT[K * 128:(K + 1) * 128, t * NT:(t + 1) * NT], in_=M.ap))(K, M),
                           reads=[M], kind="dma")
                return
            for dc in range(KC):
                po = psum[dc % 2]
                for K in range(KC):
                    M = ms[t % 2][K]
                    P.emit("pe", (lambda K, M, dc, po: lambda e: e.matmul(po.ap, lhsT=wov[:, K, dc * 128:(dc + 1) * 128], rhs=M.ap,
                                                                          start=(K == 0), stop=(K == KC - 1)))(K, M, dc, po),
                           reads=[wo_t, M], writes=[po], inc=(K == KC - 1))
                ht = h[dc][t]
                P.emit("dve", (lambda po, ht: lambda e: e.tensor_tensor(out=ht.ap, in0=ht.ap, in1=po.ap, op=ALU.add))(po, ht),
                       reads=[po, ht], writes=[ht])
        for t in range(4):
            wo_tile(t)
        P.barrier()
        abf.reset(mb)
        af.reset(mf)

    def mixer(L):
        ug_t = [T(None) for c in range(4)]
        mxl_t = [T(None) for c in range(4)]
        mix_gather(L, ug_t)
        mix_sb(L, ug_t, mxl_t)
        mix_diff(L, ug_t, mxl_t)
        mix_mla(L, ug_t, mxl_t)
        mix_out(L, mxl_t)

    def final():
        sq = [T(abf.alloc(NT)) for _ in range(KC)]
        rstd = T(af.alloc(NT))
        ot = [T(af.alloc(NT)) for _ in range(KC)]
        for t in range(4):
            rmsnorm([h[kc][t] for kc in range(KC)], KC, D, V_GFIN, ot, sq, psum[0], rstd)
            for kc in range(KC):
                P.emit("sync", (lambda kc, t: lambda e: e.dma_start(out=outT[kc * 128:(kc + 1) * 128, t * NT:(t + 1) * NT], in_=ot[kc].ap))(kc, t),
                       reads=[ot[kc]], kind="dma")

    def dump_h():
        for kc in range(KC):
            P.emit("sync", (lambda kc: lambda e: e.dma_start(out=outT[kc * 128:(kc + 1) * 128, :], in_=hT[:, kc, :]))(kc),
                   reads=h[kc], kind="dma")

    setup()
    stop = False
    if dbg is not None and dbg[0] == "setup":
        stop = True
        depth = 0
    for L in range(depth):
        ffn(L, w1gu, w1d, V_GF1 + L * 8)
        if dbg == ("ffn1", L):
            stop = True
            break
        mixer(L)
        if dbg == ("mix", L):
            stop = True
            break
        if dbg == ("mixraw", L):
            stop = None
            break
        ffn(L, w2gu, w2d, V_GF2 + L * 8)
        ple(L)
        if dbg == ("layer", L):
            stop = True
            break
    if stop:
        dump_h()
    elif stop is None:
        pass
    else:
        final()

    with nc.Block() as block:
        @block.tensor
        def _(e):
            P.replay("pe", e)

        @block.scalar
        def _(e):
            P.replay("act", e)

        @block.vector
        def _(e):
            P.replay("dve", e)

        @block.gpsimd
        def _(e):
            P.replay("pool", e)

        @block.sync
        def _(e):
            P.replay("sync", e)
            P.final_wait("sync", e)
    es.close()
    return nc


def _win_cols(r):
    cols = []
    H = [2 * r, 2 * r + 1]
    for base in (0, 256, 512):
        for hh in H:
            cols += [base + hh * 64 + d for d in range(64)]

    def dperm(d):
        return d + 8 if d < 8 else (d - 8 if d < 16 else d)
    for base in (768, 1280):
        for perm in (False, True):
            for hh in H:
                for c in range(2):
                    for d in range(64):
                        dd = dperm(d) if perm else d
                        cols.append(base + hh * 128 + c * 64 + dd)
    for hh in H:
        cols += [1792 + hh * 128 + e for e in range(128)]
    cols += list(range(2304, 2560))
    cols += list(range(2560, 2688))
    cols += list(range(2688, 2720))
    cols += [2688 + (j + 16 if j < 16 else j - 16) for j in range(32)]
    assert len(cols) == NWIN
    return np.array(cols)


def _wuq_cols(r):
    cols = []
    H = [2 * r, 2 * r + 1]
    for perm in (False, True):
        for hh in H:
            for j in range(96):
                jj = j
                if perm and j >= 64:
                    m = j - 64
                    jj = 64 + (m + 16 if m < 16 else m - 16)
                cols.append(hh * 96 + jj)
    return np.array(cols)


def _wukv_cols(r):
    H = [2 * r, 2 * r + 1]
    cols = []
    for hh in H:
        cols += [hh * 128 + j for j in range(64)]
    for hh in H:
        cols += [hh * 128 + 64 + j for j in range(64)]
    return np.array(cols)


def _wout_rows():
    rows = []
    for c in range(4):
        for rp in range(2):
            for i in range(128):
                if c == 0:
                    rows.append(128 * rp + i)
                elif c == 1:
                    rows.append(256 + (2 * rp) * 128 + i)
                elif c == 2:
                    rows.append(256 + (2 * rp + 1) * 128 + i)
                else:
                    rows.append(768 + 128 * rp + i)
    return np.array(rows)


def _const_tables():
    import ml_dtypes
    kp = np.arange(128)[:, None]
    qf = np.arange(512)[None, :]
    cm = np.zeros((128, 8 * 512 + 128), np.float32)
    for j in range(4):
        cm[:, j * 512:(j + 1) * 512] = (qf >= 128 * j + kp)
        cm[:, (4 + j) * 512:(5 + j) * 512] = (qf > 128 * j + kp)
    jj = np.arange(128)[:, None]
    ss = np.arange(128)[None, :]
    cm[:, 8 * 512:] = (jj >= ss)
    return cm.astype(ml_dtypes.bfloat16)


def _freq_cols():
    invf = np.zeros((128, 3), np.float64)
    sgn = np.zeros((128, 3), np.float64)
    for row in range(128):
        d = row % 64
        if d < 16:
            invf[row, 0] = THETA ** (-(d % 8) / 8.0)
            sgn[row, 0] = -1.0 if d < 8 else 1.0
        if 64 <= row < 96:
            m = row - 64
            invf[row, 1] = THETA ** (-(m % 16) / 16.0)
            sgn[row, 1] = -1.0 if m < 16 else 1.0
        if row < 32:
            invf[row, 2] = THETA ** (-(row % 16) / 16.0)
            sgn[row, 2] = -1.0 if row < 16 else 1.0
    return invf.astype(np.float32), sgn.astype(np.float32)


_NC_CACHE = {}


def make_in_maps(inp):
    f32 = np.float32
    g = {k: np.asarray(v) for k, v in inp.items()}

    def fm(v, nch):
        v = np.asarray(v, f32)
        L = v.shape[0]
        return v.reshape(L, nch, 128).transpose(2, 0, 1).reshape(128, L * nch)
    vec_common = np.zeros((128, NV), f32)
    vec_common[:, V_GF1:V_GF1 + 32] = fm(g["norm_ffn1"], 8)
    vec_common[:, V_GMIX:V_GMIX + 32] = fm(g["norm_mix"], 8)
    vec_common[:, V_GF2:V_GF2 + 32] = fm(g["norm_ffn2"], 8)
    vec_common[:, V_GPLE:V_GPLE + 32] = fm(g["norm_ple"], 8)
    vec_common[:, V_GFIN:V_GFIN + 8] = fm(g["norm_final"][None, :], 8)
    vec_common[:, V_QN:V_QN + 8] = fm(g["mla_q_norm"], 2)
    vec_common[:, V_KVN:V_KVN + 4] = fm(g["mla_kv_norm"], 1)
    vec_common[:, V_SUBLN:V_SUBLN + 4] = fm(g["diff_subln"], 1)
    invf, sgn = _freq_cols()
    vec_common[:, V_INVF:V_INVF + 3] = invf
    vec_common[:, V_SGN:V_SGN + 3] = sgn
    vec_common[:, V_NEGPI] = -math.pi
    for j, nm in enumerate(("diff_lambda_q1", "diff_lambda_k1", "diff_lambda_q2", "diff_lambda_k2")):
        vec_common[:, V_LAM + j * 256:V_LAM + (j + 1) * 256] = np.asarray(g[nm], f32).reshape(1, 256)
    cmask = _const_tables()
    wout_p = np.ascontiguousarray(np.asarray(g["w_out"], f32)[:, _wout_rows(), :])
    per_rank = []
    for r in range(2):
        per_rank.append(dict(
            win=np.ascontiguousarray(np.asarray(g["w_in"], f32)[:, :, _win_cols(r)]),
            wuq=np.ascontiguousarray(np.asarray(g["mla_w_uq"], f32)[:, :, _wuq_cols(r)]),
            wukv=np.ascontiguousarray(np.asarray(g["mla_w_ukv"], f32)[:, :, _wukv_cols(r)]),
        ))
    shared = dict(
        w1gu=np.ascontiguousarray(g["w_ffn1_gu"], dtype=f32), w1d=np.ascontiguousarray(g["w_ffn1_down"], dtype=f32),
        w2gu=np.ascontiguousarray(g["w_ffn2_gu"], dtype=f32), w2d=np.ascontiguousarray(g["w_ffn2_down"], dtype=f32),
        wout=wout_p, wgate=np.ascontiguousarray(g["w_ple_gate"], dtype=f32), wproj=np.ascontiguousarray(g["w_ple_proj"], dtype=f32),
        cmask=cmask,
    )
    x = np.asarray(g["x"], f32)
    p = np.asarray(g["p"], f32)
    pos = np.asarray(g["positions"]).astype(np.int32)
    maps = []
    for core in range(8):
        b, r = core // 2, core % 2
        sl = slice(r * TL, (r + 1) * TL)
        vec = vec_common.copy()
        vec[:, V_SEL + r] = 1.0
        m = dict(shared)
        m.update(per_rank[r])
        m["xT"] = np.ascontiguousarray(x[b, sl, :].T)
        m["pT"] = np.ascontiguousarray(p[:, b, sl, :].transpose(0, 2, 1))
        m["posr"] = np.ascontiguousarray(np.broadcast_to(pos[b][None, :], (128, S)))
        m["vecs"] = vec
        maps.append(m)
    return maps


def run(inp, depth=DEPTH, dbg=None, trace=False):
    key = (depth, dbg)
    if key not in _NC_CACHE:
        _NC_CACHE[key] = build_program(depth, dbg)
    nc = _NC_CACHE[key]
    maps = make_in_maps(inp)
    res = run_bass_kernel_spmd(nc, maps, core_ids=list(range(8)), trace=trace)
    out = np.zeros((4, S, D), np.float32)
    for core in range(8):
        b, r = core // 2, core % 2
        out[b, r * TL:(r + 1) * TL, :] = np.asarray(res.results[core]["outT"]).T
    return out, res


def kernel(**inputs):
    out, _ = run(inputs)
    return out
```

```python
import math
from contextlib import ExitStack
import numpy as np
import concourse.bass as bass
import concourse.mybir as mybir
from concourse.bass_utils import run_bass_kernel_spmd

F32 = mybir.dt.float32
BF16 = mybir.dt.bfloat16
I32 = mybir.dt.int32
ALU = mybir.AluOpType
AF = mybir.ActivationFunctionType
AX = mybir.AxisListType

D = 1024
KC = 8
S = 4096
TL = 2048
NT = 512
DFF = 2816
FC = 22
DEPTH = 4
EPS = 1e-6
THETA = 500000.0
NWIN = 2112
GROUPS = [[0, 1], [2, 3], [4, 5], [6, 7]]

V_GF1, V_GMIX, V_GF2, V_GPLE = 0, 32, 64, 96
V_GFIN = 128
V_QN = 136
V_KVN = 144
V_SUBLN = 148
V_SEL = 152
V_INVF = 154
V_SGN = 157
V_NEGPI = 160
V_LAM = 164
NV = V_LAM + 4 * 4 * 64


class T:
    __slots__ = ("ap", "w", "r", "name")

    def __init__(self, ap, name=""):
        self.ap = ap
        self.w = {}
        self.r = {}
        self.name = name


class Prog:
    ISSUERS = ("pe", "act", "dve", "pool", "sync")
    KSLOT = 8
    KQ = {"sync": 8, "pool": 4}

    def __init__(self, nc, es):
        self.nc = nc
        self.es = es
        self.ops = {e: [] for e in self.ISSUERS}
        self.seen = {e: {} for e in self.ISSUERS}
        self.cnt = {}
        self.sem = {}
        self.dma_i = {"sync": 0, "pool": 0}
        self.ncc = 0
        for e in ("pe", "act", "dve", "pool"):
            self.sem[e] = es.enter_context(nc.semaphore("s_" + e))
            self.cnt[e] = 0
        for q in ("sync", "pool"):
            for k in range(self.KSLOT):
                p = (q, k)
                self.sem[p] = es.enter_context(nc.semaphore("d_%s%d" % (q, k)))
                self.cnt[p] = 0

    def _waits(self, issuer, deps, skip_self_pe=True):
        out = []
        seen = self.seen[issuer]
        for p, c in deps.items():
            if c <= 0:
                continue
            if p == "pe" and issuer == "pe":
                continue
            if seen.get(p, 0) >= c:
                continue
            seen[p] = c
            mult = 1 if isinstance(p, str) else (16 if p[0] != "cc" else 1)
            out.append((self.sem[p], c * mult))
        return out

    def emit(self, issuer, fn, reads=(), writes=(), kind="c", inc=True):
        deps = {}

        def merge(d):
            for p, c in d.items():
                if deps.get(p, 0) < c:
                    deps[p] = c
        for t in reads:
            merge(t.w)
        for t in writes:
            merge(t.w)
            merge(t.r)
        if kind == "c":
            prod = issuer
            inc_default = 1
        elif kind == "dma":
            i = self.dma_i[issuer]
            self.dma_i[issuer] = i + 1
            prod = (issuer, i % self.KQ[issuer])
            if self.cnt[prod] > 0:
                merge({prod: self.cnt[prod]})
            inc_default = 16
        else:
            prod = ("cc", self.ncc)
            self.ncc += 1
            self.sem[prod] = self.es.enter_context(self.nc.semaphore("cc%d" % prod[1]))
            self.cnt[prod] = 0
            inc_default = 1
        waits = self._waits(issuer, deps)
        if kind == "c" and not inc:
            my = self.cnt[prod] + 1
            inc_amt = 0
        else:
            self.cnt[prod] += 1
            my = self.cnt[prod]
            inc_amt = inc_default
        for t in reads:
            if t.r.get(prod, 0) < my:
                t.r[prod] = my
        for t in writes:
            t.w = {prod: my}
            t.r = {}
        self.ops[issuer].append((waits, fn, self.sem[prod], inc_amt))

    def barrier(self):
        allp = {p: c for p, c in self.cnt.items() if c > 0}
        for issuer in self.ISSUERS:
            waits = self._waits(issuer, dict(allp))
            if issuer == "pe" and self.cnt["pe"] > 0:
                pass
            if waits:
                self.ops[issuer].append((waits, None, None, 0))

    def replay(self, issuer, eng):
        for waits, fn, sem, inc in self.ops[issuer]:
            for s, v in waits:
                eng.wait_ge(s, v)
            if fn is not None:
                if inc:
                    fn(eng).then_inc(sem, inc)
                else:
                    fn(eng)

    def final_wait(self, issuer, eng):
        for p, c in self.cnt.items():
            if c > 0:
                mult = 1 if isinstance(p, str) else (16 if p[0] != "cc" else 1)
                eng.wait_ge(self.sem[p], c * mult)


class Arena:
    def __init__(self, ap, n):
        self.ap = ap
        self.n = n
        self.off = 0

    def alloc(self, ncols):
        assert self.off + ncols <= self.n, ("arena overflow", self.off, ncols, self.n)
        a = self.ap[:, self.off:self.off + ncols]
        self.off += ncols
        return a

    def mark(self):
        return self.off

    def reset(self, m):
        self.off = m


def build_program(depth=DEPTH, dbg=None):
    nc = bass.Bass("TRN2", target_bir_lowering=False)
    es = ExitStack()

    def din(name, shape, dt=F32):
        return nc.dram_tensor(name, list(shape), dt, kind="ExternalInput").ap()

    xT = din("xT", [D, TL])
    pT = din("pT", [DEPTH, 256, TL])
    posr = din("posr", [128, S], I32)
    vecs_d = din("vecs", [128, NV])
    cmask_d = din("cmask", [128, 8 * 512 + 128], BF16)
    w1gu = din("w1gu", [DEPTH, D, 2 * DFF])
    w1d = din("w1d", [DEPTH, DFF, D])
    w2gu = din("w2gu", [DEPTH, D, 2 * DFF])
    w2d = din("w2d", [DEPTH, DFF, D])
    win = din("win", [DEPTH, D, NWIN])
    wuq = din("wuq", [DEPTH, 256, 384])
    wukv = din("wukv", [DEPTH, 128, 256])
    wout = din("wout", [DEPTH, D, D])
    wgate = din("wgate", [DEPTH, D, D])
    wproj = din("wproj", [DEPTH, 256, D])
    outT = nc.dram_tensor("outT", [D, TL], F32, kind="ExternalOutput").ap()

    tabs = [nc.dram_tensor("tab%d" % i, [128, S], F32) for i in range(6)]
    u_loc = [[nc.dram_tensor("uloc%d_%d" % (L, c), [256, TL], BF16) for c in range(4)] for L in range(depth)]
    u_g = [[nc.dram_tensor("ug%d_%d" % (L, c), [512, TL], BF16) for c in range(4)] for L in range(depth)]
    mx_loc = [[nc.dram_tensor("mxl%d_%d" % (L, c), [128, S], BF16) for c in range(4)] for L in range(depth)]
    mx_g = [[nc.dram_tensor("mxg%d_%d" % (L, c), [256, S], BF16) for c in range(4)] for L in range(depth)]

    NBF = 50176
    NF = 8448
    hT_t = es.enter_context(nc.sbuf_tensor("hT", [128, KC * TL], F32))
    abf_t = es.enter_context(nc.sbuf_tensor("abf", [128, NBF], BF16))
    af_t = es.enter_context(nc.sbuf_tensor("af32", [128, NF], F32))
    P = Prog(nc, es)
    abf = Arena(abf_t[:, :], NBF)
    af = Arena(af_t[:, :], NF)
    psum = [T(es.enter_context(nc.psum_tensor("ps%d" % i, [128, 512], F32))[:, :], "ps%d" % i) for i in range(8)]

    hT = hT_t[:, :].rearrange("p (k t) -> p k t", k=KC)
    h = [[T(hT[:, kc, t * NT:(t + 1) * NT], "h%d_%d" % (kc, t)) for t in range(4)] for kc in range(KC)]

    vecs = T(af.alloc(NV), "vecs")
    lamv = T(af.alloc(8), "lamv")
    cm = T(abf.alloc(8 * 512 + 128), "cmask")
    ones = T(abf.alloc(128), "ones")
    P.emit("sync", lambda e: e.dma_start(out=vecs.ap, in_=vecs_d[:, :]), writes=[vecs], kind="dma")
    P.emit("pool", lambda e: e.dma_start(out=cm.ap, in_=cmask_d[:, :]), writes=[cm], kind="dma")
    P.emit("dve", lambda e: e.memset(ones.ap, 1.0), writes=[ones])
    maskI = [cm.ap[:, j * 512:(j + 1) * 512] for j in range(4)]
    maskS = [cm.ap[:, (4 + j) * 512:(5 + j) * 512] for j in range(4)]
    trim = cm.ap[:, 8 * 512:8 * 512 + 128]
    for kc in range(KC):
        P.emit("sync", (lambda kc: lambda e: e.dma_start(out=hT[:, kc, :], in_=xT[kc * 128:(kc + 1) * 128, :]))(kc),
               writes=h[kc], kind="dma")
    pers_bf = abf.mark()
    pers_f = af.mark()

    def vcol(c, n=1):
        return vecs.ap[:, c:c + n]

    def setup():
        HS = 1024
        ki_t = es.enter_context(nc.sbuf_tensor("ki", [128, HS], I32))
        ki = T(ki_t[:, :])
        posf = T(af.alloc(HS))
        ang = T(af.alloc(HS))
        tq = T(af.alloc(HS))
        yy = T(af.alloc(HS))
        sv = T(af.alloc(HS))
        TWO_PI = 2 * math.pi
        for part in range(S // HS):
            c0 = part * HS
            P.emit("pool", (lambda c0: lambda e: e.dma_start(out=posf.ap, in_=posr[:, c0:c0 + HS]))(c0), writes=[posf], kind="dma")
            for s in range(3):
                P.emit("dve", (lambda s: lambda e: e.tensor_scalar(out=ang.ap, in0=posf.ap, scalar1=vcol(V_INVF + s), scalar2=None,
                                                                    op0=ALU.mult))(s), reads=[posf, vecs], writes=[ang])
                for which, phase in ((1, 0.0), (0, 0.5 * math.pi)):
                    P.emit("dve", (lambda phase: lambda e: e.tensor_scalar(out=tq.ap, in0=ang.ap, scalar1=phase, scalar2=1.0 / TWO_PI,
                                                                            op0=ALU.add, op1=ALU.mult))(phase), reads=[ang], writes=[tq])
                    P.emit("dve", lambda e: e.tensor_copy(out=ki.ap, in_=tq.ap), reads=[tq], writes=[ki])
                    P.emit("dve", lambda e: e.tensor_copy(out=tq.ap, in_=ki.ap), reads=[ki], writes=[tq])
                    P.emit("dve", (lambda phase: lambda e: e.tensor_scalar(out=yy.ap, in0=ang.ap, scalar1=phase, scalar2=None, op0=ALU.add))(phase),
                           reads=[ang], writes=[yy])
                    P.emit("dve", lambda e: e.scalar_tensor_tensor(out=yy.ap, in0=tq.ap, scalar=-TWO_PI, in1=yy.ap, op0=ALU.mult, op1=ALU.add),
                           reads=[tq, yy], writes=[yy])
                    P.emit("dve", lambda e: e.tensor_scalar(out=yy.ap, in0=yy.ap, scalar1=-3.141592, scalar2=3.141592, op0=ALU.max, op1=ALU.min),
                           reads=[yy], writes=[yy])
                    P.emit("act", lambda e: e.activation(out=sv.ap, in_=yy.ap, func=AF.Sin), reads=[yy], writes=[sv])
                    if which == 1:
                        P.emit("dve", (lambda s: lambda e: e.tensor_scalar(out=sv.ap, in0=sv.ap, scalar1=vcol(V_SGN + s), scalar2=None,
                                                                            op0=ALU.mult))(s), reads=[sv, vecs], writes=[sv])
                    tt = T(None)
                    P.emit("sync", (lambda s, which, c0: lambda e: e.dma_start(out=tabs[2 * s + which][:, c0:c0 + HS], in_=sv.ap))(s, which, c0),
                           reads=[sv], writes=[tt], kind="dma")
        pr = T(af.alloc(64))
        d12 = T(af.alloc(8))
        for L in range(depth):
            for j in range(2):
                a = V_LAM + (2 * j) * 256 + L * 64
                b = V_LAM + (2 * j + 1) * 256 + L * 64
                P.emit("dve", (lambda a, b: lambda e: e.tensor_tensor(out=pr.ap, in0=vcol(a, 64), in1=vcol(b, 64), op=ALU.mult))(a, b),
                       reads=[vecs], writes=[pr])
                P.emit("dve", (lambda j: lambda e: e.reduce_sum(out=d12.ap[:, j:j + 1], in_=pr.ap, axis=AX.X))(j), reads=[pr], writes=[d12])
            P.emit("act", lambda e: e.activation(out=d12.ap[:, 2:4], in_=d12.ap[:, 0:2], func=AF.Exp), reads=[d12], writes=[d12])
            lam_init = 0.8 - 0.6 * math.exp(-0.3 * L)
            P.emit("dve", (lambda L, li: lambda e: e.scalar_tensor_tensor(out=lamv.ap[:, L:L + 1], in0=d12.ap[:, 3:4], scalar=-li,
                                                                            in1=d12.ap[:, 2:3], op0=ALU.add, op1=ALU.subtract))(L, lam_init),
                   reads=[d12], writes=[lamv])
        P.barrier()
        af.reset(pers_f)

    def rmsnorm(src_chunks, nch, dim, gcol, out_tiles, sq, ps_ss, rstd, src_aps=None):
        for c in range(nch):
            P.emit("act", (lambda c: lambda e: e.activation(out=sq[c].ap, in_=src_chunks[c].ap, func=AF.Square))(c),
                   reads=[src_chunks[c]], writes=[sq[c]])
        for c in range(nch):
            P.emit("pe", (lambda c: lambda e: e.matmul(ps_ss.ap, lhsT=ones.ap, rhs=sq[c].ap, start=(c == 0), stop=(c == nch - 1)))(c),
                   reads=[ones, sq[c]], writes=[ps_ss], inc=(c == nch - 1))
        P.emit("act", lambda e: e.activation(out=rstd.ap, in_=ps_ss.ap, func=AF.Sqrt, bias=EPS, scale=1.0 / dim),
               reads=[ps_ss], writes=[rstd])
        P.emit("dve", lambda e: e.reciprocal(out=rstd.ap, in_=rstd.ap), reads=[rstd], writes=[rstd])
        for c in range(nch):
            P.emit("dve", (lambda c: lambda e: e.scalar_tensor_tensor(out=out_tiles[c].ap, in0=src_chunks[c].ap, scalar=vcol(gcol + c),
                                                                       in1=rstd.ap, op0=ALU.mult, op1=ALU.mult))(c),
                   reads=[src_chunks[c], rstd, vecs], writes=[out_tiles[c]])

    def wview(w, L, p=128):
        return w[L].rearrange("(k p) c -> p k c", p=p)

    def ffn(L, wgu, wd, gcol):
        mb, mf = abf.mark(), af.mark()
        u = [T(abf.alloc(NT)) for _ in range(KC)]
        sq = [T(abf.alloc(NT)) for _ in range(KC)]
        act = [T(abf.alloc(NT)) for _ in range(FC)]
        NG = 3
        gbuf = [T(abf.alloc(KC * 256)) for _ in range(NG)]
        vbuf = [T(abf.alloc(KC * 256)) for _ in range(NG)]
        dbuf = [T(abf.alloc(FC * 256)) for _ in range(2)]
        rstd = T(af.alloc(NT))
        sg = [T(af.alloc(NT)) for _ in range(2)]
        wg_v = wview(wgu, L)
        wd_v = wview(wd, L)
        gseq = [(t, j) for t in range(4) for j in range(11)]
        dseq = [(t, m) for t in range(4) for m in range(4)]
        gl = [0]
        dl = [0]

        def load_g(upto):
            while gl[0] <= upto and gl[0] < len(gseq):
                k = gl[0]
                _, j = gseq[k]
                b = k % NG
                gv = gbuf[b].ap.rearrange("p (k c) -> p k c", k=KC)
                vv = vbuf[b].ap.rearrange("p (k c) -> p k c", k=KC)
                P.emit("pool", (lambda gv, j: lambda e: e.dma_start(out=gv, in_=wg_v[:, :, j * 256:(j + 1) * 256]))(gv, j),
                       writes=[gbuf[b]], kind="dma")
                P.emit("pool", (lambda vv, j: lambda e: e.dma_start(out=vv, in_=wg_v[:, :, DFF + j * 256:DFF + (j + 1) * 256]))(vv, j),
                       writes=[vbuf[b]], kind="dma")
                gl[0] += 1

        def load_d(upto):
            while dl[0] <= upto and dl[0] < len(dseq):
                k = dl[0]
                _, m = dseq[k]
                b = k % 2
                dv = dbuf[b].ap.rearrange("p (f c) -> p f c", f=FC)
                P.emit("pool", (lambda dv, m: lambda e: e.dma_start(out=dv, in_=wd_v[:, :, m * 256:(m + 1) * 256]))(dv, m),
                       writes=[dbuf[b]], kind="dma")
                dl[0] += 1

        load_g(1)
        load_d(0)
        gk = 0
        dk = 0
        for t in range(4):
            ps_ss = psum[0]
            rmsnorm([h[kc][t] for kc in range(KC)], KC, D, gcol, u, sq, ps_ss, rstd)
            for j in range(11):
                load_g(gk + 2)
                b = gk % NG
                gv = gbuf[b].ap.rearrange("p (k c) -> p k c", k=KC)
                vv = vbuf[b].ap.rearrange("p (k c) -> p k c", k=KC)
                for jj in range(2):
                    fc = 2 * j + jj
                    pg = psum[1 + (fc % 2) * 2]
                    pv = psum[2 + (fc % 2) * 2]
                    for kc in range(KC):
                        P.emit("pe", (lambda kc, jj, gv, pg: lambda e: e.matmul(pg.ap, lhsT=gv[:, kc, jj * 128:(jj + 1) * 128], rhs=u[kc].ap,
                                                                                start=(kc == 0), stop=(kc == KC - 1)))(kc, jj, gv, pg),
                               reads=[gbuf[b], u[kc]], writes=[pg], inc=(kc == KC - 1))
                    for kc in range(KC):
                        P.emit("pe", (lambda kc, jj, vv, pv: lambda e: e.matmul(pv.ap, lhsT=vv[:, kc, jj * 128:(jj + 1) * 128], rhs=u[kc].ap,
                                                                                start=(kc == 0), stop=(kc == KC - 1)))(kc, jj, vv, pv),
                               reads=[vbuf[b], u[kc]], writes=[pv], inc=(kc == KC - 1))
                    sgt = sg[fc % 2]
                    P.emit("act", (lambda pg, sgt: lambda e: e.activation(out=sgt.ap, in_=pg.ap, func=AF.Silu))(pg, sgt),
                           reads=[pg], writes=[sgt])
                    P.emit("dve", (lambda pv, sgt, fc: lambda e: e.tensor_tensor(out=act[fc].ap, in0=sgt.ap, in1=pv.ap, op=ALU.mult))(pv, sgt, fc),
                           reads=[pv, sgt], writes=[act[fc]])
                gk += 1
            for m in range(4):
                load_d(dk + 1)
                b = dk % 2
                dv = dbuf[b].ap.rearrange("p (f c) -> p f c", f=FC)
                for jj in range(2):
                    dc = 2 * m + jj
                    po = psum[5 + (dc % 2)]
                    for fc in range(FC):
                        P.emit("pe", (lambda fc, jj, dv, po: lambda e: e.matmul(po.ap, lhsT=dv[:, fc, jj * 128:(jj + 1) * 128], rhs=act[fc].ap,
                                                                                start=(fc == 0), stop=(fc == FC - 1)))(fc, jj, dv, po),
                               reads=[dbuf[b], act[fc]], writes=[po], inc=(fc == FC - 1))
                    ht = h[dc][t]
                    P.emit("dve", (lambda po, ht: lambda e: e.scalar_tensor_tensor(out=ht.ap, in0=po.ap, scalar=0.5, in1=ht.ap,
                                                                                    op0=ALU.mult, op1=ALU.add))(po, ht),
                           reads=[po, ht], writes=[ht])
                dk += 1
        P.barrier()
        abf.reset(mb)
        af.reset(mf)

    def ple(L):
        mb, mf = abf.mark(), af.mark()
        u = [T(abf.alloc(NT)) for _ in range(KC)]
        sq = [T(abf.alloc(NT)) for _ in range(KC)]
        wg = T(abf.alloc(KC * D))
        wp = T(abf.alloc(2 * D))
        pt = [T(abf.alloc(2 * NT)) for _ in range(2)]
        rstd = T(af.alloc(NT))
        sgm = [T(af.alloc(NT)) for _ in range(2)]
        wgv = wg.ap.rearrange("p (k c) -> p k c", k=KC)
        wpv = wp.ap.rearrange("p (k c) -> p k c", k=2)
        for half in range(2):
            P.emit("pool", (lambda half: lambda e: e.dma_start(out=wgv[:, half * 4:(half + 1) * 4, :],
                                                              in_=wview(wgate, L)[:, half * 4:(half + 1) * 4, :]))(half),
                   writes=[wg], kind="dma")
        P.emit("pool", lambda e: e.dma_start(out=wpv, in_=wview(wproj, L)), writes=[wp], kind="dma")
        for t in range(4):
            ptv = pt[t % 2].ap.rearrange("p (k c) -> p k c", k=2)
            P.emit("pool", (lambda ptv, t: lambda e: e.dma_start(out=ptv, in_=pT[L].rearrange("(k p) t -> p k t", p=128)[:, :, t * NT:(t + 1) * NT]))(ptv, t),
                   writes=[pt[t % 2]], kind="dma")
            rmsnorm([h[kc][t] for kc in range(KC)], KC, D, V_GPLE + L * 8, u, sq, psum[0], rstd)
            for dc in range(KC):
                pg = psum[1 + (dc % 2) * 2]
                pp = psum[2 + (dc % 2) * 2]
                for kc in range(KC):
                    P.emit("pe", (lambda kc, dc, pg: lambda e: e.matmul(pg.ap, lhsT=wgv[:, kc, dc * 128:(dc + 1) * 128], rhs=u[kc].ap,
                                                                        start=(kc == 0), stop=(kc == KC - 1)))(kc, dc, pg),
                           reads=[wg, u[kc]], writes=[pg], inc=(kc == KC - 1))
                for k2 in range(2):
                    P.emit("pe", (lambda k2, dc, pp, ptv: lambda e: e.matmul(pp.ap, lhsT=wpv[:, k2, dc * 128:(dc + 1) * 128], rhs=ptv[:, k2, :],
                                                                             start=(k2 == 0), stop=(k2 == 1)))(k2, dc, pp, ptv),
                           reads=[wp, pt[t % 2]], writes=[pp], inc=(k2 == 1))
                s_ = sgm[dc % 2]
                P.emit("act", (lambda pg, s_: lambda e: e.activation(out=s_.ap, in_=pg.ap, func=AF.Sigmoid))(pg, s_), reads=[pg], writes=[s_])
                P.emit("dve", (lambda pp, s_: lambda e: e.tensor_tensor(out=s_.ap, in0=s_.ap, in1=pp.ap, op=ALU.mult))(pp, s_),
                       reads=[pp, s_], writes=[s_])
                ht = h[dc][t]
                P.emit("dve", (lambda s_, ht: lambda e: e.tensor_tensor(out=ht.ap, in0=ht.ap, in1=s_.ap, op=ALU.add))(s_, ht),
                       reads=[s_, ht], writes=[ht])
        P.barrier()
        abf.reset(mb)
        af.reset(mf)

    def load_u(L, T8, ut, ug_t):
        rank, lt = T8 // 4, T8 % 4
        for c in range(4):
            for i in range(2):
                kc = 2 * c + i
                P.emit("sync", (lambda kc, c, i: lambda e: e.dma_start(
                    out=ut[kc].ap, in_=u_g[L][c][rank * 256 + i * 128: rank * 256 + (i + 1) * 128, lt * NT:(lt + 1) * NT]))(kc, c, i),
                    reads=[ug_t[c]], writes=[ut[kc]], kind="dma")

    def load_tab(idx, T8, dst):
        P.emit("sync", lambda e: e.dma_start(out=dst.ap, in_=tabs[idx][:, T8 * NT:(T8 + 1) * NT]), writes=[dst], kind="dma")

    def proj_fm(ps, wv, c0, ncol, ut, wt):
        for kc in range(KC):
            P.emit("pe", (lambda kc: lambda e: e.matmul(ps.ap[0:ncol, :], lhsT=wv[:, kc, c0:c0 + ncol], rhs=ut[kc].ap,
                                                        start=(kc == 0), stop=(kc == KC - 1)))(kc),
                   reads=[wt, ut[kc]], writes=[ps], inc=(kc == KC - 1))

    def proj_tm(ps, wv, c0, ncol, ut, wt, tb):
        for kc in range(KC):
            P.emit("pe", (lambda kc: lambda e: e.matmul(ps.ap[:, 0:ncol], lhsT=ut[kc].ap[:, tb * 128:(tb + 1) * 128], rhs=wv[:, kc, c0:c0 + ncol],
                                                        start=(kc == 0), stop=(kc == KC - 1)))(kc),
                   reads=[wt, ut[kc]], writes=[ps], inc=(kc == KC - 1))

    def rope_evac(px, pxp, cosT, sinT, nrow, out_ap, out_t, t1, t2, scale):
        P.emit("dve", lambda e: e.scalar_tensor_tensor(out=t1.ap[0:nrow, :], in0=px.ap[0:nrow, :], scalar=scale, in1=cosT.ap[0:nrow, :],
                                                       op0=ALU.mult, op1=ALU.mult),
               reads=[px, cosT], writes=[t1])
        P.emit("dve", lambda e: e.scalar_tensor_tensor(out=t2.ap[0:nrow, :], in0=pxp.ap[0:nrow, :], scalar=scale, in1=sinT.ap[0:nrow, :],
                                                       op0=ALU.mult, op1=ALU.mult),
               reads=[pxp, sinT], writes=[t2])
        P.emit("dve", lambda e: e.tensor_tensor(out=out_ap, in0=t1.ap[0:nrow, :], in1=t2.ap[0:nrow, :], op=ALU.add),
               reads=[t1, t2], writes=[out_t])

    def softmax_attn(qT_ap, kT_ap, qk_t, v_t, vaug_fn, Q8, pO, pD, sbank, pts, den_ones):
        nkb = 4 * Q8 + 4
        q_ap = qT_ap[:, Q8 * NT:(Q8 + 1) * NT]

        NB = len(sbank)
        LA = NB - 1

        def s_mm(i):
            ps = sbank[i % NB]
            P.emit("pe", lambda e: e.matmul(ps.ap, lhsT=kT_ap[:, i * 128:(i + 1) * 128], rhs=q_ap, start=True, stop=True),
                   reads=[qk_t], writes=[ps])
        def o_mm(i):
            pt_ = pts[i % len(pts)]
            va = vaug_fn(i)
            P.emit("pe", lambda e: e.matmul(pO.ap, lhsT=va, rhs=pt_.ap, start=(i == 0), stop=(i == nkb - 1)), reads=[v_t, pt_], writes=[pO])
            if den_ones:
                P.emit("pe", lambda e: e.matmul(pD.ap, lhsT=ones.ap, rhs=pt_.ap, start=(i == 0), stop=(i == nkb - 1)), reads=[ones, pt_], writes=[pD])
        for i0 in range(min(LA, nkb)):
            s_mm(i0)
        for i in range(nkb):
            if i + LA < nkb:
                s_mm(i + LA)
            ps = sbank[i % NB]
            pt_ = pts[i % len(pts)]
            P.emit("act", (lambda ps, pt_: lambda e: e.activation(out=pt_.ap, in_=ps.ap, func=AF.Exp))(ps, pt_), reads=[ps], writes=[pt_])
            jd = i - 4 * Q8
            if jd >= 0:
                P.emit("dve", (lambda pt_, jd: lambda e: e.tensor_tensor(out=pt_.ap, in0=pt_.ap, in1=maskI[jd], op=ALU.mult))(pt_, jd),
                       reads=[pt_, cm], writes=[pt_])
            if i >= 1:
                o_mm(i - 1)
        o_mm(nkb - 1)

    def mix_gather(L, ug_t):
        mb0, mf0 = abf.mark(), af.mark()
        u = [T(abf.alloc(NT)) for _ in range(KC)]
        sq = [T(abf.alloc(NT)) for _ in range(KC)]
        rstd = T(af.alloc(NT))
        uloc_t = [T(None) for c in range(4)]
        for t in range(4):
            rmsnorm([h[kc][t] for kc in range(KC)], KC, D, V_GMIX + L * 8, u, sq, psum[0], rstd)
            for kc in range(KC):
                c, i = kc // 2, kc % 2
                P.emit("sync", (lambda kc, c, i, t: lambda e: e.dma_start(out=u_loc[L][c][i * 128:(i + 1) * 128, t * NT:(t + 1) * NT], in_=u[kc].ap))(kc, c, i, t),
                       reads=[u[kc]], writes=[uloc_t[c]], kind="dma")
        for c in range(4):
            P.emit("pool", (lambda c: lambda e: e.collective_compute("AllGather", ALU.bypass, replica_groups=GROUPS,
                                                                    ins=[u_loc[L][c].ap().opt()], outs=[u_g[L][c].ap().opt()]))(c),
                   reads=[uloc_t[c]], writes=[ug_t[c]], kind="cc")
        P.barrier()
        abf.reset(mb0)
        af.reset(mf0)

    def mix_sb(L, ug_t, mxl_t):
        mb, mf = abf.mark(), af.mark()
        win_v = wview(win, L)
        wt = T(abf.alloc(KC * 384))
        wv = wt.ap.rearrange("p (k c) -> p k c", k=KC)
        P.emit("pool", lambda e: e.dma_start(out=wv, in_=win_v[:, :, 0:384]), writes=[wt], kind="dma")
        qk_t = T(None, "qk")
        v_t = T(None, "v")
        qT = abf.alloc(S)
        kT = abf.alloc(S)
        vv_ = abf.alloc(32 * 128).rearrange("p (b c) -> p b c", b=32)
        uts = [[T(abf.alloc(NT)) for _ in range(KC)] for _ in range(2)]
        for T8 in range(8):
            ut = uts[T8 % 2]
            load_u(L, T8, ut, ug_t)
            pq, pk = psum[(T8 % 2) * 2], psum[(T8 % 2) * 2 + 1]
            proj_fm(pq, wv, 0, 128, ut, wt)
            proj_fm(pk, wv, 128, 128, ut, wt)
            P.emit("act", (lambda pq, T8: lambda e: e.activation(out=qT[:, T8 * NT:(T8 + 1) * NT], in_=pq.ap, func=AF.Copy, scale=0.125))(pq, T8),
                   reads=[pq], writes=[qk_t])
            P.emit("dve", (lambda pk, T8: lambda e: e.tensor_copy(out=kT[:, T8 * NT:(T8 + 1) * NT], in_=pk.ap))(pk, T8), reads=[pk], writes=[qk_t])
            for tb in range(4):
                pvv = psum[4 + tb % 2]
                proj_tm(pvv, wv, 256, 128, ut, wt, tb)
                P.emit("act", (lambda pvv, T8, tb: lambda e: e.activation(out=vv_[:, T8 * 4 + tb, :], in_=pvv.ap[:, 0:128], func=AF.Copy))(pvv, T8, tb),
                       reads=[pvv], writes=[v_t])
        ebuf = [T(af.alloc(NT)) for _ in range(3)]
        t1b = [T(af.alloc(NT)) for _ in range(2)]
        Rt = T(af.alloc(NT))
        spb = [T(abf.alloc(NT)) for _ in range(3)]
        Ab = [T(abf.alloc(NT)) for _ in range(2)]
        ob = [T(abf.alloc(NT)) for _ in range(2)]
        zb = [psum[0], psum[1], psum[7]]
        cb = [psum[2], psum[3]]
        csb = [psum[4], psum[5]]
        pO = psum[6]

        def sb_tile(hh, Q8):
            r0 = hh * 64
            nkb = 4 * Q8 + 4
            order = list(range(nkb - 1, -1, -1))
            q_ap = qT[r0:r0 + 64, Q8 * NT:(Q8 + 1) * NT]

            def st1(n):
                i = order[n]
                pz = zb[n % 3]
                P.emit("pe", lambda e: e.matmul(pz.ap, lhsT=kT[r0:r0 + 64, i * 128:(i + 1) * 128], rhs=q_ap, start=True, stop=True),
                       reads=[qk_t], writes=[pz])
                eb, sp = ebuf[n % 3], spb[n % 3]
                P.emit("act", lambda e: e.activation(out=eb.ap, in_=pz.ap, func=AF.Exp), reads=[pz], writes=[eb])
                P.emit("act", lambda e: e.activation(out=sp.ap, in_=eb.ap, func=AF.Ln, bias=1.0, scale=1.0), reads=[eb], writes=[sp])
                jd = i - 4 * Q8
                if jd >= 0:
                    P.emit("dve", lambda e: e.tensor_tensor(out=sp.ap, in0=sp.ap, in1=maskS[jd], op=ALU.mult), reads=[sp, cm], writes=[sp])

            def st2(n):
                i = order[n]
                pz, sp = zb[n % 3], spb[n % 3]
                pc, pcs = cb[n % 2], csb[n % 2]
                P.emit("pe", lambda e: e.matmul(pc.ap, lhsT=trim, rhs=sp.ap, start=True, stop=True), reads=[cm, sp], writes=[pc])
                if n < nkb - 1:
                    P.emit("pe", lambda e: e.matmul(pcs.ap, lhsT=ones.ap, rhs=sp.ap, start=True, stop=True), reads=[ones, sp], writes=[pcs])
                t1 = t1b[n % 2]
                if n == 0:
                    P.emit("dve", lambda e: e.tensor_copy(out=t1.ap, in_=pz.ap), reads=[pz], writes=[t1])
                else:
                    P.emit("dve", lambda e: e.tensor_tensor(out=t1.ap, in0=pz.ap, in1=Rt.ap, op=ALU.subtract), reads=[pz, Rt], writes=[t1])
                P.emit("dve", lambda e: e.tensor_tensor(out=t1.ap, in0=t1.ap, in1=pc.ap, op=ALU.subtract), reads=[t1, pc], writes=[t1])
                A = Ab[n % 2]
                P.emit("act", lambda e: e.activation(out=A.ap, in_=t1.ap, func=AF.Exp), reads=[t1], writes=[A])
                jd = i - 4 * Q8
                if jd >= 0:
                    P.emit("dve", lambda e: e.tensor_tensor(out=A.ap, in0=A.ap, in1=maskS[jd], op=ALU.mult), reads=[A, cm], writes=[A])
                if n < nkb - 1:
                    if n == 0:
                        P.emit("dve", lambda e: e.tensor_copy(out=Rt.ap, in_=pcs.ap), reads=[pcs], writes=[Rt])
                    else:
                        P.emit("dve", lambda e: e.tensor_tensor(out=Rt.ap, in0=Rt.ap, in1=pcs.ap, op=ALU.add), reads=[pcs, Rt], writes=[Rt])

            def st3(n):
                i = order[n]
                A = Ab[n % 2]
                P.emit("pe", lambda e: e.matmul(pO.ap[0:64, :], lhsT=vv_[:, i, r0:r0 + 64], rhs=A.ap, start=(n == 0), stop=(n == nkb - 1)),
                       reads=[v_t, A], writes=[pO])
            st1(0)
            st1(1)
            for n in range(nkb):
                if n + 2 < nkb:
                    st1(n + 2)
                st2(n)
                if n >= 1:
                    st3(n - 1)
            st3(nkb - 1)
            o_ = ob[Q8 % 2]
            P.emit("act", lambda e: e.activation(out=o_.ap[0:64, :], in_=pO.ap[0:64, :], func=AF.Copy), reads=[pO], writes=[o_])
            P.emit("sync", lambda e: e.dma_start(out=mx_loc[L][0][r0:r0 + 64, Q8 * NT:(Q8 + 1) * NT], in_=o_.ap[0:64, :]),
                   reads=[o_], writes=[mxl_t[0]], kind="dma")
        for hh in range(2):
            for Q8 in range(8):
                sb_tile(hh, Q8)
        P.barrier()
        abf.reset(mb)
        af.reset(mf)

    def mix_diff(L, ug_t, mxl_t):
        mb, mf = abf.mark(), af.mark()
        win_v = wview(win, L)
        wt = T(abf.alloc(KC * 1280))
        wv = wt.ap.rearrange("p (k c) -> p k c", k=KC)
        for part in range(5):
            P.emit("pool", (lambda part: lambda e: e.dma_start(out=wv[:, :, part * 256:(part + 1) * 256],
                                                              in_=win_v[:, :, 384 + part * 256:384 + (part + 1) * 256]))(part),
                   writes=[wt], kind="dma")
        qk_t = T(None, "qk")
        v_t = T(None, "v")
        qT2 = abf.alloc(2 * S).rearrange("p (h t) -> p h t", h=2)
        kT2 = abf.alloc(2 * S).rearrange("p (h t) -> p h t", h=2)
        vd = abf.alloc(32 * 256).rearrange("p (b c) -> p b c", b=32)
        uts = [[T(abf.alloc(NT)) for _ in range(KC)] for _ in range(2)]
        cosT = [T(af.alloc(NT)) for _ in range(2)]
        sinT = [T(af.alloc(NT)) for _ in range(2)]
        t1 = T(af.alloc(NT))
        t2 = T(af.alloc(NT))
        for T8 in range(8):
            ut = uts[T8 % 2]
            load_u(L, T8, ut, ug_t)
            load_tab(0, T8, cosT[T8 % 2])
            load_tab(1, T8, sinT[T8 % 2])
            n = 0
            for which, dstT, cbase in ((0, qT2, 0), (1, kT2, 512)):
                for hh in range(2):
                    px, pxp = psum[(n % 2) * 2], psum[(n % 2) * 2 + 1]
                    n += 1
                    proj_fm(px, wv, cbase + hh * 128, 128, ut, wt)
                    proj_fm(pxp, wv, cbase + 256 + hh * 128, 128, ut, wt)
                    rope_evac(px, pxp, cosT[T8 % 2], sinT[T8 % 2], 128, dstT[:, hh, T8 * NT:(T8 + 1) * NT], qk_t, t1, t2,
                              0.125 if which == 0 else 1.0)
            for tb in range(4):
                pvv = psum[4 + tb % 2]
                proj_tm(pvv, wv, 1024, 256, ut, wt, tb)
                P.emit("act", (lambda pvv, T8, tb: lambda e: e.activation(out=vd[:, T8 * 4 + tb, :], in_=pvv.ap[:, 0:256], func=AF.Copy))(pvv, T8, tb),
                       reads=[pvv], writes=[v_t])
        P.barrier()
        af.reset(mf)
        pts = [T(abf.alloc(NT)) for _ in range(3)]
        ob = [T(abf.alloc(NT)) for _ in range(1)]
        sqd = T(abf.alloc(NT))
        rec = T(af.alloc(NT))
        o1 = T(af.alloc(NT))
        o2 = T(af.alloc(NT))
        rstd = T(af.alloc(NT))
        lam_init = 0.8 - 0.6 * math.exp(-0.3 * L)

        def diff_tile(hh, Q8):
            for comp in range(2):
                r0 = comp * 64
                pO, pD = psum[2 + comp * 2], psum[3 + comp * 2]
                softmax_attn(qT2[r0:r0 + 64, hh, :], kT2[r0:r0 + 64, hh, :], qk_t, v_t,
                             (lambda i: vd[:, i, hh * 128:(hh + 1) * 128]), Q8, pO, pD, [psum[0], psum[1], psum[7]], pts, True)
                oc = o1 if comp == 0 else o2
                P.emit("dve", (lambda pD: lambda e: e.reciprocal(out=rec.ap, in_=pD.ap))(pD), reads=[pD], writes=[rec])
                P.emit("dve", (lambda pO, oc: lambda e: e.tensor_tensor(out=oc.ap, in0=pO.ap, in1=rec.ap, op=ALU.mult))(pO, oc),
                       reads=[pO, rec], writes=[oc])
            P.emit("dve", lambda e: e.scalar_tensor_tensor(out=o1.ap, in0=o2.ap, scalar=lamv.ap[:, L:L + 1], in1=o1.ap, op0=ALU.mult, op1=ALU.add),
                   reads=[o1, o2, lamv], writes=[o1])
            P.emit("act", lambda e: e.activation(out=sqd.ap, in_=o1.ap, func=AF.Square), reads=[o1], writes=[sqd])
            pss = psum[6]
            P.emit("pe", lambda e: e.matmul(pss.ap, lhsT=ones.ap, rhs=sqd.ap, start=True, stop=True), reads=[ones, sqd], writes=[pss])
            P.emit("act", lambda e: e.activation(out=rstd.ap, in_=pss.ap, func=AF.Sqrt, bias=EPS, scale=1.0 / 128),
                   reads=[pss], writes=[rstd])
            P.emit("dve", lambda e: e.reciprocal(out=rstd.ap, in_=rstd.ap), reads=[rstd], writes=[rstd])
            P.emit("dve", lambda e: e.tensor_scalar(out=rstd.ap, in0=rstd.ap, scalar1=(1.0 - lam_init), scalar2=None, op0=ALU.mult),
                   reads=[rstd], writes=[rstd])
            o_ = ob[0]
            P.emit("dve", lambda e: e.scalar_tensor_tensor(out=o_.ap, in0=o1.ap, scalar=vcol(V_SUBLN + L), in1=rstd.ap, op0=ALU.mult, op1=ALU.mult),
                   reads=[o1, rstd, vecs], writes=[o_])
            P.emit("sync", lambda e: e.dma_start(out=mx_loc[L][1 + hh][:, Q8 * NT:(Q8 + 1) * NT], in_=o_.ap),
                   reads=[o_], writes=[mxl_t[1 + hh]], kind="dma")
        for hh in range(2):
            for Q8 in range(8):
                diff_tile(hh, Q8)
        P.barrier()
        abf.reset(mb)
        af.reset(mf)

    def mix_mla(L, ug_t, mxl_t):
        mb, mf = abf.mark(), af.mark()
        win_v = wview(win, L)
        wt = T(abf.alloc(KC * 448))
        wv = wt.ap.rearrange("p (k c) -> p k c", k=KC)
        P.emit("pool", lambda e: e.dma_start(out=wv, in_=win_v[:, :, 1664:2112]), writes=[wt], kind="dma")
        wq_t = T(abf.alloc(2 * 384))
        wqv = wq_t.ap.rearrange("p (k c) -> p k c", k=2)
        P.emit("pool", lambda e: e.dma_start(out=wqv, in_=wview(wuq, L)), writes=[wq_t], kind="dma")
        wkv_t = T(abf.alloc(256))
        P.emit("pool", lambda e: e.dma_start(out=wkv_t.ap, in_=wukv[L]), writes=[wkv_t], kind="dma")
        qk_t = T(None, "qk")
        v_t = T(None, "v")
        qT2 = abf.alloc(2 * S).rearrange("p (h t) -> p h t", h=2)
        kT2 = abf.alloc(2 * S).rearrange("p (h t) -> p h t", h=2)
        vm = abf.alloc(32 * 256).rearrange("p (b c) -> p b c", b=32)
        P.emit("dve", lambda e: e.memset(vm, 1.0), writes=[v_t])
        uts = [[T(abf.alloc(NT)) for _ in range(KC)] for _ in range(2)]
        cqn = [T(abf.alloc(NT)) for _ in range(2)]
        ckvn = [T(abf.alloc(NT))]
        sqm = [T(abf.alloc(NT)) for _ in range(2)]
        cosM = [T(af.alloc(NT)) for _ in range(2)]
        sinM = [T(af.alloc(NT)) for _ in range(2)]
        cosK = [T(af.alloc(NT)) for _ in range(2)]
        sinK = [T(af.alloc(NT)) for _ in range(2)]
        cq = [T(af.alloc(NT)) for _ in range(2)]
        ckv = [T(af.alloc(NT))]
        t1 = T(af.alloc(NT))
        t2 = T(af.alloc(NT))
        rstd = T(af.alloc(NT))
        sc_m = 96.0 ** -0.5

        def mla_proj(T8):
            ut = uts[T8 % 2]
            load_u(L, T8, ut, ug_t)
            b2 = T8 % 2
            load_tab(2, T8, cosM[b2])
            load_tab(3, T8, sinM[b2])
            load_tab(4, T8, cosK[b2])
            load_tab(5, T8, sinK[b2])
            for c in range(2):
                proj_fm(psum[c], wv, c * 128, 128, ut, wt)
                P.emit("act", (lambda c: lambda e: e.activation(out=cq[c].ap, in_=psum[c].ap, func=AF.Copy))(c), reads=[psum[c]], writes=[cq[c]])
            proj_fm(psum[2], wv, 256, 128, ut, wt)
            P.emit("act", lambda e: e.activation(out=ckv[0].ap, in_=psum[2].ap, func=AF.Copy), reads=[psum[2]], writes=[ckv[0]])
            rmsnorm(cq, 2, 256, V_QN + L * 2, cqn, sqm, psum[3], rstd)
            rmsnorm(ckv, 1, 128, V_KVN + L, ckvn, sqm, psum[3], rstd)
            for hh in range(2):
                px, pxp = psum[4], psum[5]
                for c in range(2):
                    P.emit("pe", (lambda c, hh: lambda e: e.matmul(px.ap[0:96, :], lhsT=wqv[:, c, hh * 96:(hh + 1) * 96], rhs=cqn[c].ap,
                                                                   start=(c == 0), stop=(c == 1)))(c, hh), reads=[wq_t, cqn[c]], writes=[px])
                for c in range(2):
                    P.emit("pe", (lambda c, hh: lambda e: e.matmul(pxp.ap[0:96, :], lhsT=wqv[:, c, 192 + hh * 96:192 + (hh + 1) * 96], rhs=cqn[c].ap,
                                                                   start=(c == 0), stop=(c == 1)))(c, hh), reads=[wq_t, cqn[c]], writes=[pxp])
                rope_evac(px, pxp, cosM[b2], sinM[b2], 96, qT2[0:96, hh, T8 * NT:(T8 + 1) * NT], qk_t, t1, t2, sc_m)
            for hh in range(2):
                pkn = psum[6]
                P.emit("pe", (lambda hh: lambda e: e.matmul(pkn.ap[0:64, :], lhsT=wkv_t.ap[:, hh * 64:(hh + 1) * 64], rhs=ckvn[0].ap, start=True, stop=True))(hh),
                       reads=[wkv_t, ckvn[0]], writes=[pkn])
                P.emit("act", (lambda hh: lambda e: e.activation(out=kT2[0:64, hh, T8 * NT:(T8 + 1) * NT], in_=pkn.ap[0:64, :], func=AF.Copy))(hh),
                       reads=[pkn], writes=[qk_t])
            px, pxp = psum[4], psum[5]
            proj_fm(px, wv, 384, 32, ut, wt)
            proj_fm(pxp, wv, 416, 32, ut, wt)
            rope_evac(px, pxp, cosK[b2], sinK[b2], 32, t1.ap[0:32, :], t1, t1, t2, 1.0)
            for hh in range(2):
                P.emit("act", (lambda hh: lambda e: e.activation(out=kT2[64:96, hh, T8 * NT:(T8 + 1) * NT], in_=t1.ap[0:32, :], func=AF.Copy))(hh),
                       reads=[t1], writes=[qk_t])
            for tb in range(4):
                pvv = psum[7]
                P.emit("pe", (lambda tb: lambda e: e.matmul(pvv.ap[:, 0:128], lhsT=ckvn[0].ap[:, tb * 128:(tb + 1) * 128], rhs=wkv_t.ap[:, 128:256],
                                                            start=True, stop=True))(tb), reads=[wkv_t, ckvn[0]], writes=[pvv])
                for hh in range(2):
                    P.emit("act", (lambda tb, hh: lambda e: e.activation(out=vm[:, T8 * 4 + tb, hh * 128:hh * 128 + 64],
                                                                         in_=pvv.ap[:, hh * 64:(hh + 1) * 64], func=AF.Copy))(tb, hh),
                           reads=[pvv], writes=[v_t])
        for T8 in range(8):
            mla_proj(T8)
        P.barrier()
        af.reset(mf)
        pts = [T(abf.alloc(NT)) for _ in range(4)]
        ob = [T(abf.alloc(NT)) for _ in range(2)]
        rec = T(af.alloc(NT))

        def mla_tile(hh, Q8):
            pO = psum[2 + (Q8 % 2)]
            softmax_attn(qT2[0:96, hh, :], kT2[0:96, hh, :], qk_t, v_t,
                         (lambda i: vm[:, i, hh * 128:(hh + 1) * 128]), Q8, pO, None, [psum[0], psum[1], psum[4], psum[5]], pts, False)
            P.emit("act", lambda e: e.activation(out=rec.ap[0:64, :], in_=pO.ap[64:128, :], func=AF.Copy), reads=[pO], writes=[rec])
            P.emit("dve", lambda e: e.reciprocal(out=rec.ap[0:64, :], in_=rec.ap[0:64, :]), reads=[rec], writes=[rec])
            o_ = ob[Q8 % 2]
            P.emit("dve", lambda e: e.tensor_tensor(out=o_.ap[0:64, :], in0=pO.ap[0:64, :], in1=rec.ap[0:64, :], op=ALU.mult),
                   reads=[pO, rec], writes=[o_])
            P.emit("sync", lambda e: e.dma_start(out=mx_loc[L][3][hh * 64:(hh + 1) * 64, Q8 * NT:(Q8 + 1) * NT], in_=o_.ap[0:64, :]),
                   reads=[o_], writes=[mxl_t[3]], kind="dma")
        for hh in range(2):
            for Q8 in range(8):
                mla_tile(hh, Q8)
        P.barrier()
        abf.reset(mb)
        af.reset(mf)

    def mix_out(L, mxl_t):
        mb, mf = abf.mark(), af.mark()
        mxg_t = [T(None) for c in range(4)]
        for c in range(4):
            P.emit("pool", (lambda c: lambda e: e.collective_compute("AllGather", ALU.bypass, replica_groups=GROUPS,
                                                                    ins=[mx_loc[L][c].ap().opt()], outs=[mx_g[L][c].ap().opt()]))(c),
                   reads=[mxl_t[c]], writes=[mxg_t[c]], kind="cc")
        wo_t = T(abf.alloc(KC * D))
        wov = wo_t.ap.rearrange("p (k c) -> p k c", k=KC)
        for half in range(2):
            P.emit("pool", (lambda half: lambda e: e.dma_start(out=wov[:, half * 4:(half + 1) * 4, :],
                                                              in_=wview(wout, L)[:, half * 4:(half + 1) * 4, :]))(half),
                   writes=[wo_t], kind="dma")
        ca = [[T(abf.alloc(NT)) for _ in range(KC)] for _ in range(2)]
        cb_ = [[T(abf.alloc(NT)) for _ in range(KC)] for _ in range(2)]
        ms = [[T(abf.alloc(NT)) for _ in range(KC)] for _ in range(2)]

        def wo_tile(t):
            for K in range(KC):
                c, rp = K // 2, K % 2
                A, B, M = ca[t % 2][K], cb_[t % 2][K], ms[t % 2][K]
                P.emit("sync", (lambda A, c, rp: lambda e: e.dma_start(out=A.ap, in_=mx_g[L][c][rp * 128:(rp + 1) * 128, t * NT:(t + 1) * NT]))(A, c, rp),
                       reads=[mxg_t[c]], writes=[A], kind="dma")
                P.emit("sync", (lambda B, c, rp: lambda e: e.dma_start(out=B.ap, in_=mx_g[L][c][rp * 128:(rp + 1) * 128, TL + t * NT:TL + (t + 1) * NT]))(B, c, rp),
                       reads=[mxg_t[c]], writes=[B], kind="dma")
                P.emit("dve", (lambda A, M: lambda e: e.tensor_scalar(out=M.ap, in0=A.ap, scalar1=vcol(V_SEL), scalar2=None, op0=ALU.mult))(A, M),
                       reads=[A, vecs], writes=[M])
                P.emit("dve", (lambda B, M: lambda e: e.scalar_tensor_tensor(out=M.ap, in0=B.ap, scalar=vcol(V_SEL + 1), in1=M.ap,
                                                                             op0=ALU.mult, op1=ALU.add))(B, M),
                       reads=[B, M, vecs], writes=[M])
            if dbg is not None and dbg[0] == "mixraw":
                for K in range(KC):
                    M = ms[t % 2][K]
                    P.emit("pool", (lambda K, M: lambda e: e.dma_start(out=outT[K * 128:(K + 1) * 128, t * NT:(t + 1) * NT], in_=M.ap))(K, M),
                           reads=[M], kind="dma")
                return
            for dc in range(KC):
                po = psum[dc % 2]
                for K in range(KC):
                    M = ms[t % 2][K]
                    P.emit("pe", (lambda K, M, dc, po: lambda e: e.matmul(po.ap, lhsT=wov[:, K, dc * 128:(dc + 1) * 128], rhs=M.ap,
                                                                          start=(K == 0), stop=(K == KC - 1)))(K, M, dc, po),
                           reads=[wo_t, M], writes=[po], inc=(K == KC - 1))
                ht = h[dc][t]
                P.emit("dve", (lambda po, ht: lambda e: e.tensor_tensor(out=ht.ap, in0=ht.ap, in1=po.ap, op=ALU.add))(po, ht),
                       reads=[po, ht], writes=[ht])
        for t in range(4):
            wo_tile(t)
        P.barrier()
        abf.reset(mb)
        af.reset(mf)

    def mixer(L):
        ug_t = [T(None) for c in range(4)]
        mxl_t = [T(None) for c in range(4)]
        mix_gather(L, ug_t)
        mix_sb(L, ug_t, mxl_t)
        mix_diff(L, ug_t, mxl_t)
        mix_mla(L, ug_t, mxl_t)
        mix_out(L, mxl_t)

    def final():
        sq = [T(abf.alloc(NT)) for _ in range(KC)]
        rstd = T(af.alloc(NT))
        ot = [T(af.alloc(NT)) for _ in range(KC)]
        for t in range(4):
            rmsnorm([h[kc][t] for kc in range(KC)], KC, D, V_GFIN, ot, sq, psum[0], rstd)
            for kc in range(KC):
                P.emit("sync", (lambda kc, t: lambda e: e.dma_start(out=outT[kc * 128:(kc + 1) * 128, t * NT:(t + 1) * NT], in_=ot[kc].ap))(kc, t),
                       reads=[ot[kc]], kind="dma")

    def dump_h():
        for kc in range(KC):
            P.emit("sync", (lambda kc: lambda e: e.dma_start(out=outT[kc * 128:(kc + 1) * 128, :], in_=hT[:, kc, :]))(kc),
                   reads=h[kc], kind="dma")

    setup()
    stop = False
    if dbg is not None and dbg[0] == "setup":
        stop = True
        depth = 0
    for L in range(depth):
        ffn(L, w1gu, w1d, V_GF1 + L * 8)
        if dbg == ("ffn1", L):
            stop = True
            break
        mixer(L)
        if dbg == ("mix", L):
            stop = True
            break
        if dbg == ("mixraw", L):
            stop = None
            break
        ffn(L, w2gu, w2d, V_GF2 + L * 8)
        ple(L)
        if dbg == ("layer", L):
            stop = True
            break
    if stop:
        dump_h()
    elif stop is None:
        pass
    else:
        final()

    with nc.Block() as block:
        @block.tensor
        def _(e):
            P.replay("pe", e)

        @block.scalar
        def _(e):
            P.replay("act", e)

        @block.vector
        def _(e):
            P.replay("dve", e)

        @block.gpsimd
        def _(e):
            P.replay("pool", e)

        @block.sync
        def _(e):
            P.replay("sync", e)
            P.final_wait("sync", e)
    es.close()
    return nc


def _win_cols(r):
    cols = []
    H = [2 * r, 2 * r + 1]
    for base in (0, 256, 512):
        for hh in H:
            cols += [base + hh * 64 + d for d in range(64)]

    def dperm(d):
        return d + 8 if d < 8 else (d - 8 if d < 16 else d)
    for base in (768, 1280):
        for perm in (False, True):
            for hh in H:
                for c in range(2):
                    for d in range(64):
                        dd = dperm(d) if perm else d
                        cols.append(base + hh * 128 + c * 64 + dd)
    for hh in H:
        cols += [1792 + hh * 128 + e for e in range(128)]
    cols += list(range(2304, 2560))
    cols += list(range(2560, 2688))
    cols += list(range(2688, 2720))
    cols += [2688 + (j + 16 if j < 16 else j - 16) for j in range(32)]
    assert len(cols) == NWIN
    return np.array(cols)


def _wuq_cols(r):
    cols = []
    H = [2 * r, 2 * r + 1]
    for perm in (False, True):
        for hh in H:
            for j in range(96):
                jj = j
                if perm and j >= 64:
                    m = j - 64
                    jj = 64 + (m + 16 if m < 16 else m - 16)
                cols.append(hh * 96 + jj)
    return np.array(cols)


def _wukv_cols(r):
    H = [2 * r, 2 * r + 1]
    cols = []
    for hh in H:
        cols += [hh * 128 + j for j in range(64)]
    for hh in H:
        cols += [hh * 128 + 64 + j for j in range(64)]
    return np.array(cols)


def _wout_rows():
    rows = []
    for c in range(4):
        for rp in range(2):
            for i in range(128):
                if c == 0:
                    rows.append(128 * rp + i)
                elif c == 1:
                    rows.append(256 + (2 * rp) * 128 + i)
                elif c == 2:
                    rows.append(256 + (2 * rp + 1) * 128 + i)
                else:
                    rows.append(768 + 128 * rp + i)
    return np.array(rows)


def _const_tables():
    import ml_dtypes
    kp = np.arange(128)[:, None]
    qf = np.arange(512)[None, :]
    cm = np.zeros((128, 8 * 512 + 128), np.float32)
    for j in range(4):
        cm[:, j * 512:(j + 1) * 512] = (qf >= 128 * j + kp)
        cm[:, (4 + j) * 512:(5 + j) * 512] = (qf > 128 * j + kp)
    jj = np.arange(128)[:, None]
    ss = np.arange(128)[None, :]
    cm[:, 8 * 512:] = (jj >= ss)
    return cm.astype(ml_dtypes.bfloat16)


def _freq_cols():
    invf = np.zeros((128, 3), np.float64)
    sgn = np.zeros((128, 3), np.float64)
    for row in range(128):
        d = row % 64
        if d < 16:
            invf[row, 0] = THETA ** (-(d % 8) / 8.0)
            sgn[row, 0] = -1.0 if d < 8 else 1.0
        if 64 <= row < 96:
            m = row - 64
            invf[row, 1] = THETA ** (-(m % 16) / 16.0)
            sgn[row, 1] = -1.0 if m < 16 else 1.0
        if row < 32:
            invf[row, 2] = THETA ** (-(row % 16) / 16.0)
            sgn[row, 2] = -1.0 if row < 16 else 1.0
    return invf.astype(np.float32), sgn.astype(np.float32)


_NC_CACHE = {}


def make_in_maps(inp):
    f32 = np.float32
    g = {k: np.asarray(v) for k, v in inp.items()}

    def fm(v, nch):
        v = np.asarray(v, f32)
        L = v.shape[0]
        return v.reshape(L, nch, 128).transpose(2, 0, 1).reshape(128, L * nch)
    vec_common = np.zeros((128, NV), f32)
    vec_common[:, V_GF1:V_GF1 + 32] = fm(g["norm_ffn1"], 8)
    vec_common[:, V_GMIX:V_GMIX + 32] = fm(g["norm_mix"], 8)
    vec_common[:, V_GF2:V_GF2 + 32] = fm(g["norm_ffn2"], 8)
    vec_common[:, V_GPLE:V_GPLE + 32] = fm(g["norm_ple"], 8)
    vec_common[:, V_GFIN:V_GFIN + 8] = fm(g["norm_final"][None, :], 8)
    vec_common[:, V_QN:V_QN + 8] = fm(g["mla_q_norm"], 2)
    vec_common[:, V_KVN:V_KVN + 4] = fm(g["mla_kv_norm"], 1)
    vec_common[:, V_SUBLN:V_SUBLN + 4] = fm(g["diff_subln"], 1)
    invf, sgn = _freq_cols()
    vec_common[:, V_INVF:V_INVF + 3] = invf
    vec_common[:, V_SGN:V_SGN + 3] = sgn
    vec_common[:, V_NEGPI] = -math.pi
    for j, nm in enumerate(("diff_lambda_q1", "diff_lambda_k1", "diff_lambda_q2", "diff_lambda_k2")):
        vec_common[:, V_LAM + j * 256:V_LAM + (j + 1) * 256] = np.asarray(g[nm], f32).reshape(1, 256)
    cmask = _const_tables()
    wout_p = np.ascontiguousarray(np.asarray(g["w_out"], f32)[:, _wout_rows(), :])
    per_rank = []
    for r in range(2):
        per_rank.append(dict(
            win=np.ascontiguousarray(np.asarray(g["w_in"], f32)[:, :, _win_cols(r)]),
            wuq=np.ascontiguousarray(np.asarray(g["mla_w_uq"], f32)[:, :, _wuq_cols(r)]),
            wukv=np.ascontiguousarray(np.asarray(g["mla_w_ukv"], f32)[:, :, _wukv_cols(r)]),
        ))
    shared = dict(
        w1gu=np.ascontiguousarray(g["w_ffn1_gu"], dtype=f32), w1d=np.ascontiguousarray(g["w_ffn1_down"], dtype=f32),
        w2gu=np.ascontiguousarray(g["w_ffn2_gu"], dtype=f32), w2d=np.ascontiguousarray(g["w_ffn2_down"], dtype=f32),
        wout=wout_p, wgate=np.ascontiguousarray(g["w_ple_gate"], dtype=f32), wproj=np.ascontiguousarray(g["w_ple_proj"], dtype=f32),
        cmask=cmask,
    )
    x = np.asarray(g["x"], f32)
    p = np.asarray(g["p"], f32)
    pos = np.asarray(g["positions"]).astype(np.int32)
    maps = []
    for core in range(8):
        b, r = core // 2, core % 2
        sl = slice(r * TL, (r + 1) * TL)
        vec = vec_common.copy()
        vec[:, V_SEL + r] = 1.0
        m = dict(shared)
        m.update(per_rank[r])
        m["xT"] = np.ascontiguousarray(x[b, sl, :].T)
        m["pT"] = np.ascontiguousarray(p[:, b, sl, :].transpose(0, 2, 1))
        m["posr"] = np.ascontiguousarray(np.broadcast_to(pos[b][None, :], (128, S)))
        m["vecs"] = vec
        maps.append(m)
    return maps


def run(inp, depth=DEPTH, dbg=None, trace=False):
    key = (depth, dbg)
    if key not in _NC_CACHE:
        _NC_CACHE[key] = build_program(depth, dbg)
    nc = _NC_CACHE[key]
    maps = make_in_maps(inp)
    res = run_bass_kernel_spmd(nc, maps, core_ids=list(range(8)), trace=trace)
    out = np.zeros((4, S, D), np.float32)
    for core in range(8):
        b, r = core // 2, core % 2
        out[b, r * TL:(r + 1) * TL, :] = np.asarray(res.results[core]["outT"]).T
    return out, res


def kernel(**inputs):
    out, _ = run(inputs)
    return out
```

```python
import math
from contextlib import ExitStack
import numpy as np
import concourse.bass as bass
import concourse.mybir as mybir
from concourse.bass_utils import run_bass_kernel_spmd

F32 = mybir.dt.float32
BF16 = mybir.dt.bfloat16
I32 = mybir.dt.int32
ALU = mybir.AluOpType
AF = mybir.ActivationFunctionType
AX = mybir.AxisListType

D = 1024
KC = 8
S = 4096
TL = 2048
NT = 512
DFF = 2816
FC = 22
DEPTH = 4
EPS = 1e-6
THETA = 500000.0
NWIN = 2112
GROUPS = [[0, 1], [2, 3], [4, 5], [6, 7]]

V_GF1, V_GMIX, V_GF2, V_GPLE = 0, 32, 64, 96
V_GFIN = 128
V_QN = 136
V_KVN = 144
V_SUBLN = 148
V_SEL = 152
V_INVF = 154
V_SGN = 157
V_NEGPI = 160
V_LAM = 164
NV = V_LAM + 4 * 4 * 64


class T:
    __slots__ = ("ap", "w", "r", "name")

    def __init__(self, ap, name=""):
        self.ap = ap
        self.w = {}
        self.r = {}
        self.name = name


class Prog:
    ISSUERS = ("pe", "act", "dve", "pool", "sync")
    KSLOT = 8
    KQ = {"sync": 8, "pool": 4}

    def __init__(self, nc, es):
        self.nc = nc
        self.es = es
        self.ops = {e: [] for e in self.ISSUERS}
        self.seen = {e: {} for e in self.ISSUERS}
        self.cnt = {}
        self.sem = {}
        self.dma_i = {"sync": 0, "pool": 0}
        self.ncc = 0
        for e in ("pe", "act", "dve", "pool"):
            self.sem[e] = es.enter_context(nc.semaphore("s_" + e))
            self.cnt[e] = 0
        for q in ("sync", "pool"):
            for k in range(self.KSLOT):
                p = (q, k)
                self.sem[p] = es.enter_context(nc.semaphore("d_%s%d" % (q, k)))
                self.cnt[p] = 0

    def _waits(self, issuer, deps, skip_self_pe=True):
        out = []
        seen = self.seen[issuer]
        for p, c in deps.items():
            if c <= 0:
                continue
            if p == "pe" and issuer == "pe":
                continue
            if seen.get(p, 0) >= c:
                continue
            seen[p] = c
            mult = 1 if isinstance(p, str) else (16 if p[0] != "cc" else 1)
            out.append((self.sem[p], c * mult))
        return out

    def emit(self, issuer, fn, reads=(), writes=(), kind="c", inc=True):
        deps = {}

        def merge(d):
            for p, c in d.items():
                if deps.get(p, 0) < c:
                    deps[p] = c
        for t in reads:
            merge(t.w)
        for t in writes:
            merge(t.w)
            merge(t.r)
        if kind == "c":
            prod = issuer
            inc_default = 1
        elif kind == "dma":
            i = self.dma_i[issuer]
            self.dma_i[issuer] = i + 1
            prod = (issuer, i % self.KQ[issuer])
            if self.cnt[prod] > 0:
                merge({prod: self.cnt[prod]})
            inc_default = 16
        else:
            prod = ("cc", self.ncc)
            self.ncc += 1
            self.sem[prod] = self.es.enter_context(self.nc.semaphore("cc%d" % prod[1]))
            self.cnt[prod] = 0
            inc_default = 1
        waits = self._waits(issuer, deps)
        if kind == "c" and not inc:
            my = self.cnt[prod] + 1
            inc_amt = 0
        else:
            self.cnt[prod] += 1
            my = self.cnt[prod]
            inc_amt = inc_default
        for t in reads:
            if t.r.get(prod, 0) < my:
                t.r[prod] = my
        for t in writes:
            t.w = {prod: my}
            t.r = {}
        self.ops[issuer].append((waits, fn, self.sem[prod], inc_amt))

    def barrier(self):
        allp = {p: c for p, c in self.cnt.items() if c > 0}
        for issuer in self.ISSUERS:
            waits = self._waits(issuer, dict(allp))
            if issuer == "pe" and self.cnt["pe"] > 0:
                pass
            if waits:
                self.ops[issuer].append((waits, None, None, 0))

    def replay(self, issuer, eng):
        for waits, fn, sem, inc in self.ops[issuer]:
            for s, v in waits:
                eng.wait_ge(s, v)
            if fn is not None:
                if inc:
                    fn(eng).then_inc(sem, inc)
                else:
                    fn(eng)

    def final_wait(self, issuer, eng):
        for p, c in self.cnt.items():
            if c > 0:
                mult = 1 if isinstance(p, str) else (16 if p[0] != "cc" else 1)
                eng.wait_ge(self.sem[p], c * mult)


class Arena:
    def __init__(self, ap, n):
        self.ap = ap
        self.n = n
        self.off = 0

    def alloc(self, ncols):
        assert self.off + ncols <= self.n, ("arena overflow", self.off, ncols, self.n)
        a = self.ap[:, self.off:self.off + ncols]
        self.off += ncols
        return a

    def mark(self):
        return self.off

    def reset(self, m):
        self.off = m


def build_program(depth=DEPTH, dbg=None):
    nc = bass.Bass("TRN2", target_bir_lowering=False)
    es = ExitStack()

    def din(name, shape, dt=F32):
        return nc.dram_tensor(name, list(shape), dt, kind="ExternalInput").ap()

    xT = din("xT", [D, TL])
    pT = din("pT", [DEPTH, 256, TL])
    posr = din("posr", [128, S], I32)
    vecs_d = din("vecs", [128, NV])
    cmask_d = din("cmask", [128, 8 * 512 + 128], BF16)
    w1gu = din("w1gu", [DEPTH, D, 2 * DFF])
    w1d = din("w1d", [DEPTH, DFF, D])
    w2gu = din("w2gu", [DEPTH, D, 2 * DFF])
    w2d = din("w2d", [DEPTH, DFF, D])
    win = din("win", [DEPTH, D, NWIN])
    wuq = din("wuq", [DEPTH, 256, 384])
    wukv = din("wukv", [DEPTH, 128, 256])
    wout = din("wout", [DEPTH, D, D])
    wgate = din("wgate", [DEPTH, D, D])
    wproj = din("wproj", [DEPTH, 256, D])
    outT = nc.dram_tensor("outT", [D, TL], F32, kind="ExternalOutput").ap()

    tabs = [nc.dram_tensor("tab%d" % i, [128, S], F32) for i in range(6)]
    u_loc = [[nc.dram_tensor("uloc%d_%d" % (L, c), [256, TL], BF16) for c in range(4)] for L in range(depth)]
    u_g = [[nc.dram_tensor("ug%d_%d" % (L, c), [512, TL], BF16) for c in range(4)] for L in range(depth)]
    mx_loc = [[nc.dram_tensor("mxl%d_%d" % (L, c), [128, S], BF16) for c in range(4)] for L in range(depth)]
    mx_g = [[nc.dram_tensor("mxg%d_%d" % (L, c), [256, S], BF16) for c in range(4)] for L in range(depth)]

    NBF = 50176
    NF = 8448
    hT_t = es.enter_context(nc.sbuf_tensor("hT", [128, KC * TL], F32))
    abf_t = es.enter_context(nc.sbuf_tensor("abf", [128, NBF], BF16))
    af_t = es.enter_context(nc.sbuf_tensor("af32", [128, NF], F32))
    P = Prog(nc, es)
    abf = Arena(abf_t[:, :], NBF)
    af = Arena(af_t[:, :], NF)
    psum = [T(es.enter_context(nc.psum_tensor("ps%d" % i, [128, 512], F32))[:, :], "ps%d" % i) for i in range(8)]

    hT = hT_t[:, :].rearrange("p (k t) -> p k t", k=KC)
    h = [[T(hT[:, kc, t * NT:(t + 1) * NT], "h%d_%d" % (kc, t)) for t in range(4)] for kc in range(KC)]

    vecs = T(af.alloc(NV), "vecs")
    lamv = T(af.alloc(8), "lamv")
    cm = T(abf.alloc(8 * 512 + 128), "cmask")
    ones = T(abf.alloc(128), "ones")
    P.emit("sync", lambda e: e.dma_start(out=vecs.ap, in_=vecs_d[:, :]), writes=[vecs], kind="dma")
    P.emit("pool", lambda e: e.dma_start(out=cm.ap, in_=cmask_d[:, :]), writes=[cm], kind="dma")
    P.emit("dve", lambda e: e.memset(ones.ap, 1.0), writes=[ones])
    maskI = [cm.ap[:, j * 512:(j + 1) * 512] for j in range(4)]
    maskS = [cm.ap[:, (4 + j) * 512:(5 + j) * 512] for j in range(4)]
    trim = cm.ap[:, 8 * 512:8 * 512 + 128]
    for kc in range(KC):
        P.emit("sync", (lambda kc: lambda e: e.dma_start(out=hT[:, kc, :], in_=xT[kc * 128:(kc + 1) * 128, :]))(kc),
               writes=h[kc], kind="dma")
    pers_bf = abf.mark()
    pers_f = af.mark()

    def vcol(c, n=1):
        return vecs.ap[:, c:c + n]

    def setup():
        HS = 1024
        ki_t = es.enter_context(nc.sbuf_tensor("ki", [128, HS], I32))
        ki = T(ki_t[:, :])
        posf = T(af.alloc(HS))
        ang = T(af.alloc(HS))
        tq = T(af.alloc(HS))
        yy = T(af.alloc(HS))
        sv = T(af.alloc(HS))
        TWO_PI = 2 * math.pi
        for part in range(S // HS):
            c0 = part * HS
            P.emit("pool", (lambda c0: lambda e: e.dma_start(out=posf.ap, in_=posr[:, c0:c0 + HS]))(c0), writes=[posf], kind="dma")
            for s in range(3):
                P.emit("dve", (lambda s: lambda e: e.tensor_scalar(out=ang.ap, in0=posf.ap, scalar1=vcol(V_INVF + s), scalar2=None,
                                                                    op0=ALU.mult))(s), reads=[posf, vecs], writes=[ang])
                for which, phase in ((1, 0.0), (0, 0.5 * math.pi)):
                    P.emit("dve", (lambda phase: lambda e: e.tensor_scalar(out=tq.ap, in0=ang.ap, scalar1=phase, scalar2=1.0 / TWO_PI,
                                                                            op0=ALU.add, op1=ALU.mult))(phase), reads=[ang], writes=[tq])
                    P.emit("dve", lambda e: e.tensor_copy(out=ki.ap, in_=tq.ap), reads=[tq], writes=[ki])
                    P.emit("dve", lambda e: e.tensor_copy(out=tq.ap, in_=ki.ap), reads=[ki], writes=[tq])
                    P.emit("dve", (lambda phase: lambda e: e.tensor_scalar(out=yy.ap, in0=ang.ap, scalar1=phase, scalar2=None, op0=ALU.add))(phase),
                           reads=[ang], writes=[yy])
                    P.emit("dve", lambda e: e.scalar_tensor_tensor(out=yy.ap, in0=tq.ap, scalar=-TWO_PI, in1=yy.ap, op0=ALU.mult, op1=ALU.add),
                           reads=[tq, yy], writes=[yy])
                    P.emit("dve", lambda e: e.tensor_scalar(out=yy.ap, in0=yy.ap, scalar1=-3.141592, scalar2=3.141592, op0=ALU.max, op1=ALU.min),
                           reads=[yy], writes=[yy])
                    P.emit("act", lambda e: e.activation(out=sv.ap, in_=yy.ap, func=AF.Sin), reads=[yy], writes=[sv])
                    if which == 1:
                        P.emit("dve", (lambda s: lambda e: e.tensor_scalar(out=sv.ap, in0=sv.ap, scalar1=vcol(V_SGN + s), scalar2=None,
                                                                            op0=ALU.mult))(s), reads=[sv, vecs], writes=[sv])
                    tt = T(None)
                    P.emit("sync", (lambda s, which, c0: lambda e: e.dma_start(out=tabs[2 * s + which][:, c0:c0 + HS], in_=sv.ap))(s, which, c0),
                           reads=[sv], writes=[tt], kind="dma")
        pr = T(af.alloc(64))
        d12 = T(af.alloc(8))
        for L in range(depth):
            for j in range(2):
                a = V_LAM + (2 * j) * 256 + L * 64
                b = V_LAM + (2 * j + 1) * 256 + L * 64
                P.emit("dve", (lambda a, b: lambda e: e.tensor_tensor(out=pr.ap, in0=vcol(a, 64), in1=vcol(b, 64), op=ALU.mult))(a, b),
                       reads=[vecs], writes=[pr])
                P.emit("dve", (lambda j: lambda e: e.reduce_sum(out=d12.ap[:, j:j + 1], in_=pr.ap, axis=AX.X))(j), reads=[pr], writes=[d12])
            P.emit("act", lambda e: e.activation(out=d12.ap[:, 2:4], in_=d12.ap[:, 0:2], func=AF.Exp), reads=[d12], writes=[d12])
            lam_init = 0.8 - 0.6 * math.exp(-0.3 * L)
            P.emit("dve", (lambda L, li: lambda e: e.scalar_tensor_tensor(out=lamv.ap[:, L:L + 1], in0=d12.ap[:, 3:4], scalar=-li,
                                                                            in1=d12.ap[:, 2:3], op0=ALU.add, op1=ALU.subtract))(L, lam_init),
                   reads=[d12], writes=[lamv])
        P.barrier()
        af.reset(pers_f)

    def rmsnorm_a(src_chunks, nch, sq):
        for c in range(nch):
            P.emit("act", (lambda c: lambda e: e.activation(out=sq[c].ap, in_=src_chunks[c].ap, func=AF.Square))(c),
                   reads=[src_chunks[c]], writes=[sq[c]])

    def rmsnorm_b(src_chunks, nch, dim, gcol, out_tiles, sq, ps_ss, rstd):
        for c in range(nch):
            P.emit("pe", (lambda c: lambda e: e.matmul(ps_ss.ap, lhsT=ones.ap, rhs=sq[c].ap, start=(c == 0), stop=(c == nch - 1)))(c),
                   reads=[ones, sq[c]], writes=[ps_ss], inc=(c == nch - 1))
        P.emit("act", lambda e: e.activation(out=rstd.ap, in_=ps_ss.ap, func=AF.Sqrt, bias=EPS, scale=1.0 / dim),
               reads=[ps_ss], writes=[rstd])
        P.emit("dve", lambda e: e.reciprocal(out=rstd.ap, in_=rstd.ap), reads=[rstd], writes=[rstd])
        for c in range(nch):
            P.emit("dve", (lambda c: lambda e: e.scalar_tensor_tensor(out=out_tiles[c].ap, in0=src_chunks[c].ap, scalar=vcol(gcol + c),
                                                                       in1=rstd.ap, op0=ALU.mult, op1=ALU.mult))(c),
                   reads=[src_chunks[c], rstd, vecs], writes=[out_tiles[c]])

    def rmsnorm(src_chunks, nch, dim, gcol, out_tiles, sq, ps_ss, rstd, src_aps=None):
        rmsnorm_a(src_chunks, nch, sq)
        rmsnorm_b(src_chunks, nch, dim, gcol, out_tiles, sq, ps_ss, rstd)

    def wview(w, L, p=128):
        return w[L].rearrange("(k p) c -> p k c", p=p)

    def ffn(L, wgu, wd, gcol):
        mb, mf = abf.mark(), af.mark()
        u2 = [[T(abf.alloc(NT)) for _ in range(KC)] for _ in range(2)]
        act = [T(abf.alloc(NT)) for _ in range(FC)]
        NG = 3
        gbuf = [T(abf.alloc(KC * 256)) for _ in range(NG)]
        vbuf = [T(abf.alloc(KC * 256)) for _ in range(NG)]
        dbuf = [T(abf.alloc(FC * 256)) for _ in range(2)]
        rstd2 = [T(af.alloc(NT)) for _ in range(2)]
        sg = [T(af.alloc(NT)) for _ in range(2)]
        wg_v = wview(wgu, L)
        wd_v = wview(wd, L)
        gseq = [(t, j) for t in range(4) for j in range(11)]
        dseq = [(t, m) for t in range(4) for m in range(4)]
        gl = [0]
        dl = [0]

        def load_g(upto):
            while gl[0] <= upto and gl[0] < len(gseq):
                k = gl[0]
                _, j = gseq[k]
                b = k % NG
                gv = gbuf[b].ap.rearrange("p (k c) -> p k c", k=KC)
                vv = vbuf[b].ap.rearrange("p (k c) -> p k c", k=KC)
                P.emit("pool", (lambda gv, j: lambda e: e.dma_start(out=gv, in_=wg_v[:, :, j * 256:(j + 1) * 256]))(gv, j),
                       writes=[gbuf[b]], kind="dma")
                P.emit("pool", (lambda vv, j: lambda e: e.dma_start(out=vv, in_=wg_v[:, :, DFF + j * 256:DFF + (j + 1) * 256]))(vv, j),
                       writes=[vbuf[b]], kind="dma")
                gl[0] += 1

        def load_d(upto):
            while dl[0] <= upto and dl[0] < len(dseq):
                k = dl[0]
                _, m = dseq[k]
                b = k % 2
                dv = dbuf[b].ap.rearrange("p (f c) -> p f c", f=FC)
                P.emit("pool", (lambda dv, m: lambda e: e.dma_start(out=dv, in_=wd_v[:, :, m * 256:(m + 1) * 256]))(dv, m),
                       writes=[dbuf[b]], kind="dma")
                dl[0] += 1

        load_g(1)
        load_d(0)
        gk = 0
        dk = 0
        hsrc = lambda t: [h[kc][t] for kc in range(KC)]
        rmsnorm_a(hsrc(0), KC, u2[0])
        rmsnorm_b(hsrc(0), KC, D, gcol, u2[0], u2[0], psum[0], rstd2[0])
        for t in range(4):
            u = u2[t % 2]
            for j in range(11):
                if t + 1 < 4 and j == 5:
                    rmsnorm_a(hsrc(t + 1), KC, u2[(t + 1) % 2])
                if t + 1 < 4 and j == 8:
                    rmsnorm_b(hsrc(t + 1), KC, D, gcol, u2[(t + 1) % 2], u2[(t + 1) % 2], psum[0], rstd2[(t + 1) % 2])
                load_g(gk + 2)
                b = gk % NG
                gv = gbuf[b].ap.rearrange("p (k c) -> p k c", k=KC)
                vv = vbuf[b].ap.rearrange("p (k c) -> p k c", k=KC)
                for jj in range(2):
                    fc = 2 * j + jj
                    pg = psum[1 + (fc % 2) * 2]
                    pv = psum[2 + (fc % 2) * 2]
                    for kc in range(KC):
                        P.emit("pe", (lambda kc, jj, gv, pg, ur: lambda e: e.matmul(pg.ap, lhsT=gv[:, kc, jj * 128:(jj + 1) * 128], rhs=ur,
                                                                                start=(kc == 0), stop=(kc == KC - 1)))(kc, jj, gv, pg, u[kc].ap),
                               reads=[gbuf[b], u[kc]], writes=[pg], inc=(kc == KC - 1))
                    for kc in range(KC):
                        P.emit("pe", (lambda kc, jj, vv, pv, ur: lambda e: e.matmul(pv.ap, lhsT=vv[:, kc, jj * 128:(jj + 1) * 128], rhs=ur,
                                                                                start=(kc == 0), stop=(kc == KC - 1)))(kc, jj, vv, pv, u[kc].ap),
                               reads=[vbuf[b], u[kc]], writes=[pv], inc=(kc == KC - 1))
                    sgt = sg[fc % 2]
                    P.emit("act", (lambda pg, sgt: lambda e: e.activation(out=sgt.ap, in_=pg.ap, func=AF.Silu))(pg, sgt),
                           reads=[pg], writes=[sgt])
                    P.emit("dve", (lambda pv, sgt, fc: lambda e: e.tensor_tensor(out=act[fc].ap, in0=sgt.ap, in1=pv.ap, op=ALU.mult))(pv, sgt, fc),
                           reads=[pv, sgt], writes=[act[fc]])
                gk += 1
            for m in range(4):
                load_d(dk + 1)
                b = dk % 2
                dv = dbuf[b].ap.rearrange("p (f c) -> p f c", f=FC)
                for jj in range(2):
                    dc = 2 * m + jj
                    po = psum[5 + (dc % 2)]
                    for fc in range(FC):
                        P.emit("pe", (lambda fc, jj, dv, po: lambda e: e.matmul(po.ap, lhsT=dv[:, fc, jj * 128:(jj + 1) * 128], rhs=act[fc].ap,
                                                                                start=(fc == 0), stop=(fc == FC - 1)))(fc, jj, dv, po),
                               reads=[dbuf[b], act[fc]], writes=[po], inc=(fc == FC - 1))
                    ht = h[dc][t]
                    P.emit("dve", (lambda po, ht: lambda e: e.scalar_tensor_tensor(out=ht.ap, in0=po.ap, scalar=0.5, in1=ht.ap,
                                                                                    op0=ALU.mult, op1=ALU.add))(po, ht),
                           reads=[po, ht], writes=[ht])
                dk += 1
        P.barrier()
        abf.reset(mb)
        af.reset(mf)

    def ple(L):
        mb, mf = abf.mark(), af.mark()
        u = [T(abf.alloc(NT)) for _ in range(KC)]
        sq = [T(abf.alloc(NT)) for _ in range(KC)]
        wg = T(abf.alloc(KC * D))
        wp = T(abf.alloc(2 * D))
        pt = [T(abf.alloc(2 * NT)) for _ in range(2)]
        rstd = T(af.alloc(NT))
        sgm = [T(af.alloc(NT)) for _ in range(2)]
        wgv = wg.ap.rearrange("p (k c) -> p k c", k=KC)
        wpv = wp.ap.rearrange("p (k c) -> p k c", k=2)
        for half in range(2):
            P.emit("pool", (lambda half: lambda e: e.dma_start(out=wgv[:, half * 4:(half + 1) * 4, :],
                                                              in_=wview(wgate, L)[:, half * 4:(half + 1) * 4, :]))(half),
                   writes=[wg], kind="dma")
        P.emit("pool", lambda e: e.dma_start(out=wpv, in_=wview(wproj, L)), writes=[wp], kind="dma")
        for t in range(4):
            ptv = pt[t % 2].ap.rearrange("p (k c) -> p k c", k=2)
            P.emit("pool", (lambda ptv, t: lambda e: e.dma_start(out=ptv, in_=pT[L].rearrange("(k p) t -> p k t", p=128)[:, :, t * NT:(t + 1) * NT]))(ptv, t),
                   writes=[pt[t % 2]], kind="dma")
            rmsnorm([h[kc][t] for kc in range(KC)], KC, D, V_GPLE + L * 8, u, sq, psum[0], rstd)
            for dc in range(KC):
                pg = psum[1 + (dc % 2) * 2]
                pp = psum[2 + (dc % 2) * 2]
                for kc in range(KC):
                    P.emit("pe", (lambda kc, dc, pg: lambda e: e.matmul(pg.ap, lhsT=wgv[:, kc, dc * 128:(dc + 1) * 128], rhs=u[kc].ap,
                                                                        start=(kc == 0), stop=(kc == KC - 1)))(kc, dc, pg),
                           reads=[wg, u[kc]], writes=[pg], inc=(kc == KC - 1))
                for k2 in range(2):
                    P.emit("pe", (lambda k2, dc, pp, ptv: lambda e: e.matmul(pp.ap, lhsT=wpv[:, k2, dc * 128:(dc + 1) * 128], rhs=ptv[:, k2, :],
                                                                             start=(k2 == 0), stop=(k2 == 1)))(k2, dc, pp, ptv),
                           reads=[wp, pt[t % 2]], writes=[pp], inc=(k2 == 1))
                s_ = sgm[dc % 2]
                P.emit("act", (lambda pg, s_: lambda e: e.activation(out=s_.ap, in_=pg.ap, func=AF.Sigmoid))(pg, s_), reads=[pg], writes=[s_])
                P.emit("dve", (lambda pp, s_: lambda e: e.tensor_tensor(out=s_.ap, in0=s_.ap, in1=pp.ap, op=ALU.mult))(pp, s_),
                       reads=[pp, s_], writes=[s_])
                ht = h[dc][t]
                P.emit("dve", (lambda s_, ht: lambda e: e.tensor_tensor(out=ht.ap, in0=ht.ap, in1=s_.ap, op=ALU.add))(s_, ht),
                       reads=[s_, ht], writes=[ht])
        P.barrier()
        abf.reset(mb)
        af.reset(mf)

    def load_u(L, T8, ut, ug_t):
        rank, lt = T8 // 4, T8 % 4
        for c in range(4):
            for i in range(2):
                kc = 2 * c + i
                P.emit("sync", (lambda kc, c, i: lambda e: e.dma_start(
                    out=ut[kc].ap, in_=u_g[L][c][rank * 256 + i * 128: rank * 256 + (i + 1) * 128, lt * NT:(lt + 1) * NT]))(kc, c, i),
                    reads=[ug_t[c]], writes=[ut[kc]], kind="dma")

    def load_tab(idx, T8, dst):
        P.emit("sync", lambda e: e.dma_start(out=dst.ap, in_=tabs[idx][:, T8 * NT:(T8 + 1) * NT]), writes=[dst], kind="dma")

    def proj_fm(ps, wv, c0, ncol, ut, wt):
        for kc in range(KC):
            P.emit("pe", (lambda kc: lambda e: e.matmul(ps.ap[0:ncol, :], lhsT=wv[:, kc, c0:c0 + ncol], rhs=ut[kc].ap,
                                                        start=(kc == 0), stop=(kc == KC - 1)))(kc),
                   reads=[wt, ut[kc]], writes=[ps], inc=(kc == KC - 1))

    def proj_tm(ps, wv, c0, ncol, ut, wt, tb):
        for kc in range(KC):
            P.emit("pe", (lambda kc: lambda e: e.matmul(ps.ap[:, 0:ncol], lhsT=ut[kc].ap[:, tb * 128:(tb + 1) * 128], rhs=wv[:, kc, c0:c0 + ncol],
                                                        start=(kc == 0), stop=(kc == KC - 1)))(kc),
                   reads=[wt, ut[kc]], writes=[ps], inc=(kc == KC - 1))

    def rope_evac(px, pxp, cosT, sinT, nrow, out_ap, out_t, t1, t2, scale):
        P.emit("dve", lambda e: e.scalar_tensor_tensor(out=t1.ap[0:nrow, :], in0=px.ap[0:nrow, :], scalar=scale, in1=cosT.ap[0:nrow, :],
                                                       op0=ALU.mult, op1=ALU.mult),
               reads=[px, cosT], writes=[t1])
        P.emit("dve", lambda e: e.scalar_tensor_tensor(out=t2.ap[0:nrow, :], in0=pxp.ap[0:nrow, :], scalar=scale, in1=sinT.ap[0:nrow, :],
                                                       op0=ALU.mult, op1=ALU.mult),
               reads=[pxp, sinT], writes=[t2])
        P.emit("dve", lambda e: e.tensor_tensor(out=out_ap, in0=t1.ap[0:nrow, :], in1=t2.ap[0:nrow, :], op=ALU.add),
               reads=[t1, t2], writes=[out_t])

    def softmax_attn(qT_ap, kT_ap, qk_t, v_t, vaug_fn, Q8, pO, pD, sbank, pts, den_ones):
        nkb = 4 * Q8 + 4
        q_ap = qT_ap[:, Q8 * NT:(Q8 + 1) * NT]

        NB = len(sbank)
        LA = NB - 1

        def s_mm(i):
            ps = sbank[i % NB]
            P.emit("pe", lambda e: e.matmul(ps.ap, lhsT=kT_ap[:, i * 128:(i + 1) * 128], rhs=q_ap, start=True, stop=True),
                   reads=[qk_t], writes=[ps])
        def o_mm(i):
            pt_ = pts[i % len(pts)]
            va = vaug_fn(i)
            P.emit("pe", lambda e: e.matmul(pO.ap, lhsT=va, rhs=pt_.ap, start=(i == 0), stop=(i == nkb - 1)), reads=[v_t, pt_], writes=[pO])
            if den_ones:
                P.emit("pe", lambda e: e.matmul(pD.ap, lhsT=ones.ap, rhs=pt_.ap, start=(i == 0), stop=(i == nkb - 1)), reads=[ones, pt_], writes=[pD])
        for i0 in range(min(LA, nkb)):
            s_mm(i0)
        for i in range(nkb):
            if i + LA < nkb:
                s_mm(i + LA)
            ps = sbank[i % NB]
            pt_ = pts[i % len(pts)]
            P.emit("act", (lambda ps, pt_: lambda e: e.activation(out=pt_.ap, in_=ps.ap, func=AF.Exp))(ps, pt_), reads=[ps], writes=[pt_])
            jd = i - 4 * Q8
            if jd >= 0:
                P.emit("dve", (lambda pt_, jd: lambda e: e.tensor_tensor(out=pt_.ap, in0=pt_.ap, in1=maskI[jd], op=ALU.mult))(pt_, jd),
                       reads=[pt_, cm], writes=[pt_])
            if i >= 1:
                o_mm(i - 1)
        o_mm(nkb - 1)

    def mix_gather(L, ug_t):
        mb0, mf0 = abf.mark(), af.mark()
        u = [T(abf.alloc(NT)) for _ in range(KC)]
        sq = [T(abf.alloc(NT)) for _ in range(KC)]
        rstd = T(af.alloc(NT))
        uloc_t = [T(None) for c in range(4)]
        for t in range(4):
            rmsnorm([h[kc][t] for kc in range(KC)], KC, D, V_GMIX + L * 8, u, sq, psum[0], rstd)
            for kc in range(KC):
                c, i = kc // 2, kc % 2
                P.emit("sync", (lambda kc, c, i, t: lambda e: e.dma_start(out=u_loc[L][c][i * 128:(i + 1) * 128, t * NT:(t + 1) * NT], in_=u[kc].ap))(kc, c, i, t),
                       reads=[u[kc]], writes=[uloc_t[c]], kind="dma")
        for c in range(4):
            P.emit("pool", (lambda c: lambda e: e.collective_compute("AllGather", ALU.bypass, replica_groups=GROUPS,
                                                                    ins=[u_loc[L][c].ap().opt()], outs=[u_g[L][c].ap().opt()]))(c),
                   reads=[uloc_t[c]], writes=[ug_t[c]], kind="cc")
        P.barrier()
        abf.reset(mb0)
        af.reset(mf0)

    def mix_sb(L, ug_t, mxl_t):
        mb, mf = abf.mark(), af.mark()
        win_v = wview(win, L)
        wt = T(abf.alloc(KC * 384))
        wv = wt.ap.rearrange("p (k c) -> p k c", k=KC)
        P.emit("pool", lambda e: e.dma_start(out=wv, in_=win_v[:, :, 0:384]), writes=[wt], kind="dma")
        qk_t = T(None, "qk")
        v_t = T(None, "v")
        qT = abf.alloc(S)
        kT = abf.alloc(S)
        vv_ = abf.alloc(32 * 128).rearrange("p (b c) -> p b c", b=32)
        uts = [[T(abf.alloc(NT)) for _ in range(KC)] for _ in range(2)]
        for T8 in range(8):
            ut = uts[T8 % 2]
            load_u(L, T8, ut, ug_t)
            pq, pk = psum[(T8 % 2) * 2], psum[(T8 % 2) * 2 + 1]
            proj_fm(pq, wv, 0, 128, ut, wt)
            proj_fm(pk, wv, 128, 128, ut, wt)
            P.emit("act", (lambda pq, T8: lambda e: e.activation(out=qT[:, T8 * NT:(T8 + 1) * NT], in_=pq.ap, func=AF.Copy, scale=0.125))(pq, T8),
                   reads=[pq], writes=[qk_t])
            P.emit("dve", (lambda pk, T8: lambda e: e.tensor_copy(out=kT[:, T8 * NT:(T8 + 1) * NT], in_=pk.ap))(pk, T8), reads=[pk], writes=[qk_t])
            for tb in range(4):
                pvv = psum[4 + tb % 2]
                proj_tm(pvv, wv, 256, 128, ut, wt, tb)
                P.emit("act", (lambda pvv, T8, tb: lambda e: e.activation(out=vv_[:, T8 * 4 + tb, :], in_=pvv.ap[:, 0:128], func=AF.Copy))(pvv, T8, tb),
                       reads=[pvv], writes=[v_t])
        ebuf = [T(af.alloc(NT)) for _ in range(3)]
        t1b = [T(af.alloc(NT)) for _ in range(2)]
        Rt = T(af.alloc(NT))
        spb = [T(abf.alloc(NT)) for _ in range(3)]
        Ab = [T(abf.alloc(NT)) for _ in range(2)]
        ob = [T(abf.alloc(NT)) for _ in range(2)]
        zb = [psum[0], psum[1], psum[7]]
        cb = [psum[2], psum[3]]
        csb = [psum[4], psum[5]]
        pO = psum[6]

        def sb_tile(hh, Q8):
            r0 = hh * 64
            nkb = 4 * Q8 + 4
            order = list(range(nkb - 1, -1, -1))
            q_ap = qT[r0:r0 + 64, Q8 * NT:(Q8 + 1) * NT]

            def st1(n):
                i = order[n]
                pz = zb[n % 3]
                P.emit("pe", lambda e: e.matmul(pz.ap, lhsT=kT[r0:r0 + 64, i * 128:(i + 1) * 128], rhs=q_ap, start=True, stop=True),
                       reads=[qk_t], writes=[pz])
                eb, sp = ebuf[n % 3], spb[n % 3]
                P.emit("act", lambda e: e.activation(out=eb.ap, in_=pz.ap, func=AF.Exp), reads=[pz], writes=[eb])
                P.emit("act", lambda e: e.activation(out=sp.ap, in_=eb.ap, func=AF.Ln, bias=1.0, scale=1.0), reads=[eb], writes=[sp])
                jd = i - 4 * Q8
                if jd >= 0:
                    P.emit("dve", lambda e: e.tensor_tensor(out=sp.ap, in0=sp.ap, in1=maskS[jd], op=ALU.mult), reads=[sp, cm], writes=[sp])

            def st2(n):
                i = order[n]
                pz, sp = zb[n % 3], spb[n % 3]
                pc, pcs = cb[n % 2], csb[n % 2]
                P.emit("pe", lambda e: e.matmul(pc.ap, lhsT=trim, rhs=sp.ap, start=True, stop=True), reads=[cm, sp], writes=[pc])
                if n < nkb - 1:
                    P.emit("pe", lambda e: e.matmul(pcs.ap, lhsT=ones.ap, rhs=sp.ap, start=True, stop=True), reads=[ones, sp], writes=[pcs])
                t1 = t1b[n % 2]
                if n == 0:
                    P.emit("dve", lambda e: e.tensor_copy(out=t1.ap, in_=pz.ap), reads=[pz], writes=[t1])
                else:
                    P.emit("dve", lambda e: e.tensor_tensor(out=t1.ap, in0=pz.ap, in1=Rt.ap, op=ALU.subtract), reads=[pz, Rt], writes=[t1])
                P.emit("dve", lambda e: e.tensor_tensor(out=t1.ap, in0=t1.ap, in1=pc.ap, op=ALU.subtract), reads=[t1, pc], writes=[t1])
                A = Ab[n % 2]
                P.emit("act", lambda e: e.activation(out=A.ap, in_=t1.ap, func=AF.Exp), reads=[t1], writes=[A])
                jd = i - 4 * Q8
                if jd >= 0:
                    P.emit("dve", lambda e: e.tensor_tensor(out=A.ap, in0=A.ap, in1=maskS[jd], op=ALU.mult), reads=[A, cm], writes=[A])
                if n < nkb - 1:
                    if n == 0:
                        P.emit("dve", lambda e: e.tensor_copy(out=Rt.ap, in_=pcs.ap), reads=[pcs], writes=[Rt])
                    else:
                        P.emit("dve", lambda e: e.tensor_tensor(out=Rt.ap, in0=Rt.ap, in1=pcs.ap, op=ALU.add), reads=[pcs, Rt], writes=[Rt])

            def st3(n):
                i = order[n]
                A = Ab[n % 2]
                P.emit("pe", lambda e: e.matmul(pO.ap[0:64, :], lhsT=vv_[:, i, r0:r0 + 64], rhs=A.ap, start=(n == 0), stop=(n == nkb - 1)),
                       reads=[v_t, A], writes=[pO])
            st1(0)
            st1(1)
            for n in range(nkb):
                if n + 2 < nkb:
                    st1(n + 2)
                st2(n)
                if n >= 1:
                    st3(n - 1)
            st3(nkb - 1)
            o_ = ob[Q8 % 2]
            P.emit("act", lambda e: e.activation(out=o_.ap[0:64, :], in_=pO.ap[0:64, :], func=AF.Copy), reads=[pO], writes=[o_])
            P.emit("sync", lambda e: e.dma_start(out=mx_loc[L][0][r0:r0 + 64, Q8 * NT:(Q8 + 1) * NT], in_=o_.ap[0:64, :]),
                   reads=[o_], writes=[mxl_t[0]], kind="dma")
        for hh in range(2):
            for Q8 in range(8):
                sb_tile(hh, Q8)
        P.barrier()
        abf.reset(mb)
        af.reset(mf)

    def mix_diff(L, ug_t, mxl_t):
        mb, mf = abf.mark(), af.mark()
        win_v = wview(win, L)
        wt = T(abf.alloc(KC * 1280))
        wv = wt.ap.rearrange("p (k c) -> p k c", k=KC)
        for part in range(5):
            P.emit("pool", (lambda part: lambda e: e.dma_start(out=wv[:, :, part * 256:(part + 1) * 256],
                                                              in_=win_v[:, :, 384 + part * 256:384 + (part + 1) * 256]))(part),
                   writes=[wt], kind="dma")
        qk_t = T(None, "qk")
        v_t = T(None, "v")
        qT2 = abf.alloc(2 * S).rearrange("p (h t) -> p h t", h=2)
        kT2 = abf.alloc(2 * S).rearrange("p (h t) -> p h t", h=2)
        vd = abf.alloc(32 * 256).rearrange("p (b c) -> p b c", b=32)
        uts = [[T(abf.alloc(NT)) for _ in range(KC)] for _ in range(2)]
        cosT = [T(af.alloc(NT)) for _ in range(2)]
        sinT = [T(af.alloc(NT)) for _ in range(2)]
        t1 = T(af.alloc(NT))
        t2 = T(af.alloc(NT))
        for T8 in range(8):
            ut = uts[T8 % 2]
            load_u(L, T8, ut, ug_t)
            load_tab(0, T8, cosT[T8 % 2])
            load_tab(1, T8, sinT[T8 % 2])
            n = 0
            for which, dstT, cbase in ((0, qT2, 0), (1, kT2, 512)):
                for hh in range(2):
                    px, pxp = psum[(n % 2) * 2], psum[(n % 2) * 2 + 1]
                    n += 1
                    proj_fm(px, wv, cbase + hh * 128, 128, ut, wt)
                    proj_fm(pxp, wv, cbase + 256 + hh * 128, 128, ut, wt)
                    rope_evac(px, pxp, cosT[T8 % 2], sinT[T8 % 2], 128, dstT[:, hh, T8 * NT:(T8 + 1) * NT], qk_t, t1, t2,
                              0.125 if which == 0 else 1.0)
            for tb in range(4):
                pvv = psum[4 + tb % 2]
                proj_tm(pvv, wv, 1024, 256, ut, wt, tb)
                P.emit("act", (lambda pvv, T8, tb: lambda e: e.activation(out=vd[:, T8 * 4 + tb, :], in_=pvv.ap[:, 0:256], func=AF.Copy))(pvv, T8, tb),
                       reads=[pvv], writes=[v_t])
        P.barrier()
        af.reset(mf)
        pts = [T(abf.alloc(NT)) for _ in range(3)]
        ob = [T(abf.alloc(NT)) for _ in range(1)]
        sqd = T(abf.alloc(NT))
        rec = T(af.alloc(NT))
        o1 = T(af.alloc(NT))
        o2 = T(af.alloc(NT))
        rstd = T(af.alloc(NT))
        lam_init = 0.8 - 0.6 * math.exp(-0.3 * L)

        def diff_tile(hh, Q8):
            for comp in range(2):
                r0 = comp * 64
                pO, pD = psum[2 + comp * 2], psum[3 + comp * 2]
                softmax_attn(qT2[r0:r0 + 64, hh, :], kT2[r0:r0 + 64, hh, :], qk_t, v_t,
                             (lambda i: vd[:, i, hh * 128:(hh + 1) * 128]), Q8, pO, pD, [psum[0], psum[1], psum[7]], pts, True)
                oc = o1 if comp == 0 else o2
                P.emit("dve", (lambda pD: lambda e: e.reciprocal(out=rec.ap, in_=pD.ap))(pD), reads=[pD], writes=[rec])
                P.emit("dve", (lambda pO, oc: lambda e: e.tensor_tensor(out=oc.ap, in0=pO.ap, in1=rec.ap, op=ALU.mult))(pO, oc),
                       reads=[pO, rec], writes=[oc])
            P.emit("dve", lambda e: e.scalar_tensor_tensor(out=o1.ap, in0=o2.ap, scalar=lamv.ap[:, L:L + 1], in1=o1.ap, op0=ALU.mult, op1=ALU.add),
                   reads=[o1, o2, lamv], writes=[o1])
            P.emit("act", lambda e: e.activation(out=sqd.ap, in_=o1.ap, func=AF.Square), reads=[o1], writes=[sqd])
            pss = psum[6]
            P.emit("pe", lambda e: e.matmul(pss.ap, lhsT=ones.ap, rhs=sqd.ap, start=True, stop=True), reads=[ones, sqd], writes=[pss])
            P.emit("act", lambda e: e.activation(out=rstd.ap, in_=pss.ap, func=AF.Sqrt, bias=EPS, scale=1.0 / 128),
                   reads=[pss], writes=[rstd])
            P.emit("dve", lambda e: e.reciprocal(out=rstd.ap, in_=rstd.ap), reads=[rstd], writes=[rstd])
            P.emit("dve", lambda e: e.tensor_scalar(out=rstd.ap, in0=rstd.ap, scalar1=(1.0 - lam_init), scalar2=None, op0=ALU.mult),
                   reads=[rstd], writes=[rstd])
            o_ = ob[0]
            P.emit("dve", lambda e: e.scalar_tensor_tensor(out=o_.ap, in0=o1.ap, scalar=vcol(V_SUBLN + L), in1=rstd.ap, op0=ALU.mult, op1=ALU.mult),
                   reads=[o1, rstd, vecs], writes=[o_])
            P.emit("sync", lambda e: e.dma_start(out=mx_loc[L][1 + hh][:, Q8 * NT:(Q8 + 1) * NT], in_=o_.ap),
                   reads=[o_], writes=[mxl_t[1 + hh]], kind="dma")
        for hh in range(2):
            for Q8 in range(8):
                diff_tile(hh, Q8)
        P.barrier()
        abf.reset(mb)
        af.reset(mf)

    def mix_mla(L, ug_t, mxl_t):
        mb, mf = abf.mark(), af.mark()
        win_v = wview(win, L)
        wt = T(abf.alloc(KC * 448))
        wv = wt.ap.rearrange("p (k c) -> p k c", k=KC)
        P.emit("pool", lambda e: e.dma_start(out=wv, in_=win_v[:, :, 1664:2112]), writes=[wt], kind="dma")
        wq_t = T(abf.alloc(2 * 384))
        wqv = wq_t.ap.rearrange("p (k c) -> p k c", k=2)
        P.emit("pool", lambda e: e.dma_start(out=wqv, in_=wview(wuq, L)), writes=[wq_t], kind="dma")
        wkv_t = T(abf.alloc(256))
        P.emit("pool", lambda e: e.dma_start(out=wkv_t.ap, in_=wukv[L]), writes=[wkv_t], kind="dma")
        qk_t = T(None, "qk")
        v_t = T(None, "v")
        qT2 = abf.alloc(2 * S).rearrange("p (h t) -> p h t", h=2)
        kT2 = abf.alloc(2 * S).rearrange("p (h t) -> p h t", h=2)
        vm = abf.alloc(32 * 256).rearrange("p (b c) -> p b c", b=32)
        P.emit("dve", lambda e: e.memset(vm, 1.0), writes=[v_t])
        uts = [[T(abf.alloc(NT)) for _ in range(KC)] for _ in range(2)]
        cqn = [T(abf.alloc(NT)) for _ in range(2)]
        ckvn = [T(abf.alloc(NT))]
        sqm = [T(abf.alloc(NT)) for _ in range(2)]
        cosM = [T(af.alloc(NT)) for _ in range(2)]
        sinM = [T(af.alloc(NT)) for _ in range(2)]
        cosK = [T(af.alloc(NT)) for _ in range(2)]
        sinK = [T(af.alloc(NT)) for _ in range(2)]
        cq = [T(af.alloc(NT)) for _ in range(2)]
        ckv = [T(af.alloc(NT))]
        t1 = T(af.alloc(NT))
        t2 = T(af.alloc(NT))
        rstd = T(af.alloc(NT))
        sc_m = 96.0 ** -0.5

        def mla_proj(T8):
            ut = uts[T8 % 2]
            load_u(L, T8, ut, ug_t)
            b2 = T8 % 2
            load_tab(2, T8, cosM[b2])
            load_tab(3, T8, sinM[b2])
            load_tab(4, T8, cosK[b2])
            load_tab(5, T8, sinK[b2])
            for c in range(2):
                proj_fm(psum[c], wv, c * 128, 128, ut, wt)
                P.emit("act", (lambda c: lambda e: e.activation(out=cq[c].ap, in_=psum[c].ap, func=AF.Copy))(c), reads=[psum[c]], writes=[cq[c]])
            proj_fm(psum[2], wv, 256, 128, ut, wt)
            P.emit("act", lambda e: e.activation(out=ckv[0].ap, in_=psum[2].ap, func=AF.Copy), reads=[psum[2]], writes=[ckv[0]])
            rmsnorm(cq, 2, 256, V_QN + L * 2, cqn, sqm, psum[3], rstd)
            rmsnorm(ckv, 1, 128, V_KVN + L, ckvn, sqm, psum[3], rstd)
            for hh in range(2):
                px, pxp = psum[4], psum[5]
                for c in range(2):
                    P.emit("pe", (lambda c, hh: lambda e: e.matmul(px.ap[0:96, :], lhsT=wqv[:, c, hh * 96:(hh + 1) * 96], rhs=cqn[c].ap,
                                                                   start=(c == 0), stop=(c == 1)))(c, hh), reads=[wq_t, cqn[c]], writes=[px])
                for c in range(2):
                    P.emit("pe", (lambda c, hh: lambda e: e.matmul(pxp.ap[0:96, :], lhsT=wqv[:, c, 192 + hh * 96:192 + (hh + 1) * 96], rhs=cqn[c].ap,
                                                                   start=(c == 0), stop=(c == 1)))(c, hh), reads=[wq_t, cqn[c]], writes=[pxp])
                rope_evac(px, pxp, cosM[b2], sinM[b2], 96, qT2[0:96, hh, T8 * NT:(T8 + 1) * NT], qk_t, t1, t2, sc_m)
            for hh in range(2):
                pkn = psum[6]
                P.emit("pe", (lambda hh: lambda e: e.matmul(pkn.ap[0:64, :], lhsT=wkv_t.ap[:, hh * 64:(hh + 1) * 64], rhs=ckvn[0].ap, start=True, stop=True))(hh),
                       reads=[wkv_t, ckvn[0]], writes=[pkn])
                P.emit("act", (lambda hh: lambda e: e.activation(out=kT2[0:64, hh, T8 * NT:(T8 + 1) * NT], in_=pkn.ap[0:64, :], func=AF.Copy))(hh),
                       reads=[pkn], writes=[qk_t])
            px, pxp = psum[4], psum[5]
            proj_fm(px, wv, 384, 32, ut, wt)
            proj_fm(pxp, wv, 416, 32, ut, wt)
            rope_evac(px, pxp, cosK[b2], sinK[b2], 32, t1.ap[0:32, :], t1, t1, t2, 1.0)
            for hh in range(2):
                P.emit("act", (lambda hh: lambda e: e.activation(out=kT2[64:96, hh, T8 * NT:(T8 + 1) * NT], in_=t1.ap[0:32, :], func=AF.Copy))(hh),
                       reads=[t1], writes=[qk_t])
            for tb in range(4):
                pvv = psum[7]
                P.emit("pe", (lambda tb: lambda e: e.matmul(pvv.ap[:, 0:128], lhsT=ckvn[0].ap[:, tb * 128:(tb + 1) * 128], rhs=wkv_t.ap[:, 128:256],
                                                            start=True, stop=True))(tb), reads=[wkv_t, ckvn[0]], writes=[pvv])
                for hh in range(2):
                    P.emit("act", (lambda tb, hh: lambda e: e.activation(out=vm[:, T8 * 4 + tb, hh * 128:hh * 128 + 64],
                                                                         in_=pvv.ap[:, hh * 64:(hh + 1) * 64], func=AF.Copy))(tb, hh),
                           reads=[pvv], writes=[v_t])
        for T8 in range(8):
            mla_proj(T8)
        P.barrier()
        af.reset(mf)
        pts = [T(abf.alloc(NT)) for _ in range(4)]
        ob = [T(abf.alloc(NT)) for _ in range(2)]
        rec = T(af.alloc(NT))

        def mla_tile(hh, Q8):
            pO = psum[2 + (Q8 % 2)]
            softmax_attn(qT2[0:96, hh, :], kT2[0:96, hh, :], qk_t, v_t,
                         (lambda i: vm[:, i, hh * 128:(hh + 1) * 128]), Q8, pO, None, [psum[0], psum[1], psum[4], psum[5]], pts, False)
            P.emit("act", lambda e: e.activation(out=rec.ap[0:64, :], in_=pO.ap[64:128, :], func=AF.Copy), reads=[pO], writes=[rec])
            P.emit("dve", lambda e: e.reciprocal(out=rec.ap[0:64, :], in_=rec.ap[0:64, :]), reads=[rec], writes=[rec])
            o_ = ob[Q8 % 2]
            P.emit("dve", lambda e: e.tensor_tensor(out=o_.ap[0:64, :], in0=pO.ap[0:64, :], in1=rec.ap[0:64, :], op=ALU.mult),
                   reads=[pO, rec], writes=[o_])
            P.emit("sync", lambda e: e.dma_start(out=mx_loc[L][3][hh * 64:(hh + 1) * 64, Q8 * NT:(Q8 + 1) * NT], in_=o_.ap[0:64, :]),
                   reads=[o_], writes=[mxl_t[3]], kind="dma")
        for hh in range(2):
            for Q8 in range(8):
                mla_tile(hh, Q8)
        P.barrier()
        abf.reset(mb)
        af.reset(mf)

    def mix_out(L, mxl_t):
        mb, mf = abf.mark(), af.mark()
        mxg_t = [T(None) for c in range(4)]
        for c in range(4):
            P.emit("pool", (lambda c: lambda e: e.collective_compute("AllGather", ALU.bypass, replica_groups=GROUPS,
                                                                    ins=[mx_loc[L][c].ap().opt()], outs=[mx_g[L][c].ap().opt()]))(c),
                   reads=[mxl_t[c]], writes=[mxg_t[c]], kind="cc")
        wo_t = T(abf.alloc(KC * D))
        wov = wo_t.ap.rearrange("p (k c) -> p k c", k=KC)
        for half in range(2):
            P.emit("pool", (lambda half: lambda e: e.dma_start(out=wov[:, half * 4:(half + 1) * 4, :],
                                                              in_=wview(wout, L)[:, half * 4:(half + 1) * 4, :]))(half),
                   writes=[wo_t], kind="dma")
        ca = [[T(abf.alloc(NT)) for _ in range(KC)] for _ in range(2)]
        cb_ = [[T(abf.alloc(NT)) for _ in range(KC)] for _ in range(2)]
        ms = [[T(abf.alloc(NT)) for _ in range(KC)] for _ in range(2)]

        def wo_tile(t):
            for K in range(KC):
                c, rp = K // 2, K % 2
                A, B, M = ca[t % 2][K], cb_[t % 2][K], ms[t % 2][K]
                P.emit("sync", (lambda A, c, rp: lambda e: e.dma_start(out=A.ap, in_=mx_g[L][c][rp * 128:(rp + 1) * 128, t * NT:(t + 1) * NT]))(A, c, rp),
                       reads=[mxg_t[c]], writes=[A], kind="dma")
                P.emit("sync", (lambda B, c, rp: lambda e: e.dma_start(out=B.ap, in_=mx_g[L][c][rp * 128:(rp + 1) * 128, TL + t * NT:TL + (t + 1) * NT]))(B, c, rp),
                       reads=[mxg_t[c]], writes=[B], kind="dma")
                P.emit("dve", (lambda A, M: lambda e: e.tensor_scalar(out=M.ap, in0=A.ap, scalar1=vcol(V_SEL), scalar2=None, op0=ALU.mult))(A, M),
                       reads=[A, vecs], writes=[M])
                P.emit("dve", (lambda B, M: lambda e: e.scalar_tensor_tensor(out=M.ap, in0=B.ap, scalar=vcol(V_SEL + 1), in1=M.ap,
                                                                             op0=ALU.mult, op1=ALU.add))(B, M),
                       reads=[B, M, vecs], writes=[M])
            if dbg is not None and dbg[0] == "mixraw":
                for K in range(KC):
                    M = ms[t % 2][K]
                    P.emit("pool", (lambda K, M: lambda e: e.dma_start(out=outT[K * 128:(K + 1) * 128, t * NT:(t + 1) * NT], in_=M.ap))(K, M),
                           reads=[M], kind="dma")
                return
            for dc in range(KC):
                po = psum[dc % 2]
                for K in range(KC):
                    M = ms[t % 2][K]
                    P.emit("pe", (lambda K, M, dc, po: lambda e: e.matmul(po.ap, lhsT=wov[:, K, dc * 128:(dc + 1) * 128], rhs=M.ap,
                                                                          start=(K == 0), stop=(K == KC - 1)))(K, M, dc, po),
                           reads=[wo_t, M], writes=[po], inc=(K == KC - 1))
                ht = h[dc][t]
                P.emit("dve", (lambda po, ht: lambda e: e.tensor_tensor(out=ht.ap, in0=ht.ap, in1=po.ap, op=ALU.add))(po, ht),
                       reads=[po, ht], writes=[ht])
        for t in range(4):
            wo_tile(t)
        P.barrier()
        abf.reset(mb)
        af.reset(mf)

    def mixer(L):
        ug_t = [T(None) for c in range(4)]
        mxl_t = [T(None) for c in range(4)]
        mix_gather(L, ug_t)
        mix_sb(L, ug_t, mxl_t)
        mix_diff(L, ug_t, mxl_t)
        mix_mla(L, ug_t, mxl_t)
        mix_out(L, mxl_t)

    def final():
        sq = [T(abf.alloc(NT)) for _ in range(KC)]
        rstd = T(af.alloc(NT))
        ot = [T(af.alloc(NT)) for _ in range(KC)]
        for t in range(4):
            rmsnorm([h[kc][t] for kc in range(KC)], KC, D, V_GFIN, ot, sq, psum[0], rstd)
            for kc in range(KC):
                P.emit("sync", (lambda kc, t: lambda e: e.dma_start(out=outT[kc * 128:(kc + 1) * 128, t * NT:(t + 1) * NT], in_=ot[kc].ap))(kc, t),
                       reads=[ot[kc]], kind="dma")

    def dump_h():
        for kc in range(KC):
            P.emit("sync", (lambda kc: lambda e: e.dma_start(out=outT[kc * 128:(kc + 1) * 128, :], in_=hT[:, kc, :]))(kc),
                   reads=h[kc], kind="dma")

    setup()
    stop = False
    if dbg is not None and dbg[0] == "setup":
        stop = True
        depth = 0
    for L in range(depth):
        ffn(L, w1gu, w1d, V_GF1 + L * 8)
        if dbg == ("ffn1", L):
            stop = True
            break
        mixer(L)
        if dbg == ("mix", L):
            stop = True
            break
        if dbg == ("mixraw", L):
            stop = None
            break
        ffn(L, w2gu, w2d, V_GF2 + L * 8)
        ple(L)
        if dbg == ("layer", L):
            stop = True
            break
    if stop:
        dump_h()
    elif stop is None:
        pass
    else:
        final()

    with nc.Block() as block:
        @block.tensor
        def _(e):
            P.replay("pe", e)

        @block.scalar
        def _(e):
            P.replay("act", e)

        @block.vector
        def _(e):
            P.replay("dve", e)

        @block.gpsimd
        def _(e):
            P.replay("pool", e)

        @block.sync
        def _(e):
            P.replay("sync", e)
            P.final_wait("sync", e)
    es.close()
    return nc


def _win_cols(r):
    cols = []
    H = [2 * r, 2 * r + 1]
    for base in (0, 256, 512):
        for hh in H:
            cols += [base + hh * 64 + d for d in range(64)]

    def dperm(d):
        return d + 8 if d < 8 else (d - 8 if d < 16 else d)
    for base in (768, 1280):
        for perm in (False, True):
            for hh in H:
                for c in range(2):
                    for d in range(64):
                        dd = dperm(d) if perm else d
                        cols.append(base + hh * 128 + c * 64 + dd)
    for hh in H:
        cols += [1792 + hh * 128 + e for e in range(128)]
    cols += list(range(2304, 2560))
    cols += list(range(2560, 2688))
    cols += list(range(2688, 2720))
    cols += [2688 + (j + 16 if j < 16 else j - 16) for j in range(32)]
    assert len(cols) == NWIN
    return np.array(cols)


def _wuq_cols(r):
    cols = []
    H = [2 * r, 2 * r + 1]
    for perm in (False, True):
        for hh in H:
            for j in range(96):
                jj = j
                if perm and j >= 64:
                    m = j - 64
                    jj = 64 + (m + 16 if m < 16 else m - 16)
                cols.append(hh * 96 + jj)
    return np.array(cols)


def _wukv_cols(r):
    H = [2 * r, 2 * r + 1]
    cols = []
    for hh in H:
        cols += [hh * 128 + j for j in range(64)]
    for hh in H:
        cols += [hh * 128 + 64 + j for j in range(64)]
    return np.array(cols)


def _wout_rows():
    rows = []
    for c in range(4):
        for rp in range(2):
            for i in range(128):
                if c == 0:
                    rows.append(128 * rp + i)
                elif c == 1:
                    rows.append(256 + (2 * rp) * 128 + i)
                elif c == 2:
                    rows.append(256 + (2 * rp + 1) * 128 + i)
                else:
                    rows.append(768 + 128 * rp + i)
    return np.array(rows)


def _const_tables():
    import ml_dtypes
    kp = np.arange(128)[:, None]
    qf = np.arange(512)[None, :]
    cm = np.zeros((128, 8 * 512 + 128), np.float32)
    for j in range(4):
        cm[:, j * 512:(j + 1) * 512] = (qf >= 128 * j + kp)
        cm[:, (4 + j) * 512:(5 + j) * 512] = (qf > 128 * j + kp)
    jj = np.arange(128)[:, None]
    ss = np.arange(128)[None, :]
    cm[:, 8 * 512:] = (jj >= ss)
    return cm.astype(ml_dtypes.bfloat16)


def _freq_cols():
    invf = np.zeros((128, 3), np.float64)
    sgn = np.zeros((128, 3), np.float64)
    for row in range(128):
        d = row % 64
        if d < 16:
            invf[row, 0] = THETA ** (-(d % 8) / 8.0)
            sgn[row, 0] = -1.0 if d < 8 else 1.0
        if 64 <= row < 96:
            m = row - 64
            invf[row, 1] = THETA ** (-(m % 16) / 16.0)
            sgn[row, 1] = -1.0 if m < 16 else 1.0
        if row < 32:
            invf[row, 2] = THETA ** (-(row % 16) / 16.0)
            sgn[row, 2] = -1.0 if row < 16 else 1.0
    return invf.astype(np.float32), sgn.astype(np.float32)


_NC_CACHE = {}


def make_in_maps(inp):
    f32 = np.float32
    g = {k: np.asarray(v) for k, v in inp.items()}

    def fm(v, nch):
        v = np.asarray(v, f32)
        L = v.shape[0]
        return v.reshape(L, nch, 128).transpose(2, 0, 1).reshape(128, L * nch)
    vec_common = np.zeros((128, NV), f32)
    vec_common[:, V_GF1:V_GF1 + 32] = fm(g["norm_ffn1"], 8)
    vec_common[:, V_GMIX:V_GMIX + 32] = fm(g["norm_mix"], 8)
    vec_common[:, V_GF2:V_GF2 + 32] = fm(g["norm_ffn2"], 8)
    vec_common[:, V_GPLE:V_GPLE + 32] = fm(g["norm_ple"], 8)
    vec_common[:, V_GFIN:V_GFIN + 8] = fm(g["norm_final"][None, :], 8)
    vec_common[:, V_QN:V_QN + 8] = fm(g["mla_q_norm"], 2)
    vec_common[:, V_KVN:V_KVN + 4] = fm(g["mla_kv_norm"], 1)
    vec_common[:, V_SUBLN:V_SUBLN + 4] = fm(g["diff_subln"], 1)
    invf, sgn = _freq_cols()
    vec_common[:, V_INVF:V_INVF + 3] = invf
    vec_common[:, V_SGN:V_SGN + 3] = sgn
    vec_common[:, V_NEGPI] = -math.pi
    for j, nm in enumerate(("diff_lambda_q1", "diff_lambda_k1", "diff_lambda_q2", "diff_lambda_k2")):
        vec_common[:, V_LAM + j * 256:V_LAM + (j + 1) * 256] = np.asarray(g[nm], f32).reshape(1, 256)
    cmask = _const_tables()
    wout_p = np.ascontiguousarray(np.asarray(g["w_out"], f32)[:, _wout_rows(), :])
    per_rank = []
    for r in range(2):
        per_rank.append(dict(
            win=np.ascontiguousarray(np.asarray(g["w_in"], f32)[:, :, _win_cols(r)]),
            wuq=np.ascontiguousarray(np.asarray(g["mla_w_uq"], f32)[:, :, _wuq_cols(r)]),
            wukv=np.ascontiguousarray(np.asarray(g["mla_w_ukv"], f32)[:, :, _wukv_cols(r)]),
        ))
    shared = dict(
        w1gu=np.ascontiguousarray(g["w_ffn1_gu"], dtype=f32), w1d=np.ascontiguousarray(g["w_ffn1_down"], dtype=f32),
        w2gu=np.ascontiguousarray(g["w_ffn2_gu"], dtype=f32), w2d=np.ascontiguousarray(g["w_ffn2_down"], dtype=f32),
        wout=wout_p, wgate=np.ascontiguousarray(g["w_ple_gate"], dtype=f32), wproj=np.ascontiguousarray(g["w_ple_proj"], dtype=f32),
        cmask=cmask,
    )
    x = np.asarray(g["x"], f32)
    p = np.asarray(g["p"], f32)
    pos = np.asarray(g["positions"]).astype(np.int32)
    maps = []
    for core in range(8):
        b, r = core // 2, core % 2
        sl = slice(r * TL, (r + 1) * TL)
        vec = vec_common.copy()
        vec[:, V_SEL + r] = 1.0
        m = dict(shared)
        m.update(per_rank[r])
        m["xT"] = np.ascontiguousarray(x[b, sl, :].T)
        m["pT"] = np.ascontiguousarray(p[:, b, sl, :].transpose(0, 2, 1))
        m["posr"] = np.ascontiguousarray(np.broadcast_to(pos[b][None, :], (128, S)))
        m["vecs"] = vec
        maps.append(m)
    return maps


def run(inp, depth=DEPTH, dbg=None, trace=False):
    key = (depth, dbg)
    if key not in _NC_CACHE:
        _NC_CACHE[key] = build_program(depth, dbg)
    nc = _NC_CACHE[key]
    maps = make_in_maps(inp)
    res = run_bass_kernel_spmd(nc, maps, core_ids=list(range(8)), trace=trace)
    out = np.zeros((4, S, D), np.float32)
    for core in range(8):
        b, r = core // 2, core % 2
        out[b, r * TL:(r + 1) * TL, :] = np.asarray(res.results[core]["outT"]).T
    return out, res


def kernel(**inputs):
    out, _ = run(inputs)
    return out
```

```python
import math
from contextlib import ExitStack
import numpy as np
import concourse.bass as bass
import concourse.mybir as mybir
from concourse.bass_utils import run_bass_kernel_spmd

F32 = mybir.dt.float32
BF16 = mybir.dt.bfloat16
I32 = mybir.dt.int32
ALU = mybir.AluOpType
AF = mybir.ActivationFunctionType
AX = mybir.AxisListType

D = 1024
KC = 8
S = 4096
TL = 2048
NT = 512
DFF = 2816
FC = 22
DEPTH = 4
EPS = 1e-6
THETA = 500000.0
NWIN = 2112
GROUPS = [[0, 1], [2, 3], [4, 5], [6, 7]]

V_GF1, V_GMIX, V_GF2, V_GPLE = 0, 32, 64, 96
V_GFIN = 128
V_QN = 136
V_KVN = 144
V_SUBLN = 148
V_SEL = 152
V_INVF = 154
V_SGN = 157
V_NEGPI = 160
V_LAM = 164
NV = V_LAM + 4 * 4 * 64


class T:
    __slots__ = ("ap", "w", "r", "name")

    def __init__(self, ap, name=""):
        self.ap = ap
        self.w = {}
        self.r = {}
        self.name = name


class Prog:
    ISSUERS = ("pe", "act", "dve", "pool", "sync")
    KSLOT = 8
    KQ = {"sync": 8, "pool": 4}

    def __init__(self, nc, es):
        self.nc = nc
        self.es = es
        self.ops = {e: [] for e in self.ISSUERS}
        self.seen = {e: {} for e in self.ISSUERS}
        self.cnt = {}
        self.sem = {}
        self.dma_i = {"sync": 0, "pool": 0}
        self.ncc = 0
        for e in ("pe", "act", "dve", "pool"):
            self.sem[e] = es.enter_context(nc.semaphore("s_" + e))
            self.cnt[e] = 0
        for q in ("sync", "pool"):
            for k in range(self.KSLOT):
                p = (q, k)
                self.sem[p] = es.enter_context(nc.semaphore("d_%s%d" % (q, k)))
                self.cnt[p] = 0

    def _waits(self, issuer, deps, skip_self_pe=True):
        out = []
        seen = self.seen[issuer]
        for p, c in deps.items():
            if c <= 0:
                continue
            if p == "pe" and issuer == "pe":
                continue
            if seen.get(p, 0) >= c:
                continue
            seen[p] = c
            mult = 1 if isinstance(p, str) else (16 if p[0] != "cc" else 1)
            out.append((self.sem[p], c * mult))
        return out

    def emit(self, issuer, fn, reads=(), writes=(), kind="c", inc=True):
        deps = {}

        def merge(d):
            for p, c in d.items():
                if deps.get(p, 0) < c:
                    deps[p] = c
        for t in reads:
            merge(t.w)
        for t in writes:
            merge(t.w)
            merge(t.r)
        if kind == "c":
            prod = issuer
            inc_default = 1
        elif kind == "dma":
            i = self.dma_i[issuer]
            self.dma_i[issuer] = i + 1
            prod = (issuer, i % self.KQ[issuer])
            if self.cnt[prod] > 0:
                merge({prod: self.cnt[prod]})
            inc_default = 16
        else:
            prod = ("cc", self.ncc)
            self.ncc += 1
            self.sem[prod] = self.es.enter_context(self.nc.semaphore("cc%d" % prod[1]))
            self.cnt[prod] = 0
            inc_default = 1
        waits = self._waits(issuer, deps)
        if kind == "c" and not inc:
            my = self.cnt[prod] + 1
            inc_amt = 0
        else:
            self.cnt[prod] += 1
            my = self.cnt[prod]
            inc_amt = inc_default
        for t in reads:
            if t.r.get(prod, 0) < my:
                t.r[prod] = my
        for t in writes:
            t.w = {prod: my}
            t.r = {}
        self.ops[issuer].append((waits, fn, self.sem[prod], inc_amt))

    def barrier(self):
        allp = {p: c for p, c in self.cnt.items() if c > 0}
        for issuer in self.ISSUERS:
            waits = self._waits(issuer, dict(allp))
            if issuer == "pe" and self.cnt["pe"] > 0:
                pass
            if waits:
                self.ops[issuer].append((waits, None, None, 0))

    def replay(self, issuer, eng):
        for waits, fn, sem, inc in self.ops[issuer]:
            for s, v in waits:
                eng.wait_ge(s, v)
            if fn is not None:
                if inc:
                    fn(eng).then_inc(sem, inc)
                else:
                    fn(eng)

    def final_wait(self, issuer, eng):
        for p, c in self.cnt.items():
            if c > 0:
                mult = 1 if isinstance(p, str) else (16 if p[0] != "cc" else 1)
                eng.wait_ge(self.sem[p], c * mult)


class Arena:
    def __init__(self, ap, n):
        self.ap = ap
        self.n = n
        self.off = 0

    def alloc(self, ncols):
        assert self.off + ncols <= self.n, ("arena overflow", self.off, ncols, self.n)
        a = self.ap[:, self.off:self.off + ncols]
        self.off += ncols
        return a

    def mark(self):
        return self.off

    def reset(self, m):
        self.off = m


def build_program(depth=DEPTH, dbg=None):
    nc = bass.Bass("TRN2", target_bir_lowering=False)
    es = ExitStack()

    def din(name, shape, dt=F32):
        return nc.dram_tensor(name, list(shape), dt, kind="ExternalInput").ap()

    xT = din("xT", [D, TL])
    pT = din("pT", [DEPTH, 256, TL])
    posr = din("posr", [128, S], I32)
    vecs_d = din("vecs", [128, NV])
    cmask_d = din("cmask", [128, 8 * 512 + 128], BF16)
    w1gu = din("w1gu", [DEPTH, D, 2 * DFF])
    w1d = din("w1d", [DEPTH, DFF, D])
    w2gu = din("w2gu", [DEPTH, D, 2 * DFF])
    w2d = din("w2d", [DEPTH, DFF, D])
    win = din("win", [DEPTH, D, NWIN])
    wuq = din("wuq", [DEPTH, 256, 384])
    wukv = din("wukv", [DEPTH, 128, 256])
    wout = din("wout", [DEPTH, D, D])
    wgate = din("wgate", [DEPTH, D, D])
    wproj = din("wproj", [DEPTH, 256, D])
    outT = nc.dram_tensor("outT", [D, TL], F32, kind="ExternalOutput").ap()

    tabs = [nc.dram_tensor("tab%d" % i, [128, S], F32) for i in range(6)]
    u_loc = [[nc.dram_tensor("uloc%d_%d" % (L, c), [256, TL], BF16) for c in range(4)] for L in range(depth)]
    u_g = [[nc.dram_tensor("ug%d_%d" % (L, c), [512, TL], BF16) for c in range(4)] for L in range(depth)]
    mx_loc = [[nc.dram_tensor("mxl%d_%d" % (L, c), [128, S], BF16) for c in range(4)] for L in range(depth)]
    mx_g = [[nc.dram_tensor("mxg%d_%d" % (L, c), [256, S], BF16) for c in range(4)] for L in range(depth)]

    NBF = 50176
    NF = 8448
    hT_t = es.enter_context(nc.sbuf_tensor("hT", [128, KC * TL], F32))
    abf_t = es.enter_context(nc.sbuf_tensor("abf", [128, NBF], BF16))
    af_t = es.enter_context(nc.sbuf_tensor("af32", [128, NF], F32))
    P = Prog(nc, es)
    abf = Arena(abf_t[:, :], NBF)
    af = Arena(af_t[:, :], NF)
    psum = [T(es.enter_context(nc.psum_tensor("ps%d" % i, [128, 512], F32))[:, :], "ps%d" % i) for i in range(8)]

    hT = hT_t[:, :].rearrange("p (k t) -> p k t", k=KC)
    h = [[T(hT[:, kc, t * NT:(t + 1) * NT], "h%d_%d" % (kc, t)) for t in range(4)] for kc in range(KC)]

    vecs = T(af.alloc(NV), "vecs")
    lamv = T(af.alloc(8), "lamv")
    cm = T(abf.alloc(8 * 512 + 128), "cmask")
    ones = T(abf.alloc(128), "ones")
    P.emit("sync", lambda e: e.dma_start(out=vecs.ap, in_=vecs_d[:, :]), writes=[vecs], kind="dma")
    P.emit("pool", lambda e: e.dma_start(out=cm.ap, in_=cmask_d[:, :]), writes=[cm], kind="dma")
    P.emit("dve", lambda e: e.memset(ones.ap, 1.0), writes=[ones])
    maskI = [cm.ap[:, j * 512:(j + 1) * 512] for j in range(4)]
    maskS = [cm.ap[:, (4 + j) * 512:(5 + j) * 512] for j in range(4)]
    trim = cm.ap[:, 8 * 512:8 * 512 + 128]
    for kc in range(KC):
        P.emit("sync", (lambda kc: lambda e: e.dma_start(out=hT[:, kc, :], in_=xT[kc * 128:(kc + 1) * 128, :]))(kc),
               writes=h[kc], kind="dma")
    pers_bf = abf.mark()
    pers_f = af.mark()

    def vcol(c, n=1):
        return vecs.ap[:, c:c + n]

    def setup():
        HS = 1024
        ki_t = es.enter_context(nc.sbuf_tensor("ki", [128, HS], I32))
        ki = T(ki_t[:, :])
        posf = T(af.alloc(HS))
        ang = T(af.alloc(HS))
        tq = T(af.alloc(HS))
        yy = T(af.alloc(HS))
        sv = T(af.alloc(HS))
        TWO_PI = 2 * math.pi
        for part in range(S // HS):
            c0 = part * HS
            P.emit("pool", (lambda c0: lambda e: e.dma_start(out=posf.ap, in_=posr[:, c0:c0 + HS]))(c0), writes=[posf], kind="dma")
            for s in range(3):
                P.emit("dve", (lambda s: lambda e: e.tensor_scalar(out=ang.ap, in0=posf.ap, scalar1=vcol(V_INVF + s), scalar2=None,
                                                                    op0=ALU.mult))(s), reads=[posf, vecs], writes=[ang])
                for which, phase in ((1, 0.0), (0, 0.5 * math.pi)):
                    P.emit("dve", (lambda phase: lambda e: e.tensor_scalar(out=tq.ap, in0=ang.ap, scalar1=phase, scalar2=1.0 / TWO_PI,
                                                                            op0=ALU.add, op1=ALU.mult))(phase), reads=[ang], writes=[tq])
                    P.emit("dve", lambda e: e.tensor_copy(out=ki.ap, in_=tq.ap), reads=[tq], writes=[ki])
                    P.emit("dve", lambda e: e.tensor_copy(out=tq.ap, in_=ki.ap), reads=[ki], writes=[tq])
                    P.emit("dve", (lambda phase: lambda e: e.tensor_scalar(out=yy.ap, in0=ang.ap, scalar1=phase, scalar2=None, op0=ALU.add))(phase),
                           reads=[ang], writes=[yy])
                    P.emit("dve", lambda e: e.scalar_tensor_tensor(out=yy.ap, in0=tq.ap, scalar=-TWO_PI, in1=yy.ap, op0=ALU.mult, op1=ALU.add),
                           reads=[tq, yy], writes=[yy])
                    P.emit("dve", lambda e: e.tensor_scalar(out=yy.ap, in0=yy.ap, scalar1=-3.141592, scalar2=3.141592, op0=ALU.max, op1=ALU.min),
                           reads=[yy], writes=[yy])
                    P.emit("act", lambda e: e.activation(out=sv.ap, in_=yy.ap, func=AF.Sin), reads=[yy], writes=[sv])
                    if which == 1:
                        P.emit("dve", (lambda s: lambda e: e.tensor_scalar(out=sv.ap, in0=sv.ap, scalar1=vcol(V_SGN + s), scalar2=None,
                                                                            op0=ALU.mult))(s), reads=[sv, vecs], writes=[sv])
                    tt = T(None)
                    P.emit("sync", (lambda s, which, c0: lambda e: e.dma_start(out=tabs[2 * s + which][:, c0:c0 + HS], in_=sv.ap))(s, which, c0),
                           reads=[sv], writes=[tt], kind="dma")
        pr = T(af.alloc(64))
        d12 = T(af.alloc(8))
        for L in range(depth):
            for j in range(2):
                a = V_LAM + (2 * j) * 256 + L * 64
                b = V_LAM + (2 * j + 1) * 256 + L * 64
                P.emit("dve", (lambda a, b: lambda e: e.tensor_tensor(out=pr.ap, in0=vcol(a, 64), in1=vcol(b, 64), op=ALU.mult))(a, b),
                       reads=[vecs], writes=[pr])
                P.emit("dve", (lambda j: lambda e: e.reduce_sum(out=d12.ap[:, j:j + 1], in_=pr.ap, axis=AX.X))(j), reads=[pr], writes=[d12])
            P.emit("act", lambda e: e.activation(out=d12.ap[:, 2:4], in_=d12.ap[:, 0:2], func=AF.Exp), reads=[d12], writes=[d12])
            lam_init = 0.8 - 0.6 * math.exp(-0.3 * L)
            P.emit("dve", (lambda L, li: lambda e: e.scalar_tensor_tensor(out=lamv.ap[:, L:L + 1], in0=d12.ap[:, 3:4], scalar=-li,
                                                                            in1=d12.ap[:, 2:3], op0=ALU.add, op1=ALU.subtract))(L, lam_init),
                   reads=[d12], writes=[lamv])
        P.barrier()
        af.reset(pers_f)

    def rmsnorm_a(src_chunks, nch, sq):
        for c in range(nch):
            P.emit("act", (lambda c: lambda e: e.activation(out=sq[c].ap, in_=src_chunks[c].ap, func=AF.Square))(c),
                   reads=[src_chunks[c]], writes=[sq[c]])

    def rmsnorm_b(src_chunks, nch, dim, gcol, out_tiles, sq, ps_ss, rstd):
        for c in range(nch):
            P.emit("pe", (lambda c: lambda e: e.matmul(ps_ss.ap, lhsT=ones.ap, rhs=sq[c].ap, start=(c == 0), stop=(c == nch - 1)))(c),
                   reads=[ones, sq[c]], writes=[ps_ss], inc=(c == nch - 1))
        P.emit("act", lambda e: e.activation(out=rstd.ap, in_=ps_ss.ap, func=AF.Sqrt, bias=EPS, scale=1.0 / dim),
               reads=[ps_ss], writes=[rstd])
        P.emit("dve", lambda e: e.reciprocal(out=rstd.ap, in_=rstd.ap), reads=[rstd], writes=[rstd])
        for c in range(nch):
            P.emit("dve", (lambda c: lambda e: e.scalar_tensor_tensor(out=out_tiles[c].ap, in0=src_chunks[c].ap, scalar=vcol(gcol + c),
                                                                       in1=rstd.ap, op0=ALU.mult, op1=ALU.mult))(c),
                   reads=[src_chunks[c], rstd, vecs], writes=[out_tiles[c]])

    def rmsnorm(src_chunks, nch, dim, gcol, out_tiles, sq, ps_ss, rstd, src_aps=None):
        rmsnorm_a(src_chunks, nch, sq)
        rmsnorm_b(src_chunks, nch, dim, gcol, out_tiles, sq, ps_ss, rstd)

    def wview(w, L, p=128):
        return w[L].rearrange("(k p) c -> p k c", p=p)

    def ffn(L, wgu, wd, gcol):
        mb, mf = abf.mark(), af.mark()
        u2 = [[T(abf.alloc(NT)) for _ in range(KC)] for _ in range(2)]
        act = [T(abf.alloc(NT)) for _ in range(FC)]
        NG = 3
        gbuf = [T(abf.alloc(KC * 256)) for _ in range(NG)]
        vbuf = [T(abf.alloc(KC * 256)) for _ in range(NG)]
        dbuf = [T(abf.alloc(FC * 256)) for _ in range(2)]
        rstd2 = [T(af.alloc(NT)) for _ in range(2)]
        sg = [T(af.alloc(NT)) for _ in range(2)]
        wg_v = wview(wgu, L)
        wd_v = wview(wd, L)
        gseq = [(t, j) for t in range(4) for j in range(11)]
        dseq = [(t, m) for t in range(4) for m in range(4)]
        gl = [0]
        dl = [0]

        def load_g(upto):
            while gl[0] <= upto and gl[0] < len(gseq):
                k = gl[0]
                _, j = gseq[k]
                b = k % NG
                gv = gbuf[b].ap.rearrange("p (k c) -> p k c", k=KC)
                vv = vbuf[b].ap.rearrange("p (k c) -> p k c", k=KC)
                P.emit("pool", (lambda gv, j: lambda e: e.dma_start(out=gv, in_=wg_v[:, :, j * 256:(j + 1) * 256]))(gv, j),
                       writes=[gbuf[b]], kind="dma")
                P.emit("pool", (lambda vv, j: lambda e: e.dma_start(out=vv, in_=wg_v[:, :, DFF + j * 256:DFF + (j + 1) * 256]))(vv, j),
                       writes=[vbuf[b]], kind="dma")
                gl[0] += 1

        def load_d(upto):
            while dl[0] <= upto and dl[0] < len(dseq):
                k = dl[0]
                _, m = dseq[k]
                b = k % 2
                dv = dbuf[b].ap.rearrange("p (f c) -> p f c", f=FC)
                P.emit("pool", (lambda dv, m: lambda e: e.dma_start(out=dv, in_=wd_v[:, :, m * 256:(m + 1) * 256]))(dv, m),
                       writes=[dbuf[b]], kind="dma")
                dl[0] += 1

        load_g(1)
        load_d(0)
        gk = 0
        dk = 0
        hsrc = lambda t: [h[kc][t] for kc in range(KC)]
        rmsnorm_a(hsrc(0), KC, u2[0])
        rmsnorm_b(hsrc(0), KC, D, gcol, u2[0], u2[0], psum[0], rstd2[0])
        for t in range(4):
            u = u2[t % 2]
            for j in range(11):
                if t + 1 < 4 and j == 5:
                    rmsnorm_a(hsrc(t + 1), KC, u2[(t + 1) % 2])
                if t + 1 < 4 and j == 8:
                    rmsnorm_b(hsrc(t + 1), KC, D, gcol, u2[(t + 1) % 2], u2[(t + 1) % 2], psum[0], rstd2[(t + 1) % 2])
                load_g(gk + 2)
                b = gk % NG
                gv = gbuf[b].ap.rearrange("p (k c) -> p k c", k=KC)
                vv = vbuf[b].ap.rearrange("p (k c) -> p k c", k=KC)
                for jj in range(2):
                    fc = 2 * j + jj
                    pg = psum[1 + (fc % 2) * 2]
                    pv = psum[2 + (fc % 2) * 2]
                    for kc in range(KC):
                        P.emit("pe", (lambda kc, jj, gv, pg, ur: lambda e: e.matmul(pg.ap, lhsT=gv[:, kc, jj * 128:(jj + 1) * 128], rhs=ur,
                                                                                start=(kc == 0), stop=(kc == KC - 1)))(kc, jj, gv, pg, u[kc].ap),
                               reads=[gbuf[b], u[kc]], writes=[pg], inc=(kc == KC - 1))
                    for kc in range(KC):
                        P.emit("pe", (lambda kc, jj, vv, pv, ur: lambda e: e.matmul(pv.ap, lhsT=vv[:, kc, jj * 128:(jj + 1) * 128], rhs=ur,
                                                                                start=(kc == 0), stop=(kc == KC - 1)))(kc, jj, vv, pv, u[kc].ap),
                               reads=[vbuf[b], u[kc]], writes=[pv], inc=(kc == KC - 1))
                    sgt = sg[fc % 2]
                    P.emit("act", (lambda pg, sgt: lambda e: e.activation(out=sgt.ap, in_=pg.ap, func=AF.Silu))(pg, sgt),
                           reads=[pg], writes=[sgt])
                    P.emit("dve", (lambda pv, sgt, fc: lambda e: e.tensor_tensor(out=act[fc].ap, in0=sgt.ap, in1=pv.ap, op=ALU.mult))(pv, sgt, fc),
                           reads=[pv, sgt], writes=[act[fc]])
                gk += 1
            for m in range(4):
                load_d(dk + 1)
                b = dk % 2
                dv = dbuf[b].ap.rearrange("p (f c) -> p f c", f=FC)
                for jj in range(2):
                    dc = 2 * m + jj
                    po = psum[5 + (dc % 2)]
                    for fc in range(FC):
                        P.emit("pe", (lambda fc, jj, dv, po: lambda e: e.matmul(po.ap, lhsT=dv[:, fc, jj * 128:(jj + 1) * 128], rhs=act[fc].ap,
                                                                                start=(fc == 0), stop=(fc == FC - 1)))(fc, jj, dv, po),
                               reads=[dbuf[b], act[fc]], writes=[po], inc=(fc == FC - 1))
                    ht = h[dc][t]
                    P.emit("dve", (lambda po, ht: lambda e: e.scalar_tensor_tensor(out=ht.ap, in0=po.ap, scalar=0.5, in1=ht.ap,
                                                                                    op0=ALU.mult, op1=ALU.add))(po, ht),
                           reads=[po, ht], writes=[ht])
                dk += 1
        P.barrier()
        abf.reset(mb)
        af.reset(mf)

    def ple(L):
        mb, mf = abf.mark(), af.mark()
        u2 = [[T(abf.alloc(NT)) for _ in range(KC)] for _ in range(2)]
        wg = T(abf.alloc(KC * D))
        wp = T(abf.alloc(2 * D))
        pt = [T(abf.alloc(2 * NT)) for _ in range(2)]
        rstd2 = [T(af.alloc(NT)) for _ in range(2)]
        sgm = [T(af.alloc(NT)) for _ in range(2)]
        wgv = wg.ap.rearrange("p (k c) -> p k c", k=KC)
        wpv = wp.ap.rearrange("p (k c) -> p k c", k=2)
        for half in range(2):
            P.emit("pool", (lambda half: lambda e: e.dma_start(out=wgv[:, half * 4:(half + 1) * 4, :],
                                                              in_=wview(wgate, L)[:, half * 4:(half + 1) * 4, :]))(half),
                   writes=[wg], kind="dma")
        P.emit("pool", lambda e: e.dma_start(out=wpv, in_=wview(wproj, L)), writes=[wp], kind="dma")
        hsrc = lambda t: [h[kc][t] for kc in range(KC)]
        for t in range(4):
            ptv = pt[t % 2].ap.rearrange("p (k c) -> p k c", k=2)
            P.emit("pool", (lambda ptv, t: lambda e: e.dma_start(out=ptv, in_=pT[L].rearrange("(k p) t -> p k t", p=128)[:, :, t * NT:(t + 1) * NT]))(ptv, t),
                   writes=[pt[t % 2]], kind="dma")
            u = u2[t % 2]
            if t == 0:
                rmsnorm_a(hsrc(0), KC, u2[0])
                rmsnorm_b(hsrc(0), KC, D, V_GPLE + L * 8, u2[0], u2[0], psum[0], rstd2[0])
            for dc in range(KC):
                if t + 1 < 4 and dc == 2:
                    rmsnorm_a(hsrc(t + 1), KC, u2[(t + 1) % 2])
                if t + 1 < 4 and dc == 5:
                    rmsnorm_b(hsrc(t + 1), KC, D, V_GPLE + L * 8, u2[(t + 1) % 2], u2[(t + 1) % 2], psum[0], rstd2[(t + 1) % 2])
                pg = psum[1 + (dc % 2) * 2]
                pp = psum[2 + (dc % 2) * 2]
                for kc in range(KC):
                    P.emit("pe", (lambda kc, dc, pg, ur: lambda e: e.matmul(pg.ap, lhsT=wgv[:, kc, dc * 128:(dc + 1) * 128], rhs=ur,
                                                                        start=(kc == 0), stop=(kc == KC - 1)))(kc, dc, pg, u[kc].ap),
                           reads=[wg, u[kc]], writes=[pg], inc=(kc == KC - 1))
                for k2 in range(2):
                    P.emit("pe", (lambda k2, dc, pp, ptv: lambda e: e.matmul(pp.ap, lhsT=wpv[:, k2, dc * 128:(dc + 1) * 128], rhs=ptv[:, k2, :],
                                                                             start=(k2 == 0), stop=(k2 == 1)))(k2, dc, pp, ptv),
                           reads=[wp, pt[t % 2]], writes=[pp], inc=(k2 == 1))
                s_ = sgm[dc % 2]
                P.emit("act", (lambda pg, s_: lambda e: e.activation(out=s_.ap, in_=pg.ap, func=AF.Sigmoid))(pg, s_), reads=[pg], writes=[s_])
                P.emit("dve", (lambda pp, s_: lambda e: e.tensor_tensor(out=s_.ap, in0=s_.ap, in1=pp.ap, op=ALU.mult))(pp, s_),
                       reads=[pp, s_], writes=[s_])
                ht = h[dc][t]
                P.emit("dve", (lambda s_, ht: lambda e: e.tensor_tensor(out=ht.ap, in0=ht.ap, in1=s_.ap, op=ALU.add))(s_, ht),
                       reads=[s_, ht], writes=[ht])
        P.barrier()
        abf.reset(mb)
        af.reset(mf)

    def load_u(L, T8, ut, ug_t):
        rank, lt = T8 // 4, T8 % 4
        for c in range(4):
            for i in range(2):
                kc = 2 * c + i
                P.emit("sync", (lambda kc, c, i: lambda e: e.dma_start(
                    out=ut[kc].ap, in_=u_g[L][c][rank * 256 + i * 128: rank * 256 + (i + 1) * 128, lt * NT:(lt + 1) * NT]))(kc, c, i),
                    reads=[ug_t[c]], writes=[ut[kc]], kind="dma")

    def load_tab(idx, T8, dst):
        P.emit("sync", lambda e: e.dma_start(out=dst.ap, in_=tabs[idx][:, T8 * NT:(T8 + 1) * NT]), writes=[dst], kind="dma")

    def proj_fm(ps, wv, c0, ncol, ut, wt):
        for kc in range(KC):
            P.emit("pe", (lambda kc: lambda e: e.matmul(ps.ap[0:ncol, :], lhsT=wv[:, kc, c0:c0 + ncol], rhs=ut[kc].ap,
                                                        start=(kc == 0), stop=(kc == KC - 1)))(kc),
                   reads=[wt, ut[kc]], writes=[ps], inc=(kc == KC - 1))

    def proj_tm(ps, wv, c0, ncol, ut, wt, tb):
        for kc in range(KC):
            P.emit("pe", (lambda kc: lambda e: e.matmul(ps.ap[:, 0:ncol], lhsT=ut[kc].ap[:, tb * 128:(tb + 1) * 128], rhs=wv[:, kc, c0:c0 + ncol],
                                                        start=(kc == 0), stop=(kc == KC - 1)))(kc),
                   reads=[wt, ut[kc]], writes=[ps], inc=(kc == KC - 1))

    def rope_evac(px, pxp, cosT, sinT, nrow, out_ap, out_t, t1, t2, scale):
        P.emit("dve", lambda e: e.scalar_tensor_tensor(out=t1.ap[0:nrow, :], in0=px.ap[0:nrow, :], scalar=scale, in1=cosT.ap[0:nrow, :],
                                                       op0=ALU.mult, op1=ALU.mult),
               reads=[px, cosT], writes=[t1])
        P.emit("dve", lambda e: e.scalar_tensor_tensor(out=t2.ap[0:nrow, :], in0=pxp.ap[0:nrow, :], scalar=scale, in1=sinT.ap[0:nrow, :],
                                                       op0=ALU.mult, op1=ALU.mult),
               reads=[pxp, sinT], writes=[t2])
        P.emit("dve", lambda e: e.tensor_tensor(out=out_ap, in0=t1.ap[0:nrow, :], in1=t2.ap[0:nrow, :], op=ALU.add),
               reads=[t1, t2], writes=[out_t])

    def softmax_attn(qT_ap, kT_ap, qk_t, v_t, vaug_fn, Q8, pO, pD, sbank, pts, den_ones):
        nkb = 4 * Q8 + 4
        q_ap = qT_ap[:, Q8 * NT:(Q8 + 1) * NT]

        NB = len(sbank)
        LA = NB - 1

        def s_mm(i):
            ps = sbank[i % NB]
            P.emit("pe", lambda e: e.matmul(ps.ap, lhsT=kT_ap[:, i * 128:(i + 1) * 128], rhs=q_ap, start=True, stop=True),
                   reads=[qk_t], writes=[ps])
        def o_mm(i):
            pt_ = pts[i % len(pts)]
            va = vaug_fn(i)
            P.emit("pe", lambda e: e.matmul(pO.ap, lhsT=va, rhs=pt_.ap, start=(i == 0), stop=(i == nkb - 1)), reads=[v_t, pt_], writes=[pO])
            if den_ones:
                P.emit("pe", lambda e: e.matmul(pD.ap, lhsT=ones.ap, rhs=pt_.ap, start=(i == 0), stop=(i == nkb - 1)), reads=[ones, pt_], writes=[pD])
        for i0 in range(min(LA, nkb)):
            s_mm(i0)
        for i in range(nkb):
            if i + LA < nkb:
                s_mm(i + LA)
            ps = sbank[i % NB]
            pt_ = pts[i % len(pts)]
            P.emit("act", (lambda ps, pt_: lambda e: e.activation(out=pt_.ap, in_=ps.ap, func=AF.Exp))(ps, pt_), reads=[ps], writes=[pt_])
            jd = i - 4 * Q8
            if jd >= 0:
                P.emit("dve", (lambda pt_, jd: lambda e: e.tensor_tensor(out=pt_.ap, in0=pt_.ap, in1=maskI[jd], op=ALU.mult))(pt_, jd),
                       reads=[pt_, cm], writes=[pt_])
            if i >= 1:
                o_mm(i - 1)
        o_mm(nkb - 1)

    def mix_gather(L, ug_t):
        mb0, mf0 = abf.mark(), af.mark()
        u = [T(abf.alloc(NT)) for _ in range(KC)]
        sq = [T(abf.alloc(NT)) for _ in range(KC)]
        rstd = T(af.alloc(NT))
        uloc_t = [T(None) for c in range(4)]
        for t in range(4):
            rmsnorm([h[kc][t] for kc in range(KC)], KC, D, V_GMIX + L * 8, u, sq, psum[0], rstd)
            for kc in range(KC):
                c, i = kc // 2, kc % 2
                P.emit("sync", (lambda kc, c, i, t: lambda e: e.dma_start(out=u_loc[L][c][i * 128:(i + 1) * 128, t * NT:(t + 1) * NT], in_=u[kc].ap))(kc, c, i, t),
                       reads=[u[kc]], writes=[uloc_t[c]], kind="dma")
        for c in range(4):
            P.emit("pool", (lambda c: lambda e: e.collective_compute("AllGather", ALU.bypass, replica_groups=GROUPS,
                                                                    ins=[u_loc[L][c].ap().opt()], outs=[u_g[L][c].ap().opt()]))(c),
                   reads=[uloc_t[c]], writes=[ug_t[c]], kind="cc")
        P.barrier()
        abf.reset(mb0)
        af.reset(mf0)

    def mix_sb(L, ug_t, mxl_t):
        mb, mf = abf.mark(), af.mark()
        win_v = wview(win, L)
        wt = T(abf.alloc(KC * 384))
        wv = wt.ap.rearrange("p (k c) -> p k c", k=KC)
        P.emit("pool", lambda e: e.dma_start(out=wv, in_=win_v[:, :, 0:384]), writes=[wt], kind="dma")
        qk_t = T(None, "qk")
        v_t = T(None, "v")
        qT = abf.alloc(S)
        kT = abf.alloc(S)
        vv_ = abf.alloc(32 * 128).rearrange("p (b c) -> p b c", b=32)
        uts = [[T(abf.alloc(NT)) for _ in range(KC)] for _ in range(2)]
        for T8 in range(8):
            ut = uts[T8 % 2]
            load_u(L, T8, ut, ug_t)
            pq, pk = psum[(T8 % 2) * 2], psum[(T8 % 2) * 2 + 1]
            proj_fm(pq, wv, 0, 128, ut, wt)
            proj_fm(pk, wv, 128, 128, ut, wt)
            P.emit("act", (lambda pq, T8: lambda e: e.activation(out=qT[:, T8 * NT:(T8 + 1) * NT], in_=pq.ap, func=AF.Copy, scale=0.125))(pq, T8),
                   reads=[pq], writes=[qk_t])
            P.emit("dve", (lambda pk, T8: lambda e: e.tensor_copy(out=kT[:, T8 * NT:(T8 + 1) * NT], in_=pk.ap))(pk, T8), reads=[pk], writes=[qk_t])
            for tb in range(4):
                pvv = psum[4 + tb % 2]
                proj_tm(pvv, wv, 256, 128, ut, wt, tb)
                P.emit("act", (lambda pvv, T8, tb: lambda e: e.activation(out=vv_[:, T8 * 4 + tb, :], in_=pvv.ap[:, 0:128], func=AF.Copy))(pvv, T8, tb),
                       reads=[pvv], writes=[v_t])
        ebuf = [T(af.alloc(NT)) for _ in range(3)]
        t1b = [T(af.alloc(NT)) for _ in range(2)]
        Rt = T(af.alloc(NT))
        spb = [T(abf.alloc(NT)) for _ in range(3)]
        Ab = [T(abf.alloc(NT)) for _ in range(2)]
        ob = [T(abf.alloc(NT)) for _ in range(2)]
        zb = [psum[0], psum[1], psum[7]]
        cb = [psum[2], psum[3]]
        csb = [psum[4], psum[5]]
        pO = psum[6]

        def sb_tile(hh, Q8):
            r0 = hh * 64
            nkb = 4 * Q8 + 4
            order = list(range(nkb - 1, -1, -1))
            q_ap = qT[r0:r0 + 64, Q8 * NT:(Q8 + 1) * NT]

            def st1(n):
                i = order[n]
                pz = zb[n % 3]
                P.emit("pe", lambda e: e.matmul(pz.ap, lhsT=kT[r0:r0 + 64, i * 128:(i + 1) * 128], rhs=q_ap, start=True, stop=True),
                       reads=[qk_t], writes=[pz])
                eb, sp = ebuf[n % 3], spb[n % 3]
                P.emit("act", lambda e: e.activation(out=eb.ap, in_=pz.ap, func=AF.Exp), reads=[pz], writes=[eb])
                P.emit("act", lambda e: e.activation(out=sp.ap, in_=eb.ap, func=AF.Ln, bias=1.0, scale=1.0), reads=[eb], writes=[sp])
                jd = i - 4 * Q8
                if jd >= 0:
                    P.emit("dve", lambda e: e.tensor_tensor(out=sp.ap, in0=sp.ap, in1=maskS[jd], op=ALU.mult), reads=[sp, cm], writes=[sp])

            def st2(n):
                i = order[n]
                pz, sp = zb[n % 3], spb[n % 3]
                pc, pcs = cb[n % 2], csb[n % 2]
                P.emit("pe", lambda e: e.matmul(pc.ap, lhsT=trim, rhs=sp.ap, start=True, stop=True), reads=[cm, sp], writes=[pc])
                if n < nkb - 1:
                    P.emit("pe", lambda e: e.matmul(pcs.ap, lhsT=ones.ap, rhs=sp.ap, start=True, stop=True), reads=[ones, sp], writes=[pcs])
                t1 = t1b[n % 2]
                if n == 0:
                    P.emit("dve", lambda e: e.tensor_copy(out=t1.ap, in_=pz.ap), reads=[pz], writes=[t1])
                else:
                    P.emit("dve", lambda e: e.tensor_tensor(out=t1.ap, in0=pz.ap, in1=Rt.ap, op=ALU.subtract), reads=[pz, Rt], writes=[t1])
                P.emit("dve", lambda e: e.tensor_tensor(out=t1.ap, in0=t1.ap, in1=pc.ap, op=ALU.subtract), reads=[t1, pc], writes=[t1])
                A = Ab[n % 2]
                P.emit("act", lambda e: e.activation(out=A.ap, in_=t1.ap, func=AF.Exp), reads=[t1], writes=[A])
                jd = i - 4 * Q8
                if jd >= 0:
                    P.emit("dve", lambda e: e.tensor_tensor(out=A.ap, in0=A.ap, in1=maskS[jd], op=ALU.mult), reads=[A, cm], writes=[A])
                if n < nkb - 1:
                    if n == 0:
                        P.emit("dve", lambda e: e.tensor_copy(out=Rt.ap, in_=pcs.ap), reads=[pcs], writes=[Rt])
                    else:
                        P.emit("dve", lambda e: e.tensor_tensor(out=Rt.ap, in0=Rt.ap, in1=pcs.ap, op=ALU.add), reads=[pcs, Rt], writes=[Rt])

            def st3(n):
                i = order[n]
                A = Ab[n % 2]
                P.emit("pe", lambda e: e.matmul(pO.ap[0:64, :], lhsT=vv_[:, i, r0:r0 + 64], rhs=A.ap, start=(n == 0), stop=(n == nkb - 1)),
                       reads=[v_t, A], writes=[pO])
            st1(0)
            st1(1)
            for n in range(nkb):
                if n + 2 < nkb:
                    st1(n + 2)
                st2(n)
                if n >= 1:
                    st3(n - 1)
            st3(nkb - 1)
            o_ = ob[Q8 % 2]
            P.emit("act", lambda e: e.activation(out=o_.ap[0:64, :], in_=pO.ap[0:64, :], func=AF.Copy), reads=[pO], writes=[o_])
            P.emit("sync", lambda e: e.dma_start(out=mx_loc[L][0][r0:r0 + 64, Q8 * NT:(Q8 + 1) * NT], in_=o_.ap[0:64, :]),
                   reads=[o_], writes=[mxl_t[0]], kind="dma")
        for hh in range(2):
            for Q8 in range(8):
                sb_tile(hh, Q8)
        P.barrier()
        abf.reset(mb)
        af.reset(mf)

    def mix_diff(L, ug_t, mxl_t):
        mb, mf = abf.mark(), af.mark()
        win_v = wview(win, L)
        wt = T(abf.alloc(KC * 1280))
        wv = wt.ap.rearrange("p (k c) -> p k c", k=KC)
        for part in range(5):
            P.emit("pool", (lambda part: lambda e: e.dma_start(out=wv[:, :, part * 256:(part + 1) * 256],
                                                              in_=win_v[:, :, 384 + part * 256:384 + (part + 1) * 256]))(part),
                   writes=[wt], kind="dma")
        qk_t = T(None, "qk")
        v_t = T(None, "v")
        qT2 = abf.alloc(2 * S).rearrange("p (h t) -> p h t", h=2)
        kT2 = abf.alloc(2 * S).rearrange("p (h t) -> p h t", h=2)
        vd = abf.alloc(32 * 256).rearrange("p (b c) -> p b c", b=32)
        uts = [[T(abf.alloc(NT)) for _ in range(KC)] for _ in range(2)]
        cosT = [T(af.alloc(NT)) for _ in range(2)]
        sinT = [T(af.alloc(NT)) for _ in range(2)]
        t1 = T(af.alloc(NT))
        t2 = T(af.alloc(NT))
        for T8 in range(8):
            ut = uts[T8 % 2]
            load_u(L, T8, ut, ug_t)
            load_tab(0, T8, cosT[T8 % 2])
            load_tab(1, T8, sinT[T8 % 2])
            n = 0
            for which, dstT, cbase in ((0, qT2, 0), (1, kT2, 512)):
                for hh in range(2):
                    px, pxp = psum[(n % 2) * 2], psum[(n % 2) * 2 + 1]
                    n += 1
                    proj_fm(px, wv, cbase + hh * 128, 128, ut, wt)
                    proj_fm(pxp, wv, cbase + 256 + hh * 128, 128, ut, wt)
                    rope_evac(px, pxp, cosT[T8 % 2], sinT[T8 % 2], 128, dstT[:, hh, T8 * NT:(T8 + 1) * NT], qk_t, t1, t2,
                              0.125 if which == 0 else 1.0)
            for tb in range(4):
                pvv = psum[4 + tb % 2]
                proj_tm(pvv, wv, 1024, 256, ut, wt, tb)
                P.emit("act", (lambda pvv, T8, tb: lambda e: e.activation(out=vd[:, T8 * 4 + tb, :], in_=pvv.ap[:, 0:256], func=AF.Copy))(pvv, T8, tb),
                       reads=[pvv], writes=[v_t])
        P.barrier()
        af.reset(mf)
        pts = [T(abf.alloc(NT)) for _ in range(3)]
        ob = [T(abf.alloc(NT)) for _ in range(1)]
        sqd = T(abf.alloc(NT))
        rec = T(af.alloc(NT))
        o1 = T(af.alloc(NT))
        o2 = T(af.alloc(NT))
        rstd = T(af.alloc(NT))
        lam_init = 0.8 - 0.6 * math.exp(-0.3 * L)

        def diff_tile(hh, Q8):
            for comp in range(2):
                r0 = comp * 64
                pO, pD = psum[2 + comp * 2], psum[3 + comp * 2]
                softmax_attn(qT2[r0:r0 + 64, hh, :], kT2[r0:r0 + 64, hh, :], qk_t, v_t,
                             (lambda i: vd[:, i, hh * 128:(hh + 1) * 128]), Q8, pO, pD, [psum[0], psum[1], psum[7]], pts, True)
                oc = o1 if comp == 0 else o2
                P.emit("dve", (lambda pD: lambda e: e.reciprocal(out=rec.ap, in_=pD.ap))(pD), reads=[pD], writes=[rec])
                P.emit("dve", (lambda pO, oc: lambda e: e.tensor_tensor(out=oc.ap, in0=pO.ap, in1=rec.ap, op=ALU.mult))(pO, oc),
                       reads=[pO, rec], writes=[oc])
            P.emit("dve", lambda e: e.scalar_tensor_tensor(out=o1.ap, in0=o2.ap, scalar=lamv.ap[:, L:L + 1], in1=o1.ap, op0=ALU.mult, op1=ALU.add),
                   reads=[o1, o2, lamv], writes=[o1])
            P.emit("act", lambda e: e.activation(out=sqd.ap, in_=o1.ap, func=AF.Square), reads=[o1], writes=[sqd])
            pss = psum[6]
            P.emit("pe", lambda e: e.matmul(pss.ap, lhsT=ones.ap, rhs=sqd.ap, start=True, stop=True), reads=[ones, sqd], writes=[pss])
            P.emit("act", lambda e: e.activation(out=rstd.ap, in_=pss.ap, func=AF.Sqrt, bias=EPS, scale=1.0 / 128),
                   reads=[pss], writes=[rstd])
            P.emit("dve", lambda e: e.reciprocal(out=rstd.ap, in_=rstd.ap), reads=[rstd], writes=[rstd])
            P.emit("dve", lambda e: e.tensor_scalar(out=rstd.ap, in0=rstd.ap, scalar1=(1.0 - lam_init), scalar2=None, op0=ALU.mult),
                   reads=[rstd], writes=[rstd])
            o_ = ob[0]
            P.emit("dve", lambda e: e.scalar_tensor_tensor(out=o_.ap, in0=o1.ap, scalar=vcol(V_SUBLN + L), in1=rstd.ap, op0=ALU.mult, op1=ALU.mult),
                   reads=[o1, rstd, vecs], writes=[o_])
            P.emit("sync", lambda e: e.dma_start(out=mx_loc[L][1 + hh][:, Q8 * NT:(Q8 + 1) * NT], in_=o_.ap),
                   reads=[o_], writes=[mxl_t[1 + hh]], kind="dma")
        for hh in range(2):
            for Q8 in range(8):
                diff_tile(hh, Q8)
        P.barrier()
        abf.reset(mb)
        af.reset(mf)

    def mix_mla(L, ug_t, mxl_t):
        mb, mf = abf.mark(), af.mark()
        win_v = wview(win, L)
        wt = T(abf.alloc(KC * 448))
        wv = wt.ap.rearrange("p (k c) -> p k c", k=KC)
        P.emit("pool", lambda e: e.dma_start(out=wv, in_=win_v[:, :, 1664:2112]), writes=[wt], kind="dma")
        wq_t = T(abf.alloc(2 * 384))
        wqv = wq_t.ap.rearrange("p (k c) -> p k c", k=2)
        P.emit("pool", lambda e: e.dma_start(out=wqv, in_=wview(wuq, L)), writes=[wq_t], kind="dma")
        wkv_t = T(abf.alloc(256))
        P.emit("pool", lambda e: e.dma_start(out=wkv_t.ap, in_=wukv[L]), writes=[wkv_t], kind="dma")
        qk_t = T(None, "qk")
        v_t = T(None, "v")
        qT2 = abf.alloc(2 * S).rearrange("p (h t) -> p h t", h=2)
        kT2 = abf.alloc(2 * S).rearrange("p (h t) -> p h t", h=2)
        vm = abf.alloc(32 * 256).rearrange("p (b c) -> p b c", b=32)
        P.emit("dve", lambda e: e.memset(vm, 1.0), writes=[v_t])
        uts = [[T(abf.alloc(NT)) for _ in range(KC)] for _ in range(2)]
        cqn = [T(abf.alloc(NT)) for _ in range(2)]
        ckvn = [T(abf.alloc(NT))]
        sqm = [T(abf.alloc(NT)) for _ in range(2)]
        cosM = [T(af.alloc(NT)) for _ in range(2)]
        sinM = [T(af.alloc(NT)) for _ in range(2)]
        cosK = [T(af.alloc(NT)) for _ in range(2)]
        sinK = [T(af.alloc(NT)) for _ in range(2)]
        cq = [T(af.alloc(NT)) for _ in range(2)]
        ckv = [T(af.alloc(NT))]
        t1 = T(af.alloc(NT))
        t2 = T(af.alloc(NT))
        rstd = T(af.alloc(NT))
        sc_m = 96.0 ** -0.5

        def mla_proj(T8):
            ut = uts[T8 % 2]
            load_u(L, T8, ut, ug_t)
            b2 = T8 % 2
            load_tab(2, T8, cosM[b2])
            load_tab(3, T8, sinM[b2])
            load_tab(4, T8, cosK[b2])
            load_tab(5, T8, sinK[b2])
            for c in range(2):
                proj_fm(psum[c], wv, c * 128, 128, ut, wt)
                P.emit("act", (lambda c: lambda e: e.activation(out=cq[c].ap, in_=psum[c].ap, func=AF.Copy))(c), reads=[psum[c]], writes=[cq[c]])
            proj_fm(psum[2], wv, 256, 128, ut, wt)
            P.emit("act", lambda e: e.activation(out=ckv[0].ap, in_=psum[2].ap, func=AF.Copy), reads=[psum[2]], writes=[ckv[0]])
            rmsnorm(cq, 2, 256, V_QN + L * 2, cqn, sqm, psum[3], rstd)
            rmsnorm(ckv, 1, 128, V_KVN + L, ckvn, sqm, psum[3], rstd)
            for hh in range(2):
                px, pxp = psum[4], psum[5]
                for c in range(2):
                    P.emit("pe", (lambda c, hh: lambda e: e.matmul(px.ap[0:96, :], lhsT=wqv[:, c, hh * 96:(hh + 1) * 96], rhs=cqn[c].ap,
                                                                   start=(c == 0), stop=(c == 1)))(c, hh), reads=[wq_t, cqn[c]], writes=[px])
                for c in range(2):
                    P.emit("pe", (lambda c, hh: lambda e: e.matmul(pxp.ap[0:96, :], lhsT=wqv[:, c, 192 + hh * 96:192 + (hh + 1) * 96], rhs=cqn[c].ap,
                                                                   start=(c == 0), stop=(c == 1)))(c, hh), reads=[wq_t, cqn[c]], writes=[pxp])
                rope_evac(px, pxp, cosM[b2], sinM[b2], 96, qT2[0:96, hh, T8 * NT:(T8 + 1) * NT], qk_t, t1, t2, sc_m)
            for hh in range(2):
                pkn = psum[6]
                P.emit("pe", (lambda hh: lambda e: e.matmul(pkn.ap[0:64, :], lhsT=wkv_t.ap[:, hh * 64:(hh + 1) * 64], rhs=ckvn[0].ap, start=True, stop=True))(hh),
                       reads=[wkv_t, ckvn[0]], writes=[pkn])
                P.emit("act", (lambda hh: lambda e: e.activation(out=kT2[0:64, hh, T8 * NT:(T8 + 1) * NT], in_=pkn.ap[0:64, :], func=AF.Copy))(hh),
                       reads=[pkn], writes=[qk_t])
            px, pxp = psum[4], psum[5]
            proj_fm(px, wv, 384, 32, ut, wt)
            proj_fm(pxp, wv, 416, 32, ut, wt)
            rope_evac(px, pxp, cosK[b2], sinK[b2], 32, t1.ap[0:32, :], t1, t1, t2, 1.0)
            for hh in range(2):
                P.emit("act", (lambda hh: lambda e: e.activation(out=kT2[64:96, hh, T8 * NT:(T8 + 1) * NT], in_=t1.ap[0:32, :], func=AF.Copy))(hh),
                       reads=[t1], writes=[qk_t])
            for tb in range(4):
                pvv = psum[7]
                P.emit("pe", (lambda tb: lambda e: e.matmul(pvv.ap[:, 0:128], lhsT=ckvn[0].ap[:, tb * 128:(tb + 1) * 128], rhs=wkv_t.ap[:, 128:256],
                                                            start=True, stop=True))(tb), reads=[wkv_t, ckvn[0]], writes=[pvv])
                for hh in range(2):
                    P.emit("act", (lambda tb, hh: lambda e: e.activation(out=vm[:, T8 * 4 + tb, hh * 128:hh * 128 + 64],
                                                                         in_=pvv.ap[:, hh * 64:(hh + 1) * 64], func=AF.Copy))(tb, hh),
                           reads=[pvv], writes=[v_t])
        for T8 in range(8):
            mla_proj(T8)
        P.barrier()
        af.reset(mf)
        pts = [T(abf.alloc(NT)) for _ in range(4)]
        ob = [T(abf.alloc(NT)) for _ in range(2)]
        rec = T(af.alloc(NT))

        def mla_tile(hh, Q8):
            pO = psum[2 + (Q8 % 2)]
            softmax_attn(qT2[0:96, hh, :], kT2[0:96, hh, :], qk_t, v_t,
                         (lambda i: vm[:, i, hh * 128:(hh + 1) * 128]), Q8, pO, None, [psum[0], psum[1], psum[4], psum[5]], pts, False)
            P.emit("act", lambda e: e.activation(out=rec.ap[0:64, :], in_=pO.ap[64:128, :], func=AF.Copy), reads=[pO], writes=[rec])
            P.emit("dve", lambda e: e.reciprocal(out=rec.ap[0:64, :], in_=rec.ap[0:64, :]), reads=[rec], writes=[rec])
            o_ = ob[Q8 % 2]
            P.emit("dve", lambda e: e.tensor_tensor(out=o_.ap[0:64, :], in0=pO.ap[0:64, :], in1=rec.ap[0:64, :], op=ALU.mult),
                   reads=[pO, rec], writes=[o_])
            P.emit("sync", lambda e: e.dma_start(out=mx_loc[L][3][hh * 64:(hh + 1) * 64, Q8 * NT:(Q8 + 1) * NT], in_=o_.ap[0:64, :]),
                   reads=[o_], writes=[mxl_t[3]], kind="dma")
        for hh in range(2):
            for Q8 in range(8):
                mla_tile(hh, Q8)
        P.barrier()
        abf.reset(mb)
        af.reset(mf)

    def mix_out(L, mxl_t):
        mb, mf = abf.mark(), af.mark()
        mxg_t = [T(None) for c in range(4)]
        for c in range(4):
            P.emit("pool", (lambda c: lambda e: e.collective_compute("AllGather", ALU.bypass, replica_groups=GROUPS,
                                                                    ins=[mx_loc[L][c].ap().opt()], outs=[mx_g[L][c].ap().opt()]))(c),
                   reads=[mxl_t[c]], writes=[mxg_t[c]], kind="cc")
        wo_t = T(abf.alloc(KC * D))
        wov = wo_t.ap.rearrange("p (k c) -> p k c", k=KC)
        for half in range(2):
            P.emit("pool", (lambda half: lambda e: e.dma_start(out=wov[:, half * 4:(half + 1) * 4, :],
                                                              in_=wview(wout, L)[:, half * 4:(half + 1) * 4, :]))(half),
                   writes=[wo_t], kind="dma")
        ca = [[T(abf.alloc(NT)) for _ in range(KC)] for _ in range(2)]
        cb_ = [[T(abf.alloc(NT)) for _ in range(KC)] for _ in range(2)]
        ms = [[T(abf.alloc(NT)) for _ in range(KC)] for _ in range(2)]

        def wo_tile(t):
            for K in range(KC):
                c, rp = K // 2, K % 2
                A, B, M = ca[t % 2][K], cb_[t % 2][K], ms[t % 2][K]
                P.emit("sync", (lambda A, c, rp: lambda e: e.dma_start(out=A.ap, in_=mx_g[L][c][rp * 128:(rp + 1) * 128, t * NT:(t + 1) * NT]))(A, c, rp),
                       reads=[mxg_t[c]], writes=[A], kind="dma")
                P.emit("sync", (lambda B, c, rp: lambda e: e.dma_start(out=B.ap, in_=mx_g[L][c][rp * 128:(rp + 1) * 128, TL + t * NT:TL + (t + 1) * NT]))(B, c, rp),
                       reads=[mxg_t[c]], writes=[B], kind="dma")
                P.emit("dve", (lambda A, M: lambda e: e.tensor_scalar(out=M.ap, in0=A.ap, scalar1=vcol(V_SEL), scalar2=None, op0=ALU.mult))(A, M),
                       reads=[A, vecs], writes=[M])
                P.emit("dve", (lambda B, M: lambda e: e.scalar_tensor_tensor(out=M.ap, in0=B.ap, scalar=vcol(V_SEL + 1), in1=M.ap,
                                                                             op0=ALU.mult, op1=ALU.add))(B, M),
                       reads=[B, M, vecs], writes=[M])
            if dbg is not None and dbg[0] == "mixraw":
                for K in range(KC):
                    M = ms[t % 2][K]
                    P.emit("pool", (lambda K, M: lambda e: e.dma_start(out=outT[K * 128:(K + 1) * 128, t * NT:(t + 1) * NT], in_=M.ap))(K, M),
                           reads=[M], kind="dma")
                return
            for dc in range(KC):
                po = psum[dc % 2]
                for K in range(KC):
                    M = ms[t % 2][K]
                    P.emit("pe", (lambda K, M, dc, po: lambda e: e.matmul(po.ap, lhsT=wov[:, K, dc * 128:(dc + 1) * 128], rhs=M.ap,
                                                                          start=(K == 0), stop=(K == KC - 1)))(K, M, dc, po),
                           reads=[wo_t, M], writes=[po], inc=(K == KC - 1))
                ht = h[dc][t]
                P.emit("dve", (lambda po, ht: lambda e: e.tensor_tensor(out=ht.ap, in0=ht.ap, in1=po.ap, op=ALU.add))(po, ht),
                       reads=[po, ht], writes=[ht])
        for t in range(4):
            wo_tile(t)
        P.barrier()
        abf.reset(mb)
        af.reset(mf)

    def mixer(L):
        ug_t = [T(None) for c in range(4)]
        mxl_t = [T(None) for c in range(4)]
        mix_gather(L, ug_t)
        mix_sb(L, ug_t, mxl_t)
        mix_diff(L, ug_t, mxl_t)
        mix_mla(L, ug_t, mxl_t)
        mix_out(L, mxl_t)

    def final():
        sq = [T(abf.alloc(NT)) for _ in range(KC)]
        rstd = T(af.alloc(NT))
        ot = [T(af.alloc(NT)) for _ in range(KC)]
        for t in range(4):
            rmsnorm([h[kc][t] for kc in range(KC)], KC, D, V_GFIN, ot, sq, psum[0], rstd)
            for kc in range(KC):
                P.emit("sync", (lambda kc, t: lambda e: e.dma_start(out=outT[kc * 128:(kc + 1) * 128, t * NT:(t + 1) * NT], in_=ot[kc].ap))(kc, t),
                       reads=[ot[kc]], kind="dma")

    def dump_h():
        for kc in range(KC):
            P.emit("sync", (lambda kc: lambda e: e.dma_start(out=outT[kc * 128:(kc + 1) * 128, :], in_=hT[:, kc, :]))(kc),
                   reads=h[kc], kind="dma")

    setup()
    stop = False
    if dbg is not None and dbg[0] == "setup":
        stop = True
        depth = 0
    for L in range(depth):
        ffn(L, w1gu, w1d, V_GF1 + L * 8)
        if dbg == ("ffn1", L):
            stop = True
            break
        mixer(L)
        if dbg == ("mix", L):
            stop = True
            break
        if dbg == ("mixraw", L):
            stop = None
            break
        ffn(L, w2gu, w2d, V_GF2 + L * 8)
        ple(L)
        if dbg == ("layer", L):
            stop = True
            break
    if stop:
        dump_h()
    elif stop is None:
        pass
    else:
        final()

    with nc.Block() as block:
        @block.tensor
        def _(e):
            P.replay("pe", e)

        @block.scalar
        def _(e):
            P.replay("act", e)

        @block.vector
        def _(e):
            P.replay("dve", e)

        @block.gpsimd
        def _(e):
            P.replay("pool", e)

        @block.sync
        def _(e):
            P.replay("sync", e)
            P.final_wait("sync", e)
    es.close()
    return nc


def _win_cols(r):
    cols = []
    H = [2 * r, 2 * r + 1]
    for base in (0, 256, 512):
        for hh in H:
            cols += [base + hh * 64 + d for d in range(64)]

    def dperm(d):
        return d + 8 if d < 8 else (d - 8 if d < 16 else d)
    for base in (768, 1280):
        for perm in (False, True):
            for hh in H:
                for c in range(2):
                    for d in range(64):
                        dd = dperm(d) if perm else d
                        cols.append(base + hh * 128 + c * 64 + dd)
    for hh in H:
        cols += [1792 + hh * 128 + e for e in range(128)]
    cols += list(range(2304, 2560))
    cols += list(range(2560, 2688))
    cols += list(range(2688, 2720))
    cols += [2688 + (j + 16 if j < 16 else j - 16) for j in range(32)]
    assert len(cols) == NWIN
    return np.array(cols)


def _wuq_cols(r):
    cols = []
    H = [2 * r, 2 * r + 1]
    for perm in (False, True):
        for hh in H:
            for j in range(96):
                jj = j
                if perm and j >= 64:
                    m = j - 64
                    jj = 64 + (m + 16 if m < 16 else m - 16)
                cols.append(hh * 96 + jj)
    return np.array(cols)


def _wukv_cols(r):
    H = [2 * r, 2 * r + 1]
    cols = []
    for hh in H:
        cols += [hh * 128 + j for j in range(64)]
    for hh in H:
        cols += [hh * 128 + 64 + j for j in range(64)]
    return np.array(cols)


def _wout_rows():
    rows = []
    for c in range(4):
        for rp in range(2):
            for i in range(128):
                if c == 0:
                    rows.append(128 * rp + i)
                elif c == 1:
                    rows.append(256 + (2 * rp) * 128 + i)
                elif c == 2:
                    rows.append(256 + (2 * rp + 1) * 128 + i)
                else:
                    rows.append(768 + 128 * rp + i)
    return np.array(rows)


def _const_tables():
    import ml_dtypes
    kp = np.arange(128)[:, None]
    qf = np.arange(512)[None, :]
    cm = np.zeros((128, 8 * 512 + 128), np.float32)
    for j in range(4):
        cm[:, j * 512:(j + 1) * 512] = (qf >= 128 * j + kp)
        cm[:, (4 + j) * 512:(5 + j) * 512] = (qf > 128 * j + kp)
    jj = np.arange(128)[:, None]
    ss = np.arange(128)[None, :]
    cm[:, 8 * 512:] = (jj >= ss)
    return cm.astype(ml_dtypes.bfloat16)


def _freq_cols():
    invf = np.zeros((128, 3), np.float64)
    sgn = np.zeros((128, 3), np.float64)
    for row in range(128):
        d = row % 64
        if d < 16:
            invf[row, 0] = THETA ** (-(d % 8) / 8.0)
            sgn[row, 0] = -1.0 if d < 8 else 1.0
        if 64 <= row < 96:
            m = row - 64
            invf[row, 1] = THETA ** (-(m % 16) / 16.0)
            sgn[row, 1] = -1.0 if m < 16 else 1.0
        if row < 32:
            invf[row, 2] = THETA ** (-(row % 16) / 16.0)
            sgn[row, 2] = -1.0 if row < 16 else 1.0
    return invf.astype(np.float32), sgn.astype(np.float32)


_NC_CACHE = {}


def make_in_maps(inp):
    f32 = np.float32
    g = {k: np.asarray(v) for k, v in inp.items()}

    def fm(v, nch):
        v = np.asarray(v, f32)
        L = v.shape[0]
        return v.reshape(L, nch, 128).transpose(2, 0, 1).reshape(128, L * nch)
    vec_common = np.zeros((128, NV), f32)
    vec_common[:, V_GF1:V_GF1 + 32] = fm(g["norm_ffn1"], 8)
    vec_common[:, V_GMIX:V_GMIX + 32] = fm(g["norm_mix"], 8)
    vec_common[:, V_GF2:V_GF2 + 32] = fm(g["norm_ffn2"], 8)
    vec_common[:, V_GPLE:V_GPLE + 32] = fm(g["norm_ple"], 8)
    vec_common[:, V_GFIN:V_GFIN + 8] = fm(g["norm_final"][None, :], 8)
    vec_common[:, V_QN:V_QN + 8] = fm(g["mla_q_norm"], 2)
    vec_common[:, V_KVN:V_KVN + 4] = fm(g["mla_kv_norm"], 1)
    vec_common[:, V_SUBLN:V_SUBLN + 4] = fm(g["diff_subln"], 1)
    invf, sgn = _freq_cols()
    vec_common[:, V_INVF:V_INVF + 3] = invf
    vec_common[:, V_SGN:V_SGN + 3] = sgn
    vec_common[:, V_NEGPI] = -math.pi
    for j, nm in enumerate(("diff_lambda_q1", "diff_lambda_k1", "diff_lambda_q2", "diff_lambda_k2")):
        vec_common[:, V_LAM + j * 256:V_LAM + (j + 1) * 256] = np.asarray(g[nm], f32).reshape(1, 256)
    cmask = _const_tables()
    wout_p = np.ascontiguousarray(np.asarray(g["w_out"], f32)[:, _wout_rows(), :])
    per_rank = []
    for r in range(2):
        per_rank.append(dict(
            win=np.ascontiguousarray(np.asarray(g["w_in"], f32)[:, :, _win_cols(r)]),
            wuq=np.ascontiguousarray(np.asarray(g["mla_w_uq"], f32)[:, :, _wuq_cols(r)]),
            wukv=np.ascontiguousarray(np.asarray(g["mla_w_ukv"], f32)[:, :, _wukv_cols(r)]),
        ))
    shared = dict(
        w1gu=np.ascontiguousarray(g["w_ffn1_gu"], dtype=f32), w1d=np.ascontiguousarray(g["w_ffn1_down"], dtype=f32),
        w2gu=np.ascontiguousarray(g["w_ffn2_gu"], dtype=f32), w2d=np.ascontiguousarray(g["w_ffn2_down"], dtype=f32),
        wout=wout_p, wgate=np.ascontiguousarray(g["w_ple_gate"], dtype=f32), wproj=np.ascontiguousarray(g["w_ple_proj"], dtype=f32),
        cmask=cmask,
    )
    x = np.asarray(g["x"], f32)
    p = np.asarray(g["p"], f32)
    pos = np.asarray(g["positions"]).astype(np.int32)
    maps = []
    for core in range(8):
        b, r = core // 2, core % 2
        sl = slice(r * TL, (r + 1) * TL)
        vec = vec_common.copy()
        vec[:, V_SEL + r] = 1.0
        m = dict(shared)
        m.update(per_rank[r])
        m["xT"] = np.ascontiguousarray(x[b, sl, :].T)
        m["pT"] = np.ascontiguousarray(p[:, b, sl, :].transpose(0, 2, 1))
        m["posr"] = np.ascontiguousarray(np.broadcast_to(pos[b][None, :], (128, S)))
        m["vecs"] = vec
        maps.append(m)
    return maps


def run(inp, depth=DEPTH, dbg=None, trace=False):
    key = (depth, dbg)
    if key not in _NC_CACHE:
        _NC_CACHE[key] = build_program(depth, dbg)
    nc = _NC_CACHE[key]
    maps = make_in_maps(inp)
    res = run_bass_kernel_spmd(nc, maps, core_ids=list(range(8)), trace=trace)
    out = np.zeros((4, S, D), np.float32)
    for core in range(8):
        b, r = core // 2, core % 2
        out[b, r * TL:(r + 1) * TL, :] = np.asarray(res.results[core]["outT"]).T
    return out, res


def kernel(**inputs):
    out, _ = run(inputs)
    return out
```
